# Optimizing a Trainium2 kernel written in Bass

```python
import math
import jax, jax.numpy as jnp
from jax import lax
import numpy as np

D_MODEL = 2048
BATCH = 4
SEQ = 2048
DEPTH = 1
DEC_BATCH = 16
DEC_SEQ = 16
PAST_LEN = 2048

CHUNK = 64
N_META = 16
PROMPT_PAD = (-N_META) % CHUNK
RW_HEAD = 64
RW_DIM = D_MODEL // 2
RW_HEADS = RW_DIM // RW_HEAD
RW_DECAY_LORA = 96
RW_AAA_LORA = 96
RW_GATE_LORA = 256
RW_SHIFT_COLS = 3 * RW_DIM + RW_DECAY_LORA + RW_AAA_LORA + RW_GATE_LORA
RW_SPLITS = (RW_DIM, 2 * RW_DIM, 3 * RW_DIM, 3 * RW_DIM + RW_DECAY_LORA, 3 * RW_DIM + RW_DECAY_LORA + RW_AAA_LORA)
RW_GN_EPS = 64e-5
SSM_HEAD = 64
SSM_DIM = D_MODEL
SSM_HEADS = SSM_DIM // SSM_HEAD
SSM_GROUPS = 4
SSM_HPG = SSM_HEADS // SSM_GROUPS
SSM_STATE = 128
CONV_W = 4
CONV_DIM = SSM_DIM + 2 * SSM_GROUPS * SSM_STATE
RMS_EPS = 1e-5
N_IN = RW_SHIFT_COLS + SSM_DIM + CONV_DIM + SSM_HEADS + 2 * D_MODEL
IN_SPLITS = (RW_SHIFT_COLS, RW_SHIFT_COLS + SSM_DIM, RW_SHIFT_COLS + SSM_DIM + CONV_DIM, RW_SHIFT_COLS + SSM_DIM + CONV_DIM + SSM_HEADS)
D_FF = 5632
LN_EPS = 1e-5
ALPHA = (2 * DEPTH) ** 0.25
BETA = (8 * DEPTH) ** -0.25

kernel_name = 'rwkv7_mamba2_gated_hybrid_stream_step'


def layer_norm(x, g, b):
    xf = x.astype(jnp.float32)
    mu = jnp.mean(xf, axis=-1, keepdims=True)
    var = jnp.mean(jnp.square(xf - mu), axis=-1, keepdims=True)
    return ((xf - mu) * lax.rsqrt(var + LN_EPS) * g + b).astype(x.dtype)


def swiglu(x, w_gu, w_dn):
    g, u = jnp.split(x @ w_gu, 2, axis=-1)
    return (jax.nn.silu(g) * u) @ w_dn


def token_shift(p, hist, mu):
    prev = jnp.concatenate([hist.astype(p.dtype), p[:, :-1]], axis=1)
    return p + (prev - p) * mu, p[:, -1:]


def causal_dwconv(u, hist, w, b):
    l = u.shape[1]
    full = jnp.concatenate([hist.astype(u.dtype), u], axis=1)
    out = b + sum(full[:, i:i + l] * w[i] for i in range(CONV_W))
    return out, full[:, l:]


def rwkv7_mix(p_rw, shift_hist, wkv0, rw_mu, rw_w0, rw_w2, rw_a0, rw_a2, rw_g2, rw_kk, rw_ka, rw_rk, rw_gn_w, rw_gn_b):
    f32 = jnp.float32
    ps, new_shift = token_shift(p_rw, shift_hist, rw_mu)
    r, k, v, wd, ad, gd = jnp.split(ps, RW_SPLITS, axis=-1)
    b, l = r.shape[:2]
    w_log = -jax.nn.softplus(-(rw_w0 + jnp.tanh(wd) @ rw_w2).astype(f32)) - 0.5
    decay = jnp.exp(-jnp.exp(w_log))
    a = jax.nn.sigmoid((rw_a0 + ad @ rw_a2).astype(f32))
    g = jax.nn.sigmoid(gd) @ rw_g2
    hd = lambda t: t.reshape(b, l, RW_HEADS, RW_HEAD).astype(f32)
    r, k, v, decay, a = hd(r), hd(k), hd(v), hd(decay), hd(a)
    kk = k * rw_kk
    kk = kk / jnp.maximum(jnp.linalg.norm(kk, axis=-1, keepdims=True), 1e-12)
    k = k * (1.0 + (a - 1.0) * rw_ka)

    def step(S, inp):
        r_t, w_t, k_t, v_t, kk_t, a_t = inp
        sa = jnp.einsum('bhvk,bhk->bhv', S, -kk_t)
        S = S * w_t[:, :, None, :] + sa[..., None] * (kk_t * a_t)[:, :, None, :] + v_t[..., None] * k_t[:, :, None, :]
        return S, jnp.einsum('bhvk,bhk->bhv', S, r_t)

    tm = lambda t: jnp.moveaxis(t, 1, 0)
    wkv_t, y = lax.scan(step, wkv0.astype(f32), (tm(r), tm(decay), tm(k), tm(v), tm(kk), tm(a)))
    y = jnp.moveaxis(y, 0, 1)
    mu = jnp.mean(y, axis=-1, keepdims=True)
    var = jnp.mean(jnp.square(y - mu), axis=-1, keepdims=True)
    yn = ((y - mu) * lax.rsqrt(var + RW_GN_EPS)).reshape(b, l, RW_DIM) * rw_gn_w + rw_gn_b
    bonus = (jnp.sum(r * k * rw_rk, axis=-1, keepdims=True) * v).reshape(b, l, RW_DIM)
    out = ((yn + bonus) * g).astype(p_rw.dtype)
    return out, new_shift, wkv_t.astype(wkv0.dtype)


def ssd_scan(x, dt, a, bm, cm, h0, chunk):
    b, l = x.shape[:2]
    nc = l // chunk
    x = x.reshape(b, nc, chunk, SSM_GROUPS, SSM_HPG, SSM_HEAD)
    dt = dt.reshape(b, nc, chunk, SSM_GROUPS, SSM_HPG)
    bm = bm.reshape(b, nc, chunk, SSM_GROUPS, SSM_STATE)
    cm = cm.reshape(b, nc, chunk, SSM_GROUPS, SSM_STATE)
    acs = jnp.cumsum(dt * a.reshape(SSM_GROUPS, SSM_HPG), axis=2)
    xdt = x * dt[..., None]
    causal = jnp.tril(jnp.ones((chunk, chunk), dtype=bool))[:, :, None, None]
    seg = acs[:, :, :, None] - acs[:, :, None, :]
    lmat = jnp.exp(jnp.where(causal, seg, -jnp.inf))
    cb = jnp.einsum('bcign,bcjgn->bcijg', cm, bm)
    y_diag = jnp.einsum('bcijgh,bcjghp->bcighp', cb[..., None] * lmat, xdt)
    to_end = jnp.exp(acs[:, :, -1:] - acs)
    states = jnp.einsum('bcjgn,bcjghp->bcghpn', bm, xdt * to_end[..., None])
    chunk_decay = jnp.exp(acs[:, :, -1])

    def carry(h, inp):
        st, dec = inp
        return h * dec[..., None, None] + st, h

    h0g = h0.reshape(b, SSM_GROUPS, SSM_HPG, SSM_HEAD, SSM_STATE)
    h_t, h_in = lax.scan(carry, h0g, (jnp.moveaxis(states, 1, 0), jnp.moveaxis(chunk_decay, 1, 0)))
    h_in = jnp.moveaxis(h_in, 0, 1)
    y_off = jnp.einsum('bcign,bcghpn->bcighp', cm, h_in) * jnp.exp(acs)[..., None]
    y = (y_diag + y_off).reshape(b, l, SSM_HEADS, SSM_HEAD)
    return y, h_t.reshape(b, SSM_HEADS, SSM_HEAD, SSM_STATE)


def mamba2_mix(p_z, p_xbc, p_dt, conv_hist, ssm0, pad, chunk, conv_w, conv_b, dt_bias, a_log, d_skip, ssm_norm_w):
    f32 = jnp.float32
    xbc, new_conv = causal_dwconv(p_xbc, conv_hist, conv_w, conv_b)
    xbc = jax.nn.silu(xbc)
    xs, bm, cm = jnp.split(xbc, (SSM_DIM, SSM_DIM + SSM_GROUPS * SSM_STATE), axis=-1)
    b, l = xs.shape[:2]
    dt = jax.nn.softplus((p_dt + dt_bias).astype(f32))
    a = -jnp.exp(a_log.astype(f32))
    xh = xs.reshape(b, l, SSM_HEADS, SSM_HEAD).astype(f32)
    bm = bm.reshape(b, l, SSM_GROUPS, SSM_STATE).astype(f32)
    cm = cm.reshape(b, l, SSM_GROUPS, SSM_STATE).astype(f32)
    padf = lambda t: jnp.pad(t, ((0, 0), (pad, 0)) + ((0, 0),) * (t.ndim - 2))
    y, ssm_t = ssd_scan(padf(xh), padf(dt), a, padf(bm), padf(cm), ssm0.astype(f32), chunk)
    y = y[:, pad:] + xh * d_skip[:, None]
    y = y.reshape(b, l, SSM_DIM) * jax.nn.silu(p_z.astype(f32))
    yg = y.reshape(b, l, SSM_GROUPS, SSM_DIM // SSM_GROUPS)
    yg = yg * lax.rsqrt(jnp.mean(jnp.square(yg), axis=-1, keepdims=True) + RMS_EPS)
    out = (yg.reshape(b, l, SSM_DIM) * ssm_norm_w).astype(p_z.dtype)
    return out, new_conv, ssm_t.astype(ssm0.dtype)


def trunk_layer(x, shift_hist, wkv0, conv_hist, ssm0, pad, chunk, lw):
    x = layer_norm(ALPHA * x + 0.5 * swiglu(x, lw['ffn1_gu'], lw['ffn1_dn']), lw['ln1_g'], lw['ln1_b'])
    proj = x @ lw['w_in']
    p_rw, p_z, p_xbc, p_dt, p_gate = jnp.split(proj, IN_SPLITS, axis=-1)
    y_rw, new_shift, wkv_t = rwkv7_mix(p_rw, shift_hist, wkv0, lw['rw_mu'], lw['rw_w0'], lw['rw_w2'], lw['rw_a0'], lw['rw_a2'], lw['rw_g2'], lw['rw_kk'], lw['rw_ka'], lw['rw_rk'], lw['rw_gn_w'], lw['rw_gn_b'])
    y_ssm, new_conv, ssm_t = mamba2_mix(p_z, p_xbc, p_dt, conv_hist, ssm0, pad, chunk, lw['conv_w'], lw['conv_b'], lw['dt_bias'], lw['a_log'], lw['d_skip'], lw['ssm_norm_w'])
    g_a, g_b = jnp.split(jax.nn.sigmoid(p_gate + lw['b_gate']), 2, axis=-1)
    merged = g_a * (y_rw @ lw['w_rw_out']) + g_b * (y_ssm @ lw['w_ssm_out'])
    x = layer_norm(ALPHA * x + merged @ lw['w_out'], lw['ln2_g'], lw['ln2_b'])
    x = layer_norm(ALPHA * x + 0.5 * swiglu(x, lw['ffn2_gu'], lw['ffn2_dn']), lw['ln3_g'], lw['ln3_b'])
    return x, new_shift, wkv_t, new_conv, ssm_t


def setup_inputs(seed: int = 0) -> dict:
    key = jax.random.key(seed)
    ks = iter(jax.random.split(key, 64))

    def nrm(shape, scale):
        return jax.random.normal(next(ks), shape, jnp.float32) * scale

    def unif(shape, lo, hi):
        return jax.random.uniform(next(ks), shape, jnp.float32, lo, hi)

    L = DEPTH
    dt0 = jnp.exp(unif((L, SSM_HEADS), math.log(1e-3), math.log(1e-1)))
    return {
        'x_prompt': nrm((BATCH, SEQ, D_MODEL), 1.0),
        'x_sample': nrm((DEC_BATCH, DEC_SEQ, D_MODEL), 1.0),
        'state_rwkv_shift': nrm((L, DEC_BATCH, 1, RW_SHIFT_COLS), 1.0),
        'state_wkv': nrm((L, DEC_BATCH, RW_HEADS, RW_HEAD, RW_HEAD), 0.3),
        'state_conv': nrm((L, DEC_BATCH, CONV_W - 1, CONV_DIM), 1.0),
        'state_ssm': nrm((L, DEC_BATCH, SSM_HEADS, SSM_HEAD, SSM_STATE), 0.1),
        'meta_tokens': nrm((N_META, D_MODEL), 1.0),
        'ffn1_gu': nrm((L, D_MODEL, 2 * D_FF), D_MODEL ** -0.5),
        'ffn1_dn': nrm((L, D_FF, D_MODEL), BETA * D_FF ** -0.5),
        'ln1_g': 1.0 + nrm((L, D_MODEL), 0.02),
        'ln1_b': nrm((L, D_MODEL), 0.02),
        'w_in': nrm((L, D_MODEL, N_IN), D_MODEL ** -0.5),
        'b_gate': nrm((L, 2 * D_MODEL), 0.1),
        'rw_mu': unif((L, RW_SHIFT_COLS), 0.0, 1.0),
        'rw_w0': unif((L, RW_DIM), -6.0, -1.0),
        'rw_w2': nrm((L, RW_DECAY_LORA, RW_DIM), 0.1 * RW_DECAY_LORA ** -0.5),
        'rw_a0': nrm((L, RW_DIM), 0.1),
        'rw_a2': nrm((L, RW_AAA_LORA, RW_DIM), RW_AAA_LORA ** -0.5),
        'rw_g2': nrm((L, RW_GATE_LORA, RW_DIM), RW_GATE_LORA ** -0.5),
        'rw_kk': 0.85 + nrm((L, RW_HEADS, RW_HEAD), 0.02),
        'rw_ka': 1.0 + nrm((L, RW_HEADS, RW_HEAD), 0.02),
        'rw_rk': nrm((L, RW_HEADS, RW_HEAD), 0.1),
        'rw_gn_w': 1.0 + nrm((L, RW_DIM), 0.02),
        'rw_gn_b': nrm((L, RW_DIM), 0.02),
        'conv_w': nrm((L, CONV_W, CONV_DIM), CONV_W ** -0.5),
        'conv_b': nrm((L, CONV_DIM), 0.02),
        'dt_bias': dt0 + jnp.log(-jnp.expm1(-dt0)),
        'a_log': jnp.log(unif((L, SSM_HEADS), 1.0, 16.0)),
        'd_skip': 1.0 + nrm((L, SSM_HEADS), 0.1),
        'ssm_norm_w': 1.0 + nrm((L, SSM_DIM), 0.02),
        'w_rw_out': nrm((L, RW_DIM, D_MODEL), RW_DIM ** -0.5),
        'w_ssm_out': nrm((L, SSM_DIM, D_MODEL), SSM_DIM ** -0.5),
        'w_out': nrm((L, D_MODEL, D_MODEL), BETA * D_MODEL ** -0.5),
        'ln2_g': 1.0 + nrm((L, D_MODEL), 0.02),
        'ln2_b': nrm((L, D_MODEL), 0.02),
        'ffn2_gu': nrm((L, D_MODEL, 2 * D_FF), D_MODEL ** -0.5),
        'ffn2_dn': nrm((L, D_FF, D_MODEL), BETA * D_FF ** -0.5),
        'ln3_g': 1.0 + nrm((L, D_MODEL), 0.02),
        'ln3_b': nrm((L, D_MODEL), 0.02),
    }


def reference(x_prompt, x_sample, state_rwkv_shift, state_wkv, state_conv, state_ssm, meta_tokens,
              ffn1_gu, ffn1_dn, ln1_g, ln1_b, w_in, b_gate, rw_mu, rw_w0, rw_w2, rw_a0, rw_a2, rw_g2,
              rw_kk, rw_ka, rw_rk, rw_gn_w, rw_gn_b, conv_w, conv_b, dt_bias, a_log, d_skip, ssm_norm_w,
              w_rw_out, w_ssm_out, w_out, ln2_g, ln2_b, ffn2_gu, ffn2_dn, ln3_g, ln3_b):
    weights = {
        'ffn1_gu': ffn1_gu, 'ffn1_dn': ffn1_dn, 'ln1_g': ln1_g, 'ln1_b': ln1_b, 'w_in': w_in, 'b_gate': b_gate,
        'rw_mu': rw_mu, 'rw_w0': rw_w0, 'rw_w2': rw_w2, 'rw_a0': rw_a0, 'rw_a2': rw_a2, 'rw_g2': rw_g2,
        'rw_kk': rw_kk, 'rw_ka': rw_ka, 'rw_rk': rw_rk, 'rw_gn_w': rw_gn_w, 'rw_gn_b': rw_gn_b,
        'conv_w': conv_w, 'conv_b': conv_b, 'dt_bias': dt_bias, 'a_log': a_log, 'd_skip': d_skip,
        'ssm_norm_w': ssm_norm_w, 'w_rw_out': w_rw_out, 'w_ssm_out': w_ssm_out, 'w_out': w_out,
        'ln2_g': ln2_g, 'ln2_b': ln2_b, 'ffn2_gu': ffn2_gu, 'ffn2_dn': ffn2_dn, 'ln3_g': ln3_g, 'ln3_b': ln3_b,
    }
    b = x_prompt.shape[0]
    dtp = x_prompt.dtype
    xp = jnp.concatenate([jnp.broadcast_to(meta_tokens.astype(dtp)[None], (b, N_META, D_MODEL)), x_prompt], axis=1)
    xs = x_sample
    p_shift, p_wkv, p_conv, p_ssm = [], [], [], []
    s_shift, s_wkv, s_conv, s_ssm = [], [], [], []
    for layer in range(DEPTH):
        lw = {name: w[layer] for name, w in weights.items()}
        xp, ps_, pw_, pc_, pm_ = trunk_layer(
            xp,
            jnp.zeros((b, 1, RW_SHIFT_COLS), dtp),
            jnp.zeros((b, RW_HEADS, RW_HEAD, RW_HEAD), dtp),
            jnp.zeros((b, CONV_W - 1, CONV_DIM), dtp),
            jnp.zeros((b, SSM_HEADS, SSM_HEAD, SSM_STATE), dtp),
            PROMPT_PAD, CHUNK, lw)
        xs, ss_, sw_, sc_, sm_ = trunk_layer(
            xs, state_rwkv_shift[layer], state_wkv[layer], state_conv[layer], state_ssm[layer],
            0, xs.shape[1], lw)
        p_shift.append(ps_); p_wkv.append(pw_); p_conv.append(pc_); p_ssm.append(pm_)
        s_shift.append(ss_); s_wkv.append(sw_); s_conv.append(sc_); s_ssm.append(sm_)
    y_prompt = xp[:, N_META:]
    return (y_prompt, xs, jnp.stack(p_shift), jnp.stack(p_wkv), jnp.stack(p_conv), jnp.stack(p_ssm),
            jnp.stack(s_shift), jnp.stack(s_wkv), jnp.stack(s_conv), jnp.stack(s_ssm))
```

```python
from contextlib import ExitStack
import numpy as np
import concourse.bass as bass
import concourse.mybir as mybir
from concourse.bass_utils import run_bass_kernel_spmd

F32 = mybir.dt.float32
BF16 = mybir.dt.bfloat16
AF = mybir.ActivationFunctionType
ALU = mybir.AluOpType
AX = mybir.AxisListType

D = 2048
DFF = 5632
NKC = D // 128
NFC = DFF // 128
RW_DIM = 1024
RW_H = 16
RW_COLS = 3520
SSM_H = 32
SSM_G = 4
SSM_N = 128
CONV_DIM = 3072
N_IN = 12768
ALPHA = 2.0 ** 0.25
LN_EPS = 1e-5
RW_GN_EPS = 64e-5
RMS_EPS = 1e-5
C = 64


class Tracker:
    ENGS = ("pe", "act", "dve", "pool", "sp")

    def __init__(self, nc, stack):
        self.nc = nc
        self.stack = stack
        self.sem = {e: stack.enter_context(nc.semaphore("s_" + e)) for e in self.ENGS}
        self.cnt = {e: 0 for e in self.ENGS}
        self.chan_sem = {}
        self.chan_cnt = {}
        self.seen = {}
        self.lastw = {}
        self.readers = {}
        self.prog = {e: [] for e in self.ENGS}
        self.nsem = len(self.ENGS)
        self.rename = None
        self.defer = None
        self._grp = None
        self.excl = set()

    def _split(self, reads, writes):
        if not self.excl:
            return list(reads), list(writes)
        r = [k for k in reads if k not in self.excl]
        w = list(writes) + [k for k in reads if k in self.excl]
        return r, w

    def _deps(self, reads, writes):
        deps = {}
        def add(d):
            if d is None:
                return
            s, v = d
            k = id(s)
            if k not in deps or deps[k][1] < v:
                deps[k] = (s, v)
        for k in reads:
            add(self.lastw.get(k))
        for k in writes:
            add(self.lastw.get(k))
            for d in self.readers.get(k, ()):
                add(d)
        return list(deps.values())

    def _emit_waits(self, eng, deps):
        for s, v in deps:
            key = (eng, id(s))
            if self.seen.get(key, 0) >= v:
                continue
            self.seen[key] = v
            self.prog[eng].append(lambda e, s=s, v=v: e.wait_ge(s, v))

    def _commit(self, dep, reads, writes):
        for k in reads:
            self.readers.setdefault(k, []).append(dep)
        for k in writes:
            self.lastw[k] = dep
            self.readers[k] = []

    def op(self, eng, fns, reads=(), writes=()):
        if self.defer is not None:
            self.defer.append(("op", (eng, fns, reads, writes), {}))
            return
        if not isinstance(fns, (list, tuple)):
            fns = [fns]
        reads, writes = self._split(reads, writes)
        self._emit_waits(eng, self._deps(reads, writes))
        self.cnt[eng] += 1
        n = self.cnt[eng]
        sem = self.sem[eng]
        for f in fns[:-1]:
            self.prog[eng].append(lambda e, f=f: f(e))
        last = fns[-1]
        self.prog[eng].append(lambda e, f=last, sem=sem: f(e).then_inc(sem, 1))
        self._commit((sem, n), reads, writes)

    def chan(self, name):
        if name not in self.chan_sem:
            self.chan_sem[name] = self.stack.enter_context(self.nc.semaphore("c_" + name))
            self.chan_cnt[name] = 0
            self.nsem += 1
        return self.chan_sem[name]

    def begin_group(self, chan):
        self._grp = (chan, [])

    def end_group(self):
        chan, keys = self._grp
        self._grp = None
        if chan in self.chan_sem:
            dep = (self.chan_sem[chan], self.chan_cnt[chan])
            for k in keys:
                self.lastw[k] = dep

    def dma(self, eng, chan, out, in_, reads=(), writes=(), **kw):
        if self.defer is not None:
            self.defer.append(("dma", (eng, chan, out, in_, reads, writes), kw))
            return
        if self._grp is not None:
            chan = self._grp[0]
            self._grp[1].extend(writes)
        if self.rename is not None:
            chan = chan.replace(self.rename[0], self.rename[1])
        sem = self.chan(chan)
        self._emit_waits(eng, self._deps(reads, writes))
        self.chan_cnt[chan] += 16
        n = self.chan_cnt[chan]
        self.prog[eng].append(lambda e, o=out, i=in_, sem=sem, kw=kw: e.dma_start(out=o, in_=i, **kw).then_inc(sem, 16))
        self._commit((sem, n), reads, writes)

    def capture(self, fn):
        assert self.defer is None
        self.defer = []
        try:
            fn()
        finally:
            lst, self.defer = self.defer, None
        return lst

    def replay(self, item):
        kind, a, kw = item
        if kind == "op":
            self.op(*a)
        else:
            self.dma(*a, **kw)

    def replay_merged(self, streams):
        pos = [0] * len(streams)
        tot = [max(len(x), 1) for x in streams]
        while True:
            best, bi = None, -1
            for i, x in enumerate(streams):
                if pos[i] < len(x):
                    f = pos[i] / tot[i]
                    if best is None or f < best:
                        best, bi = f, i
            if bi < 0:
                break
            self.replay(streams[bi][pos[bi]])
            pos[bi] += 1

    def drain_dmas(self, eng="sp"):
        for name, sem in self.chan_sem.items():
            v = self.chan_cnt[name]
            if v and self.seen.get((eng, id(sem)), 0) < v:
                self.seen[(eng, id(sem))] = v
                self.prog[eng].append(lambda e, s=sem, v=v: e.wait_ge(s, v))

    def flush(self):
        self.drain_dmas("sp")
        nc = self.nc
        prog = self.prog
        with nc.Block() as block:
            @block.tensor
            def _(e):
                for f in prog["pe"]:
                    f(e)

            @block.scalar
            def _(e):
                for f in prog["act"]:
                    f(e)

            @block.vector
            def _(e):
                for f in prog["dve"]:
                    f(e)

            @block.gpsimd
            def _(e):
                for f in prog["pool"]:
                    f(e)

            @block.sync
            def _(e):
                for f in prog["sp"]:
                    f(e)
        self.prog = {e: [] for e in self.ENGS}
        self.lastw = {}
        self.readers = {}


class Cfg:
    def __init__(self, n_long=32, tile_blocks=6, debug=False, stop_after=None, only=None, scratch_in=()):
        self.only = only
        self.mix_stop = None
        self.scratch_in = tuple(scratch_in)
        self.n_long = n_long
        self.nchunks = 3 + n_long
        self.nblocks = (self.nchunks + 1) // 2
        assert self.nblocks % tile_blocks == 0
        self.tile_blocks = tile_blocks
        self.ntiles = self.nblocks // tile_blocks
        self.R = self.nblocks * 128
        self.debug = debug
        self.stop_after = stop_after
        self.seq_end = [64, 128, self.nchunks * 64]


def _sbuf(nc, st, prefix):
    def sb(n, shape, dt=F32):
        return st.enter_context(nc.sbuf_tensor(f"{prefix}_{n}", shape, dt))
    return sb


def make_ident(T, ident, n=128):
    T.op("pool", lambda e: e.memset(ident[:], 0.0), writes=[ident.name])
    T.op("pool", lambda e: e.affine_select(out=ident[:], in_=ident[:], pattern=[[-1, n]], compare_op=ALU.not_equal,
                                           fill=1.0, base=0, channel_multiplier=1),
         reads=[ident.name], writes=[ident.name])


def phase_ln(nc, T, cfg, pb, src, dst_x, dst_xT, g_d, b_d, name):
    nb = cfg.nblocks
    GB = 3
    norm = g_d is not None
    with ExitStack() as st:
        sb = _sbuf(nc, st, name)
        NBUF = 3
        xin = [sb(f"xin{i}", [128, D]) for i in range(NBUF)]
        if norm:
            gbc = sb("gbc", [128, D])
            bbc = sb("bbc", [128, D])
            T.dma("sp", gbc.name, gbc[:], g_d.partition_broadcast(128), writes=[gbc.name])
            T.dma("sp", bbc.name, bbc[:], b_d.partition_broadcast(128), writes=[bbc.name])
            yv = [sb(f"y{i}", [128, D]) for i in range(NBUF)]
            stats = [sb(f"stats{i}", [128, 4, 6]) for i in range(NBUF)]
            mv = [sb(f"mv{i}", [128, 2]) for i in range(NBUF)]
            rstd = [sb(f"rstd{i}", [128, 1]) for i in range(NBUF)]
        if dst_xT is not None:
            ident = sb("ident", [128, 128])
            make_ident(T, ident)
            xTb = [sb(f"xTb{i}", [128, NKC, GB * 128], BF16) for i in range(2)]
            xTv = dst_xT.rearrange("(kc p) r -> p kc r", p=128)
        def head(b):
            s = b % NBUF
            x = xin[s]
            T.dma("sp", x.name, x[:], src[b * 128:(b + 1) * 128, :], writes=[x.name])
            y = x
            if norm:
                y = yv[s]
                for i in range(4):
                    T.op("dve", lambda e, i=i, x=x, s=s: e.bn_stats(out=stats[s][:, i, :], in_=x[:, i * 512:(i + 1) * 512]),
                         reads=[x.name], writes=[(stats[s].name, i)])
                T.op("dve", lambda e, s=s: e.bn_aggr(out=mv[s][:], in_=stats[s][:]),
                     reads=[(stats[s].name, i) for i in range(4)], writes=[mv[s].name])
                T.op("act", lambda e, s=s: e.activation(out=rstd[s][:], in_=mv[s][:, 1:2], func=AF.Sqrt, bias=LN_EPS),
                     reads=[mv[s].name], writes=[rstd[s].name])
                T.op("dve", lambda e, s=s: e.reciprocal(out=rstd[s][:], in_=rstd[s][:]),
                     reads=[rstd[s].name], writes=[rstd[s].name])
                T.op("dve", lambda e, s=s, x=x, y=y: e.tensor_scalar(out=y[:], in0=x[:], scalar1=mv[s][:, 0:1], scalar2=rstd[s][:, 0:1],
                                                                     op0=ALU.subtract, op1=ALU.mult),
                     reads=[x.name, mv[s].name, rstd[s].name], writes=[y.name])
                T.op("dve", lambda e, y=y: e.tensor_tensor(out=y[:], in0=y[:], in1=gbc[:], op=ALU.mult),
                     reads=[y.name, gbc.name], writes=[y.name])
                T.op("pool", lambda e, y=y: e.tensor_tensor(out=y[:], in0=y[:], in1=bbc[:], op=ALU.add),
                     reads=[y.name, bbc.name], writes=[y.name])
            return None

        def tail(b):
            s = b % NBUF
            y = yv[s] if norm else xin[s]
            if dst_x is not None:
                T.dma("act", y.name + "_st", dst_x[b * 128:(b + 1) * 128, :], y[:], reads=[y.name])
            if dst_xT is not None:
                g, gi = divmod(b, GB)
                gs = g % 2
                for q in range(4):
                    bank = pb[(b % 2) * 4 + q]
                    T.op("pe", [lambda e, kc=4 * q + j, j=j, bank=bank, y=y: e.transpose(
                        out=bank[:, j * 128:(j + 1) * 128], in_=y[:, kc * 128:(kc + 1) * 128], identity=ident[:]) for j in range(4)],
                        reads=[y.name, ident.name], writes=[bank.name])
                    eng = "act" if q % 2 == 0 else "dve"
                    outap = xTb[gs][:, 4 * q:4 * q + 4, gi * 128:(gi + 1) * 128]
                    inap = bank[:, :].rearrange("p (j t) -> p j t", j=4)
                    if eng == "act":
                        T.op("act", lambda e, o=outap, i=inap: e.copy(out=o, in_=i), reads=[bank.name], writes=[(xTb[gs].name, gi, q)])
                    else:
                        T.op("dve", lambda e, o=outap, i=inap: e.tensor_copy(out=o, in_=i), reads=[bank.name], writes=[(xTb[gs].name, gi, q)])
                if gi == GB - 1 or b == nb - 1:
                    nbk = gi + 1
                    r0 = g * GB * 128
                    keys = [(xTb[gs].name, a, q) for a in range(nbk) for q in range(4)]
                    T.dma("act", xTb[gs].name + "_st", xTv[:, :, r0:r0 + nbk * 128], xTb[gs][:, :, 0:nbk * 128], reads=keys)

        for b in range(nb):
            head(b)
            if b > 0:
                tail(b - 1)
        tail(nb - 1)
        T.flush()


def phase_ffn(nc, T, cfg, pb, xT_d, xres_d, wgu_d, wdn_d, dst_pre, name, wdn_cache=None):
    TB = cfg.tile_blocks
    TT = TB * 128
    npc = (TT + 511) // 512
    pw = TT // npc
    with ExitStack() as st:
        sb = _sbuf(nc, st, name)
        xT = sb("xT", [128, NKC, TT], BF16)
        hT = sb("hT", [128, NFC, TT], BF16)
        wgu = [[sb(f"wgu{i}{p}", [128, NKC, 512], BF16) for p in "gu"] for i in range(2)]
        wdn = [sb(f"wdn{i}", [128, 4, 512], BF16) for i in range(3)]
        sg = [sb(f"sg{i}", [128, pw]) for i in range(4)]
        xr = sb("xr", [128, TB, 512])
        so = sb("so", [128, TB, 512])
        xTv = xT_d.rearrange("(kc p) r -> p kc r", p=128)
        wguv = wgu_d.rearrange("(kc p) n -> p kc n", p=128)
        q = 0
        wq = 0
        xq = 0
        for t in range(cfg.ntiles):
            r0 = t * TT
            T.dma("pool" if xT_d.dtype == F32 else "sp", xT.name, xT[:], xTv[:, :, r0:r0 + TT], writes=[xT.name])
            for jg in range(NFC // 4):
                s = jg % 2
                for gu in range(2):
                    c0 = gu * DFF + jg * 512
                    w = wgu[s][gu]
                    T.dma("pool", w.name, w[:], wguv[:, :, c0:c0 + 512], writes=[w.name])
                for jj in range(4):
                    j = jg * 4 + jj
                    for pc in range(npc):
                        bg = pb[(2 * q) % 8]
                        bu = pb[(2 * q + 1) % 8]
                        sgt = sg[q % 4]
                        q += 1
                        for bank, w in ((bg, wgu[s][0]), (bu, wgu[s][1])):
                            T.op("pe", [lambda e, kc=kc, bank=bank, w=w, jj=jj, pc=pc: e.matmul(
                                bank[:, 0:pw], lhsT=w[:, kc, jj * 128:(jj + 1) * 128], rhs=xT[:, kc, pc * pw:(pc + 1) * pw],
                                start=(kc == 0), stop=(kc == NKC - 1)) for kc in range(NKC)],
                                reads=[w.name, xT.name], writes=[bank.name])
                        T.op("act", lambda e, bg=bg, sgt=sgt: e.activation(out=sgt[:], in_=bg[:, 0:pw], func=AF.Silu),
                             reads=[bg.name], writes=[sgt.name])
                        T.op("dve", lambda e, bu=bu, sgt=sgt, j=j, pc=pc: e.tensor_tensor(
                            out=hT[:, j, pc * pw:(pc + 1) * pw], in0=sgt[:], in1=bu[:, 0:pw], op=ALU.mult),
                            reads=[sgt.name, bu.name], writes=[(hT.name, j, pc)])
            for cb in range(4):
                T.dma("sp", xr.name, xr[:], xres_d[r0:r0 + TT, cb * 512:(cb + 1) * 512].rearrange("(b p) n -> p b n", p=128), writes=[xr.name])
                T.op("act", lambda e: e.mul(out=xr[:], in_=xr[:], mul=ALPHA), reads=[xr.name], writes=[xr.name])
                for j4 in range(NFC // 4):
                    w = wdn[wq % 3]
                    wq += 1
                    jr = slice(j4 * 512, (j4 + 1) * 512)
                    cs_ = slice(cb * 512, (cb + 1) * 512)
                    if wdn_cache is not None and t > 0:
                        T.dma("pool", w.name, w[:], wdn_cache[jr, cs_].rearrange("(a p) n -> p a n", p=128),
                              reads=[("wdnc", cb, j4)], writes=[w.name])
                    else:
                        T.dma("pool", w.name, w[:], wdn_d[jr, cs_].rearrange("(a p) n -> p a n", p=128), writes=[w.name])
                        if wdn_cache is not None and cfg.ntiles > 1:
                            T.dma("sp", w.name + "_wb", wdn_cache[jr, cs_].rearrange("(a p) n -> p a n", p=128), w[:],
                                  reads=[w.name], writes=[("wdnc", cb, j4)])
                    first, last = (j4 == 0), (j4 == NFC // 4 - 1)
                    wr = [pb[b].name for b in range(TB)] if (first or last) else []
                    T.op("pe", [lambda e, b=b, j=j4 * 4 + jj, jj=jj, w=w: e.matmul(
                        pb[b][:, :], lhsT=hT[:, j, b * 128:(b + 1) * 128], rhs=w[:, jj, :], start=(j == 0), stop=(j == NFC - 1))
                        for jj in range(4) for b in range(TB)],
                        reads=[w.name] + [(hT.name, j4 * 4 + jj, pc) for jj in range(4) for pc in range(npc)], writes=wr)
                for b in range(TB):
                    T.op("dve", lambda e, b=b: e.scalar_tensor_tensor(
                        out=so[:, b, :], in0=pb[b][:, :], scalar=0.5, in1=xr[:, b, :], op0=ALU.mult, op1=ALU.add),
                        reads=[pb[b].name, xr.name], writes=[(so.name, b)])
                T.dma("sp", so.name, dst_pre[r0:r0 + TT, cb * 512:(cb + 1) * 512].rearrange("(b p) n -> p b n", p=128), so[:],
                      reads=[(so.name, b) for b in range(TB)])
        T.flush()


def phase_win(nc, T, cfg, pb, S, W, O, name):
    R = cfg.R
    nb = cfg.nblocks
    with ExitStack() as st:
        sb = _sbuf(nc, st, name)
        x1T = sb("x1T", [128, NKC, R], BF16)
        wsl = [sb(f"w{i}", [128, NKC, 512], BF16) for i in range(2)]
        stg = [sb(f"stg{i}", [128, 512]) for i in range(4)]
        fstg = [sb(f"fstg{i}", [128, R]) for i in range(2)]
        tl = [sb(f"tl{i}", [3, 512]) for i in range(2)]
        zt = sb("zt", [128, RW_COLS])
        x1Tv = S["X1T"].rearrange("(kc p) r -> p kc r", p=128)
        wv = W["w_in"].rearrange("(kc p) n -> p kc n", p=128)
        TT = cfg.tile_blocks * 128
        for t in range(cfg.ntiles):
            T.dma("sp", f"{name}_x1T{t % 2}", x1T[:, :, t * TT:(t + 1) * TT], x1Tv[:, :, t * TT:(t + 1) * TT], writes=[(x1T.name, t)])
        x1keys = [(x1T.name, t) for t in range(cfg.ntiles)]
        T.op("pool", lambda e: e.memset(zt[:], 0.0), writes=[zt.name])
        T.dma("sp", f"{name}_z0", S["P_RW"][0:1, :], zt[0:1, :], reads=[zt.name])
        T.dma("sp", f"{name}_z1", S["PT_XBC"].rearrange("(c p) r -> p c r", p=128)[:, :, 0:4],
              zt[:, 0:96].rearrange("p (c r) -> p c r", r=4), reads=[zt.name])
        groups = []
        for c0 in range(0, RW_COLS, 512):
            groups.append((c0, min(512, RW_COLS - c0), S["P_RW"], c0, 1))
        for c0 in range(0, D, 512):
            groups.append((RW_COLS + c0, 512, S["P_Z"], c0, 0))
        groups.append((8640, SSM_H, S["P_DT"], 0, 0))
        q = 0
        wq = 0
        sq = 0
        for (c0, n, dst, dc0, roff) in groups:
            w = wsl[wq % 2]
            wq += 1
            T.dma("pool", w.name, w[:, :, 0:n], wv[:, :, c0:c0 + n], writes=[w.name])
            for b in range(nb):
                bank = pb[q % 8]
                q += 1
                T.op("pe", [lambda e, kc=kc, bank=bank, w=w, b=b, n=n: e.matmul(
                    bank[:, 0:n], lhsT=x1T[:, kc, b * 128:(b + 1) * 128], rhs=w[:, kc, 0:n],
                    start=(kc == 0), stop=(kc == NKC - 1)) for kc in range(NKC)],
                    reads=[w.name] + x1keys, writes=[bank.name])
                sg = stg[sq % 4]
                if sq % 2 == 0:
                    T.op("act", lambda e, sg=sg, bank=bank, n=n: e.copy(out=sg[:, 0:n], in_=bank[:, 0:n]), reads=[bank.name], writes=[sg.name])
                else:
                    T.op("dve", lambda e, sg=sg, bank=bank, n=n: e.tensor_copy(out=sg[:, 0:n], in_=bank[:, 0:n]), reads=[bank.name], writes=[sg.name])
                sq += 1
                T.dma("sp", sg.name, dst[roff + b * 128:roff + (b + 1) * 128, dc0:dc0 + n], sg[:, 0:n], reads=[sg.name])
        PW = 384
        npc = R // PW
        fq = 0
        tq = 0
        for cg in range(6):
            c0 = 5568 + cg * 512
            w = wsl[wq % 2]
            wq += 1
            T.dma("pool", w.name, w[:], wv[:, :, c0:c0 + 512], writes=[w.name])
            for cc in range(4):
                fs = fstg[fq % 2]
                fq += 1
                for pc in range(npc):
                    bank = pb[q % 8]
                    q += 1
                    T.op("pe", [lambda e, kc=kc, bank=bank, w=w, cc=cc, pc=pc: e.matmul(
                        bank[:, 0:PW], lhsT=w[:, kc, cc * 128:(cc + 1) * 128], rhs=x1T[:, kc, pc * PW:(pc + 1) * PW],
                        start=(kc == 0), stop=(kc == NKC - 1)) for kc in range(NKC)],
                        reads=[w.name] + x1keys, writes=[bank.name])
                    if sq % 2 == 0:
                        T.op("act", lambda e, fs=fs, bank=bank, pc=pc: e.copy(out=fs[:, pc * PW:(pc + 1) * PW], in_=bank[:, 0:PW]),
                             reads=[bank.name], writes=[(fs.name, pc)])
                    else:
                        T.op("dve", lambda e, fs=fs, bank=bank, pc=pc: e.tensor_copy(out=fs[:, pc * PW:(pc + 1) * PW], in_=bank[:, 0:PW]),
                             reads=[bank.name], writes=[(fs.name, pc)])
                    sq += 1
                ch = cg * 4 + cc
                T.dma("sp", fs.name, S["PT_XBC"][ch * 128:(ch + 1) * 128, 4:4 + R], fs[:], reads=[(fs.name, pc) for pc in range(npc)])
            for si, rend in enumerate(cfg.seq_end):
                bank = pb[q % 8]
                q += 1
                T.op("pe", [lambda e, kc=kc, bank=bank, w=w, rend=rend: e.matmul(
                    bank[0:3, :], lhsT=x1T[:, kc, rend - 3:rend], rhs=w[:, kc, :],
                    start=(kc == 0), stop=(kc == NKC - 1)) for kc in range(NKC)],
                    reads=[w.name] + x1keys, writes=[bank.name])
                tt = tl[tq % 2]
                tq += 1
                T.op("dve", lambda e, tt=tt, bank=bank: e.tensor_copy(out=tt[:], in_=bank[0:3, :]), reads=[bank.name], writes=[tt.name])
                T.dma("sp", tt.name, O["o_conv"][si * 3:(si + 1) * 3, cg * 512:(cg + 1) * 512], tt[:], reads=[tt.name])
        T.flush()


class _Stop(Exception):
    pass


def phase_mix(nc, T, cfg, pb, S, W, I, O, name):
    try:
        _phase_mix(nc, T, cfg, pb, S, W, I, O, name)
    except _Stop:
        pass


def _phase_mix(nc, T, cfg, pb, S, W, I, O, name):
    R = cfg.R

    def chk(n):
        if cfg.mix_stop == n:
            T.flush()
            raise _Stop()
    NH = 4
    HW = NH * 64
    NSTR = 2
    with ExitStack() as st:
        sb = _sbuf(nc, st, name)
        bq = [0]
        bpool = [list(range(8))]

        def bank():
            p = bpool[0]
            b = pb[p[bq[0] % len(p)]]
            bq[0] += 1
            return b

        kpref = [""]
        LOCALK = set(["ew", "av", "kkn", "kp", "Ep", "Em", "Wp", "rt", "kt", "bt", "at", "gg", "tA", "tB", "Us", "Ys", "yo", "vb", "btb", "ktb",
                      "n2", "rn", "s1", "s2", "mean", "var", "bs"])

        class _TP:
            @staticmethod
            def _m(keys):
                return [kpref[0] + k if (isinstance(k, str) and k in LOCALK) else k for k in keys]

            @staticmethod
            def op(eng, fns, reads=(), writes=()):
                T.op(eng, fns, reads=_TP._m(reads), writes=_TP._m(writes))
        TP = _TP

        def TT(eng, out, in0, in1, op, r, w):
            TP.op(eng, lambda e: e.tensor_tensor(out=out, in0=in0, in1=in1, op=op), reads=r, writes=w)

        def TS(eng, out, in0, s1, s2, op0, op1, r, w):
            if s2 is None:
                TP.op(eng, lambda e: e.tensor_scalar(out=out, in0=in0, scalar1=s1, scalar2=None, op0=op0), reads=r, writes=w)
            else:
                TP.op(eng, lambda e: e.tensor_scalar(out=out, in0=in0, scalar1=s1, scalar2=s2, op0=op0, op1=op1), reads=r, writes=w)

        def STT(out, in0, sc, in1, op0, op1, r, w):
            TP.op("dve", lambda e: e.scalar_tensor_tensor(out=out, in0=in0, scalar=sc, in1=in1, op0=op0, op1=op1), reads=r, writes=w)

        def ACT(out, in_, func, r, w, bias=0.0, scale=1.0):
            TP.op("act", lambda e: e.activation(out=out, in_=in_, func=func, bias=bias, scale=scale), reads=r, writes=w)

        def CP(eng, out, in_, r, w):
            if eng == "act":
                TP.op("act", lambda e: e.copy(out=out, in_=in_), reads=r, writes=w)
            else:
                TP.op(eng, lambda e: e.tensor_copy(out=out, in_=in_), reads=r, writes=w)

        def RED(out, in_, r, w):
            TP.op("dve", lambda e: e.tensor_reduce(out=out, in_=in_, axis=AX.X, op=ALU.add), reads=r, writes=w)

        def RECIP(out, in_, r, w):
            TP.op("dve", lambda e: e.reciprocal(out=out, in_=in_), reads=r, writes=w)

        def MM(specs, r, w):
            TP.op("pe", [lambda e, sp=sp: e.matmul(sp[0], lhsT=sp[1], rhs=sp[2], start=sp[3], stop=sp[4]) for sp in specs], reads=r, writes=w)

        def TR(specs, r, w):
            TP.op("pe", [lambda e, sp=sp: e.transpose(out=sp[0], in_=sp[1], identity=sp[2]) for sp in specs], reads=r, writes=w)

        def LD(t, out, in_, w, r=()):
            T.dma("sp", t, out, in_, reads=r, writes=w)

        ident = sb("ident", [128, 128])
        make_ident(T, ident)
        tri = sb("tri", [64, 64])
        msl = sb("msl", [64, 2, 64])
        mgt = sb("mgt", [64, 64])
        ones = sb("ones", [64, 128])
        rowmask = sb("rowmask", [64, 1])

        def sel(t, ap, pattern, base, cm):
            T.op("pool", lambda e: e.memset(ap, 1.0), writes=[t.name])
            T.op("pool", lambda e: e.affine_select(out=ap, in_=ap, pattern=pattern, compare_op=ALU.is_ge, fill=0.0,
                                                   base=base, channel_multiplier=cm), reads=[t.name], writes=[t.name])
        sel(tri, tri[:], [[1, 64]], 0, -1)
        sel(msl, msl[:, 0, :], [[1, 64]], -1, -1)
        sel(msl, msl[:, 1, :], [[1, 64]], 0, -1)
        sel(mgt, mgt[:], [[-1, 64]], -1, 1)
        sel(rowmask, rowmask[:], [[0, 1]], -48, 1)
        T.op("pool", lambda e: e.memset(ones[:], 1.0), writes=[ones.name])

        T.begin_group("mix_setup")

        def bc_load(n, src, cols):
            t = sb(n, [64, cols])
            LD(t.name, t[:], src.partition_broadcast(64), [t.name])
            return t
        mu_bc = bc_load("mu_bc", W["rw_mu"], RW_COLS)
        w0_bc = bc_load("w0_bc", W["rw_w0"], RW_DIM)
        a0_bc = bc_load("a0_bc", W["rw_a0"], RW_DIM)
        kk_bc = bc_load("kk_bc", W["rw_kk"], RW_DIM)
        ka_bc = bc_load("ka_bc", W["rw_ka"], RW_DIM)
        rk_bc = bc_load("rk_bc", W["rw_rk"], RW_DIM)
        dtb_bc = bc_load("dtb_bc", W["dt_bias"], SSM_H)
        aneg_bc = bc_load("aneg_bc", W["a_log"], SSM_H)
        dsk_bc = bc_load("dsk_bc", W["d_skip"], SSM_H)
        gnw_bc = bc_load("gnw_bc", W["rw_gn_w"], RW_DIM)
        gnb_bc = bc_load("gnb_bc", W["rw_gn_b"], RW_DIM)
        w2 = sb("w2", [128, RW_DIM])
        a2 = sb("a2", [128, RW_DIM], BF16)
        g2 = sb("g2", [128, 2, RW_DIM], BF16)
        T.op("pool", lambda e: e.memset(w2[:], 0.0), writes=[w2.name])
        T.op("pool", lambda e: e.memset(a2[:], 0.0), writes=[a2.name])
        LD(w2.name, w2[0:96, :], W["rw_w2"][:, :], [w2.name])
        T.dma("pool", a2.name, a2[0:96, :], W["rw_a2"][:, :], writes=[a2.name])
        T.dma("pool", g2.name, g2[:], W["rw_g2"].rearrange("(c p) n -> p c n", p=128), writes=[g2.name])
        cw = sb("cw", [128, 24, 4])
        cb = sb("cb", [128, 24])
        snw = sb("snw", [128, 16])
        LD(cw.name, cw[:], W["conv_w"].rearrange("p (c i) -> p c i", i=4), [cw.name])
        LD(cb.name, cb[:], W["conv_b"][:, :], [cb.name])
        LD(snw.name, snw[:], W["ssm_norm_w"][:, :], [snw.name])
        T.end_group()
        ACT(aneg_bc[:], aneg_bc[:], AF.Exp, [aneg_bc.name], [aneg_bc.name])
        T.op("act", lambda e: e.mul(out=aneg_bc[:], in_=aneg_bc[:], mul=-1.0), reads=[aneg_bc.name], writes=[aneg_bc.name])
        XAB = sb("XAB", [128, 2, 24, 64])
        XA = XAB[:, 0]
        XB2 = XAB[:, 1]
        XAk, XBk = "XA", "XB2"
        histT = sb("histT", [128, 24, 9])
        hrow = XAB[0:9, 0].rearrange("p c t -> p (c t)")
        for half in range(2):
            LD("mix_hrow", hrow, I["hist_conv"][:, half * 1536:(half + 1) * 1536], [XAk])
            for q4 in range(3):
                bk = bank()
                TR([(bk[:, j * 9:(j + 1) * 9], hrow[:, (q4 * 4 + j) * 128:(q4 * 4 + j + 1) * 128], ident[0:9, 0:9]) for j in range(4)],
                   [XAk, ident.name], [bk.name])
                c0_ = half * 12 + q4 * 4
                CP("dve", histT[:, c0_:c0_ + 4, :], bk[:, 0:36].rearrange("p (j s) -> p j s", j=4), [bk.name], [histT.name])

        chk(1)
        ST = sb("ST", [64, RW_H, 64])
        STb = sb("STb", [64, RW_H, 64], BF16)
        HT = sb("HT", [128, SSM_H * 64])
        stg_ws = [sb(f"stg_w{i}", [64, 8, 64]) for i in range(NSTR)]
        stg_h = XAB[:].rearrange("p a c t -> p (a c t)")[:, 0:2048].rearrange("p (c n) -> p c n", c=16)
        SHk = [XAk, XBk]

        def load_states_rw(seq, sid):
            stg_w = stg_ws[sid]
            r0_ = seq * 1024 + sid * 512
            LD(stg_w.name, stg_w[:], I["wkv0"][r0_:r0_ + 512, :].rearrange("(h v) k -> v h k", v=64), [stg_w.name])
            bk = bank()
            TR([(bk[0:64, j * 64:(j + 1) * 64], stg_w[:, j, :], ident[0:64, 0:64]) for j in range(8)],
               [stg_w.name, ident.name], [bk.name])
            CP("act", ST[:, sid * 8:(sid + 1) * 8, :], bk[0:64, :].rearrange("p (h v) -> p h v", h=8), [bk.name],
               [(ST.name, 2 * sid), (ST.name, 2 * sid + 1)])
            CP("dve", STb[:, sid * 8:(sid + 1) * 8, :], bk[0:64, :].rearrange("p (h v) -> p h v", h=8), [bk.name],
               [(STb.name, 2 * sid), (STb.name, 2 * sid + 1)])

        def load_states_ssd(seq):
            LD("mix_stg_h", stg_h, I["ssm0"][seq * 2048:(seq + 1) * 2048, :].rearrange("(c p) n -> p c n", p=128), SHk)
            for g in range(4):
                bk = bank()
                TR([(bk[:, j * 128:(j + 1) * 128], stg_h[:, g * 4 + j, :], ident[:]) for j in range(4)],
                   SHk + [ident.name], [bk.name])
                CP("dve", HT[:, g * 512:(g + 1) * 512], bk[:, :], [bk.name], [(HT.name, g)])

        def store_states_rw(seq, sid):
            stg_w = stg_ws[sid]
            r0_ = seq * 1024 + sid * 512
            bk = bank()
            TR([(bk[0:64, j * 64:(j + 1) * 64], ST[:, sid * 8 + j, :], ident[0:64, 0:64]) for j in range(8)],
               [(ST.name, 2 * sid), (ST.name, 2 * sid + 1), ident.name], [bk.name])
            CP("act", stg_w[:], bk[0:64, :].rearrange("p (h k) -> p h k", h=8), [bk.name], [stg_w.name])
            T.dma("sp", stg_w.name + "_st", O["o_wkv"][r0_:r0_ + 512, :].rearrange("(h v) k -> v h k", v=64), stg_w[:],
                  reads=[stg_w.name])

        def store_states_ssd(seq):
            for g in range(4):
                bk = bank()
                TR([(bk[:, j * 128:(j + 1) * 128], HT[:, (g * 4 + j) * 128:(g * 4 + j + 1) * 128], ident[:]) for j in range(4)],
                   [(HT.name, g), ident.name], [bk.name])
                CP("dve", stg_h[:, g * 4:(g + 1) * 4, :], bk[:, :].rearrange("p (j n) -> p j n", j=4), [bk.name], SHk)
            T.dma("sp", "mix_stg_h_st", O["o_ssm"][seq * 2048:(seq + 1) * 2048, :].rearrange("(c p) n -> p c n", p=128), stg_h,
                  reads=SHk)

        for si, rend in enumerate(cfg.seq_end):
            T.dma("sp", f"{name}_shift", O["o_shift"][si:si + 1, :], S["P_RW"][rend:rend + 1, :])

        identb = sb("identb", [64, 64], BF16)
        CP("pool", identb[:], ident[0:64, 0:64], [ident.name], [identb.name])
        names = ["ew", "av", "kkn", "kp", "Ep", "Em", "Wp", "rt", "kt", "bt", "at", "gg", "tA", "tB", "Us", "Ys", "yo"]

        def mk_rw(i):
            p = f"r{i}_"
            return dict(
                cur_l=sb(p + "cur_l", [64, 448]), prev_l=sb(p + "prev_l", [64, 448]), lT=sb(p + "lT", [128, 64]),
                lTb=sb(p + "lTb", [128, 3, 64], BF16),
                cur3=[sb(p + f"cur3{j}", [64, 3, HW]) for j in range(2)], prev3=[sb(p + f"prev3{j}", [64, 3, HW]) for j in range(2)],
                W_={n: sb(p + n, [64, HW], BF16 if n in ("Us", "vb", "btb", "ktb") else F32) for n in names + ["vb", "btb", "ktb"]},
                ARTb=sb(p + "ARTb", [64, NH, 2, 64], BF16),
                btT=sb(p + "btT", [64, NH, 64], BF16), ktT=sb(p + "ktT", [64, NH, 64], BF16),
                AB=sb(p + "AB", [64, NH, 2, 64], BF16), AK=sb(p + "AK", [64, NH, 2, 64], BF16),
                Pm=[sb(p + f"Pm{j}", [64, NH, 64], BF16) for j in range(2)],
                Qm=[sb(p + f"Qm{j}", [64, NH, 64], BF16) for j in range(2)],
                XT=sb(p + "XT", [64, NH, 64], BF16), RHSb=sb(p + "RHSb", [64, HW], BF16), WC=sb(p + "WC", [64, NH]),
                sm={n: sb(p + n, [64, NH]) for n in ["n2", "rn", "s1", "s2", "mean", "var", "bs"]},
                yT=[sb(p + f"yT{j}", [128, HW // 128, 64], BF16) for j in range(2)])
        RWB = [mk_rw(i) for i in range(NSTR)]
        Fin = sb("Fin", [128, 24, 67])
        Btm = sb("Btm", [64, 512], BF16)
        pdt = sb("pdt", [64, SSM_H])
        dts = {n: sb(n, [64, SSM_H]) for n in ["dt", "adt", "acs", "eacs", "toend"]}
        cdec = sb("cdec", [128, SSM_H])
        gn = ["xs", "zz", "yy", "xdt", "xdtw", "ML", "EX", "MT", "t1"]
        G_ = {n: sb("g_" + n, [64, 512], BF16 if n in ("xdt", "xdtw", "MT") else F32) for n in gn}
        cbm = sb("cbm", [64, 64])
        ss1 = sb("ss1", [64, 1])
        yT2s = [sb(f"yT2{j}", [128, 4, 64], BF16) for j in range(2)]
        ZZ = [G_["zz"], sb("g_zz1", [64, 512])]
        prw3 = S["P_RW"][:, 0:3072].rearrange("t (j c) -> t j c", j=3)
        hrw3 = I["hist_rw"][:, 0:3072].rearrange("t (j c) -> t j c", j=3)
        yrwT = S["YRWT"].rearrange("(c p) r -> p c r", p=128)
        yssT = S["YSSMT"].rearrange("(c p) r -> p c r", p=128)
        xbcv = S["PT_XBC"].rearrange("(c p) r -> p c r", p=128)

        def h3(ap):
            return ap.rearrange("p (h k) -> p h k", h=NH)

        def h3s(ap):
            return ap.rearrange("p (h k) -> p h k", h=8)

        def bc8s(ap):
            return ap.unsqueeze(2).to_broadcast([64, 8, 64])

        def bc8(ap):
            return ap.unsqueeze(2).to_broadcast([64, NH, 64])

        def rwkv_chunk(c, sid):
            Bf = RWB[sid]
            cur_l, prev_l, lT, W_ = Bf["cur_l"], Bf["prev_l"], Bf["lT"], Bf["W_"]
            lTb = Bf["lTb"]
            btT, ktT, AB, AK, Pm, Qm = Bf["btT"], Bf["ktT"], Bf["AB"], Bf["AK"], Bf["Pm"], Bf["Qm"]
            ARTb = Bf["ARTb"]
            XT, RHSb, WC, sm = Bf["XT"], Bf["RHSb"], Bf["WC"], Bf["sm"]
            t0 = c * 64
            seq = min(c, 2)
            short = c < 3
            if c < 3:
                load_states_rw(seq, sid)
            LD(cur_l.name, cur_l[:], S["P_RW"][1 + t0:1 + t0 + 64, 3072:3520], [cur_l.name])
            LD(prev_l.name, prev_l[:], S["P_RW"][t0:t0 + 64, 3072:3520], [prev_l.name])
            if short:
                LD(prev_l.name, prev_l[48:49, :], I["hist_rw"][seq:seq + 1, 3072:3520], [prev_l.name])
            TT("dve", prev_l[:], prev_l[:], cur_l[:], ALU.subtract, [prev_l.name, cur_l.name], [prev_l.name])
            TT("pool", prev_l[:], prev_l[:], mu_bc[:, 3072:3520], ALU.mult, [prev_l.name, mu_bc.name], [prev_l.name])
            TT("dve", cur_l[:], cur_l[:], prev_l[:], ALU.add, [prev_l.name, cur_l.name], [cur_l.name])
            ACT(cur_l[:, 0:96], cur_l[:, 0:96], AF.Tanh, [cur_l.name], [cur_l.name])
            ACT(cur_l[:, 192:448], cur_l[:, 192:448], AF.Sigmoid, [cur_l.name], [cur_l.name])
            bk = bank()
            TR([(bk[:, j * 64:(j + 1) * 64], cur_l[:, o_:o_ + 128], ident[0:64, 0:64]) for j, o_ in enumerate((0, 96, 192, 320))],
               [cur_l.name, ident.name], [bk.name])
            CP("act", lT[:], bk[:, 0:64], [bk.name], [lT.name])
            CP("dve", lTb[:], bk[:, 64:256].rearrange("p (a t) -> p a t", a=3), [bk.name], [lTb.name])
            for qq in range(2):
                hh = 2 * sid + qq
                cur3, prev3, yT = Bf["cur3"][qq], Bf["prev3"][qq], Bf["yT"][qq]
                cs = hh * HW
                h0 = hh * NH
                w = W_
                LD(cur3.name, cur3[:], prw3[1 + t0:1 + t0 + 64, :, cs:cs + HW], [cur3.name])
                LD(prev3.name, prev3[:], prw3[t0:t0 + 64, :, cs:cs + HW], [prev3.name])
                if short:
                    LD(prev3.name, prev3[48:49, :, :], hrw3[seq:seq + 1, :, cs:cs + HW], [prev3.name])
                mu3 = mu_bc[:, 0:3072].rearrange("p (j c) -> p j c", j=3)[:, :, cs:cs + HW]
                TT("dve", prev3[:], prev3[:], cur3[:], ALU.subtract, [prev3.name, cur3.name], [prev3.name])
                TT("pool", prev3[:], prev3[:], mu3, ALU.mult, [prev3.name, mu_bc.name], [prev3.name])
                TT("dve", cur3[:], cur3[:], prev3[:], ALU.add, [prev3.name, cur3.name], [cur3.name])
                r_, k_, v_ = cur3[:, 0, :], cur3[:, 1, :], cur3[:, 2, :]
                c3 = [cur3.name]
                b_lw, b_la, b_lg = bank(), bank(), bank()
                MM([(b_lw[0:64, 0:HW], lT[:], w2[:, cs:cs + HW], True, True)], [lT.name, w2.name], [b_lw.name])
                MM([(b_la[0:64, 0:HW], lTb[:, 0, :], a2[:, cs:cs + HW], True, True)], [lTb.name, a2.name], [b_la.name])
                MM([(b_lg[0:64, 0:HW], lTb[:, 1, :], g2[:, 0, cs:cs + HW], True, False),
                    (b_lg[0:64, 0:HW], lTb[:, 2, :], g2[:, 1, cs:cs + HW], False, True)], [lTb.name, g2.name], [b_lg.name])
                CP("act", w["gg"][:], b_lg[0:64, 0:HW], [b_lg.name], ["gg"])
                TT("dve", w["tA"][:], b_lw[0:64, 0:HW], w0_bc[:, cs:cs + HW], ALU.add, [b_lw.name, w0_bc.name], ["tA"])
                ACT(w["tA"][:], w["tA"][:], AF.Exp, ["tA"], ["tA"], scale=-1.0)
                ACT(w["tA"][:], w["tA"][:], AF.Ln, ["tA"], ["tA"], bias=1.0)
                ACT(w["ew"][:], w["tA"][:], AF.Exp, ["tA"], ["ew"], bias=-0.5, scale=-1.0)
                if short:
                    TS("dve", w["ew"][:], w["ew"][:], rowmask[:, 0:1], None, ALU.mult, None, ["ew", rowmask.name], ["ew"])
                TT("dve", w["av"][:], b_la[0:64, 0:HW], a0_bc[:, cs:cs + HW], ALU.add, [b_la.name, a0_bc.name], ["av"])
                ACT(w["av"][:], w["av"][:], AF.Sigmoid, ["av"], ["av"])
                TT("pool", w["kkn"][:], k_, kk_bc[:, cs:cs + HW], ALU.mult, c3 + [kk_bc.name], ["kkn"])
                ACT(w["tB"][:], w["kkn"][:], AF.Square, ["kkn"], ["tB"])
                RED(sm["n2"][:], h3(w["tB"][:]), ["tB"], ["n2"])
                TS("dve", sm["n2"][:], sm["n2"][:], 1e-24, None, ALU.max, None, ["n2"], ["n2"])
                ACT(sm["n2"][:], sm["n2"][:], AF.Ln, ["n2"], ["n2"])
                ACT(sm["rn"][:], sm["n2"][:], AF.Exp, ["n2"], ["rn"], scale=-0.5)
                TT("dve", h3(w["kkn"][:]), h3(w["kkn"][:]), bc8(sm["rn"][:]), ALU.mult, ["kkn", "rn"], ["kkn"])
                if short:
                    TS("dve", w["kkn"][:], w["kkn"][:], rowmask[:, 0:1], None, ALU.mult, None, ["kkn", rowmask.name], ["kkn"])
                STT(w["tB"][:], w["av"][:], -1.0, ka_bc[:, cs:cs + HW], ALU.add, ALU.mult, ["av", ka_bc.name], ["tB"])
                STT(w["kp"][:], w["tB"][:], 1.0, k_, ALU.add, ALU.mult, ["tB"] + c3, ["kp"])
                if short:
                    TS("dve", w["kp"][:], w["kp"][:], rowmask[:, 0:1], None, ALU.mult, None, ["kp", rowmask.name], ["kp"])
                b_cn = bank()
                MM([(b_cn[0:64, 0:HW], tri[:], w["ew"][:], True, True)], [tri.name, "ew"], [b_cn.name])
                b_wc = bank()
                MM([(b_wc[0:64, j:j + 1], w["ew"][:, j * 64:(j + 1) * 64], ones[:, 0:1], True, True) for j in range(NH)],
                   ["ew", ones.name], [b_wc.name])
                ACT(WC[:], b_wc[0:64, 0:NH], AF.Exp, [b_wc.name], [WC.name], scale=-1.0)
                ACT(w["Ep"][:], b_cn[0:64, 0:HW], AF.Exp, [b_cn.name], ["Ep"], scale=-1.0)
                ACT(w["Em"][:], b_cn[0:64, 0:HW], AF.Exp, [b_cn.name], ["Em"])
                TT("dve", w["tA"][:], b_cn[0:64, 0:HW], w["ew"][:], ALU.subtract, [b_cn.name, "ew"], ["tA"])
                ACT(w["Wp"][:], w["tA"][:], AF.Exp, ["tA"], ["Wp"], scale=-1.0)
                TT("dve", w["rt"][:], r_, w["Ep"][:], ALU.mult, c3 + ["Ep"], ["rt"])
                TT("pool", w["kt"][:], w["kp"][:], w["Em"][:], ALU.mult, ["kp", "Em"], ["kt"])
                TT("pool", w["bt"][:], w["kkn"][:], w["av"][:], ALU.mult, ["kkn", "av"], ["bt"])
                TT("pool", w["bt"][:], w["bt"][:], w["Em"][:], ALU.mult, ["bt", "Em"], ["bt"])
                STT(w["at"][:], w["kkn"][:], -1.0, w["Wp"][:], ALU.mult, ALU.mult, ["kkn", "Wp"], ["at"])
                i64 = ident[0:64, 0:64]
                for src, dst, dkey, eng in (("at", ARTb[:, :, 0, :], (ARTb.name, 0), "act"), ("rt", ARTb[:, :, 1, :], (ARTb.name, 1), "dve"),
                                            ("bt", btT[:], btT.name, "act"), ("kt", ktT[:], ktT.name, "dve")):
                    bk = bank()
                    TR([(bk[0:64, j * 64:(j + 1) * 64], w[src][:, j * 64:(j + 1) * 64], i64) for j in range(NH)], [src, ident.name], [bk.name])
                    CP(eng, dst, bk[0:64, 0:HW].rearrange("p (h t) -> p h t", h=NH), [bk.name], [dkey])
                ARTbk = [(ARTb.name, 0), (ARTb.name, 1)]
                CP("act", w["vb"][:], v_, c3, ["vb"])
                CP("act", w["btb"][:], w["bt"][:], ["bt"], ["btb"])
                CP("act", w["ktb"][:], w["kt"][:], ["kt"], ["ktb"])
                for lhs, lkey, dstt in ((btT, btT.name, AB), (ktT, ktT.name, AK)):
                    for hb in range(NH // 4):
                        bk = bank()
                        MM([(bk[0:64, j * 128:(j + 1) * 128], lhs[:, hb * 4 + j, :], ARTb[:, hb * 4 + j, :, :].rearrange("p a t -> p (a t)"), True, True)
                            for j in range(4)], [lkey] + ARTbk, [bk.name])
                        TT("dve", dstt[:, hb * 4:(hb + 1) * 4, :, :], bk[0:64, :].rearrange("p (h a t) -> p h a t", h=4, a=2),
                           msl[:].unsqueeze(1).to_broadcast([64, 4, 2, 64]), ALU.mult, [bk.name, msl.name], [(dstt.name, hb)])
                ABk = [(AB.name, i_) for i_ in range(NH // 4)]
                AKk = [(AK.name, i_) for i_ in range(NH // 4)]
                bk = bank()
                MM([(bk[0:64, j * 64:(j + 1) * 64], ARTb[:, j, 0, :], btT[:, j, :], True, True) for j in range(NH)], ARTbk + [btT.name], [bk.name])
                TT("dve", Pm[0][:], bk[0:64, 0:HW].rearrange("p (h s) -> p h s", h=NH), mgt[:].unsqueeze(1).to_broadcast([64, NH, 64]), ALU.mult,
                   [bk.name, mgt.name], [Pm[0].name])
                Q0 = AB[:, :, 0, :]
                TT("pool", XT[:], Q0, identb[:].unsqueeze(1).to_broadcast([64, NH, 64]), ALU.add, ABk + [identb.name], [XT.name])
                pi = 0
                for lvl in range(1, 6):
                    Pc, Qc, Pn, Qn = Pm[pi], Qm[pi], Pm[1 - pi], Qm[1 - pi]
                    Qcv = Q0 if lvl == 1 else Qc[:]
                    Qck = ABk if lvl == 1 else [Qc.name]
                    bp = bank()
                    MM([(bp[0:64, j * 64:(j + 1) * 64], Qcv[:, j, :], Pc[:, j, :], True, True) for j in range(NH)], [Pc.name] + Qck, [bp.name])
                    CP("act", Pn[:], bp[0:64, 0:HW].rearrange("p (h s) -> p h s", h=NH), [bp.name], [Pn.name])
                    if lvl < 5:
                        bq_ = bank()
                        MM([(bq_[0:64, j * 64:(j + 1) * 64], Pc[:, j, :], Qcv[:, j, :], True, True) for j in range(NH)], [Pc.name] + Qck, [bq_.name])
                        CP("dve", Qn[:], bq_[0:64, 0:HW].rearrange("p (h s) -> p h s", h=NH), [bq_.name], [Qn.name])
                    bz = bank()
                    MM([(bz[0:64, j * 64:(j + 1) * 64], Pn[:, j, :], XT[:, j, :], True, True) for j in range(NH)], [Pn.name, XT.name], [bz.name])
                    TT("dve", XT[:], XT[:], bz[0:64, 0:HW].rearrange("p (h s) -> p h s", h=NH), ALU.add, [XT.name, bz.name], [XT.name])
                    pi = 1 - pi
                STk = (ST.name, hh)
                STbk = (STb.name, hh)
                bk = bank()
                sp_ = []
                for j in range(NH):
                    o = bk[0:64, j * 64:(j + 1) * 64]
                    sp_.append((o, ARTb[:, j, 0, :], STb[:, h0 + j, :], True, False))
                    sp_.append((o, AK[:, j, 0, :], w["vb"][:, j * 64:(j + 1) * 64], False, True))
                MM(sp_, ARTbk + AKk + ["vb", STbk], [bk.name])
                CP("act", RHSb[:], bk[0:64, 0:HW], [bk.name], [RHSb.name])
                bk = bank()
                MM([(bk[0:64, j * 64:(j + 1) * 64], XT[:, j, :], RHSb[:, j * 64:(j + 1) * 64], True, True) for j in range(NH)],
                   [XT.name, RHSb.name], [bk.name])
                CP("dve", w["Us"][:], bk[0:64, 0:HW], [bk.name], ["Us"])
                by = bank()
                sp_ = []
                for j in range(NH):
                    o = by[0:64, j * 64:(j + 1) * 64]
                    sp_.append((o, ARTb[:, j, 1, :], STb[:, h0 + j, :], True, False))
                    sp_.append((o, AB[:, j, 1, :], w["Us"][:, j * 64:(j + 1) * 64], False, False))
                    sp_.append((o, AK[:, j, 1, :], w["vb"][:, j * 64:(j + 1) * 64], False, True))
                MM(sp_, ARTbk + ABk + AKk + ["vb", STbk, "Us"], [by.name])
                bs_ = bank()
                sp_ = []
                for j in range(NH):
                    o = bs_[0:64, j * 64:(j + 1) * 64]
                    sp_.append((o, w["btb"][:, j * 64:(j + 1) * 64], w["Us"][:, j * 64:(j + 1) * 64], True, False))
                    sp_.append((o, w["ktb"][:, j * 64:(j + 1) * 64], w["vb"][:, j * 64:(j + 1) * 64], False, True))
                MM(sp_, ["btb", "ktb", "Us", "vb"], [bs_.name])
                STh = ST[:, h0:h0 + NH, :]
                TT("dve", STh, STh, bs_[0:64, 0:HW].rearrange("p (h v) -> p h v", h=NH), ALU.add, [STk, bs_.name], [STk])
                TT("pool", STh, STh, bc8(WC[:]), ALU.mult, [STk, WC.name], [STk])
                CP("act", STb[:, h0:h0 + NH, :], STh, [STk], [STbk])
                CP("act", w["Ys"][:], by[0:64, 0:HW], [by.name], ["Ys"])
                RED(sm["s1"][:], h3(w["Ys"][:]), ["Ys"], ["s1"])
                ACT(w["tA"][:], w["Ys"][:], AF.Square, ["Ys"], ["tA"])
                RED(sm["s2"][:], h3(w["tA"][:]), ["tA"], ["s2"])
                TS("dve", sm["mean"][:], sm["s1"][:], 1.0 / 64, None, ALU.mult, None, ["s1"], ["mean"])
                TT("dve", sm["var"][:], sm["mean"][:], sm["mean"][:], ALU.mult, ["mean"], ["var"])
                STT(sm["var"][:], sm["s2"][:], 1.0 / 64, sm["var"][:], ALU.mult, ALU.subtract, ["s2", "var"], ["var"])
                ACT(sm["var"][:], sm["var"][:], AF.Ln, ["var"], ["var"], bias=RW_GN_EPS)
                ACT(sm["var"][:], sm["var"][:], AF.Exp, ["var"], ["var"], scale=-0.5)
                TT("dve", h3(w["Ys"][:]), h3(w["Ys"][:]), bc8(sm["mean"][:]), ALU.subtract, ["Ys", "mean"], ["Ys"])
                TT("pool", h3(w["Ys"][:]), h3(w["Ys"][:]), bc8(sm["var"][:]), ALU.mult, ["Ys", "var"], ["Ys"])
                TT("pool", w["Ys"][:], w["Ys"][:], gnw_bc[:, cs:cs + HW], ALU.mult, ["Ys", gnw_bc.name], ["Ys"])
                TT("pool", w["Ys"][:], w["Ys"][:], gnb_bc[:, cs:cs + HW], ALU.add, ["Ys", gnb_bc.name], ["Ys"])
                TT("pool", w["tB"][:], r_, w["kp"][:], ALU.mult, c3 + ["kp"], ["tB"])
                TT("pool", w["tB"][:], w["tB"][:], rk_bc[:, cs:cs + HW], ALU.mult, ["tB", rk_bc.name], ["tB"])
                RED(sm["bs"][:], h3(w["tB"][:]), ["tB"], ["bs"])
                TT("dve", h3(w["tB"][:]), h3(v_), bc8(sm["bs"][:]), ALU.mult, c3 + ["bs"], ["tB"])
                TT("pool", w["Ys"][:], w["Ys"][:], w["tB"][:], ALU.add, ["Ys", "tB"], ["Ys"])
                TT("dve", w["yo"][:], w["Ys"][:], w["gg"][:], ALU.mult, ["Ys", "gg"], ["yo"])
                bk = bank()
                NJ = HW // 128
                TR([(bk[:, j * 64:(j + 1) * 64], w["yo"][:, j * 128:(j + 1) * 128], i64) for j in range(NJ)], ["yo", ident.name], [bk.name])
                CP("act", yT[:], bk[:, 0:NJ * 64].rearrange("p (j t) -> p j t", j=NJ), [bk.name], [yT.name])
                T.dma("act", yT.name, yrwT[:, hh * NJ:(hh + 1) * NJ, t0:t0 + 64], yT[:], reads=[yT.name])
            if c in (0, 1, cfg.nchunks - 1):
                store_states_rw(seq, sid)

        def ssd_chunk(c):
            t0 = c * 64
            seq = min(c, 2)
            short = c < 3
            if c < 3:
                load_states_ssd(seq)
            LD(Fin.name, Fin[:], xbcv[:, :, 4 + t0 - 3:4 + t0 + 64], [Fin.name])
            if short:
                CP("pool", Fin[:, :, 48:51], histT[:, :, seq * 3:(seq + 1) * 3], [histT.name, Fin.name], [Fin.name])
            LD(pdt.name, pdt[:], S["P_DT"][t0:t0 + 64, :], [pdt.name])

            def cwb(i):
                return cw[:, :, i:i + 1].to_broadcast([128, 24, 64])
            TT("pool", XA, Fin[:, :, 0:64], cwb(0), ALU.mult, [Fin.name, cw.name], [XAk])
            for i in range(1, 4):
                TT("pool", XB2, Fin[:, :, i:i + 64], cwb(i), ALU.mult, [Fin.name, cw.name], [XBk])
                TT("dve", XA, XA, XB2, ALU.add, [XAk, XBk], [XAk])
            TT("dve", XA, XA, cb[:].unsqueeze(2).to_broadcast([128, 24, 64]), ALU.add, [XAk, cb.name], [XAk])
            ACT(XB2, XA, AF.Silu, [XAk], [XBk])
            XB = XB2
            d = dts
            TT("dve", d["dt"][:], pdt[:], dtb_bc[:], ALU.add, [pdt.name, dtb_bc.name], ["dt"])
            ACT(d["dt"][:], d["dt"][:], AF.Exp, ["dt"], ["dt"])
            ACT(d["dt"][:], d["dt"][:], AF.Ln, ["dt"], ["dt"], bias=1.0)
            if short:
                TS("dve", d["dt"][:], d["dt"][:], rowmask[:, 0:1], None, ALU.mult, None, ["dt", rowmask.name], ["dt"])
            TT("dve", d["adt"][:], d["dt"][:], aneg_bc[:], ALU.mult, ["dt", aneg_bc.name], ["adt"])
            b_ac = bank()
            MM([(b_ac[0:64, 0:SSM_H], tri[:], d["adt"][:], True, True)], [tri.name, "adt"], [b_ac.name])
            b_tot = bank()
            MM([(b_tot[:, 0:SSM_H], ones[:], d["adt"][:], True, True)], [ones.name, "adt"], [b_tot.name])
            CP("dve", d["acs"][:], b_ac[0:64, 0:SSM_H], [b_ac.name], ["acs"])
            ACT(d["eacs"][:], b_ac[0:64, 0:SSM_H], AF.Exp, [b_ac.name], ["eacs"])
            ACT(cdec[:], b_tot[:, 0:SSM_H], AF.Exp, [b_tot.name], [cdec.name])
            TT("dve", d["toend"][:], b_tot[0:64, 0:SSM_H], d["acs"][:], ALU.subtract, [b_tot.name, "acs"], ["toend"])
            ACT(d["toend"][:], d["toend"][:], AF.Exp, ["toend"], ["toend"])
            bk = bank()
            TR([(bk[0:64, j * 128:(j + 1) * 128], XB[:, 16 + j, :], ident[:]) for j in range(4)], [XBk, ident.name], [bk.name])
            CP("act", Btm[:], bk[0:64, :], [bk.name], [Btm.name])
            g_ = G_
            for g in range(4):
                hs = slice(8 * g, 8 * g + 8)
                yT2 = yT2s[g % 2]
                zz = ZZ[g % 2]
                HTk = (HT.name, g)
                HTg = HT[:, g * 512:(g + 1) * 512]
                bk = bank()
                TR([(bk[0:64, j * 128:(j + 1) * 128], XB[:, 4 * g + j, :], ident[:]) for j in range(4)], [XBk, ident.name], [bk.name])
                CP("act", g_["xs"][:], bk[0:64, :], [bk.name], ["xs"])
                LD(zz.name, zz[:], S["P_Z"][t0:t0 + 64, g * 512:(g + 1) * 512], [zz.name])
                ACT(zz[:], zz[:], AF.Silu, [zz.name], [zz.name])
                TT("dve", h3s(g_["xdt"][:]), h3s(g_["xs"][:]), bc8s(d["dt"][:, hs]), ALU.mult, ["xs", "dt"], ["xdt"])
                TT("pool", h3s(g_["xdtw"][:]), h3s(g_["xdt"][:]), bc8s(d["toend"][:, hs]), ALU.mult, ["xdt", "toend"], ["xdtw"])
                bcb = bank()
                MM([(bcb[0:64, 0:64], XB[:, 16 + g, :], XB[:, 20 + g, :], True, True)], [XBk], [bcb.name])
                TT("dve", cbm[:], bcb[0:64, 0:64], msl[:, 1, :], ALU.mult, [bcb.name, msl.name], [cbm.name])
                TT("pool", h3s(g_["ML"][:]), bc8s(d["adt"][:, hs]), mgt[:].unsqueeze(1).to_broadcast([64, 8, 64]), ALU.mult,
                   ["adt", mgt.name], ["ML"])
                bsg = bank()
                MM([(bsg[0:64, j * 64:(j + 1) * 64], g_["ML"][:, j * 64:(j + 1) * 64], tri[:], True, True) for j in range(8)],
                   ["ML", tri.name], [bsg.name])
                ACT(g_["EX"][:], bsg[0:64, :], AF.Exp, [bsg.name], ["EX"])
                TT("pool", h3s(g_["MT"][:]), h3s(g_["EX"][:]), cbm[:].unsqueeze(1).to_broadcast([64, 8, 64]), ALU.mult, ["EX", cbm.name], ["MT"])
                byd = bank()
                MM([(byd[0:64, j * 64:(j + 1) * 64], g_["MT"][:, j * 64:(j + 1) * 64], g_["xdt"][:, j * 64:(j + 1) * 64], True, True)
                    for j in range(8)], ["MT", "xdt"], [byd.name])
                byo = bank()
                MM([(byo[0:64, :], XB[:, 20 + g, :], HTg, True, True)], [XBk, HTk], [byo.name])
                TT("dve", h3s(g_["yy"][:]), byo[0:64, :].rearrange("p (h k) -> p h k", h=8), bc8s(d["eacs"][:, hs]), ALU.mult,
                   [byo.name, "eacs"], ["yy"])
                TT("dve", g_["yy"][:], g_["yy"][:], byd[0:64, :], ALU.add, ["yy", byd.name], ["yy"])
                TT("pool", h3s(g_["t1"][:]), h3s(g_["xs"][:]), bc8s(dsk_bc[:, hs]), ALU.mult, ["xs", dsk_bc.name], ["t1"])
                TT("pool", g_["yy"][:], g_["yy"][:], g_["t1"][:], ALU.add, ["yy", "t1"], ["yy"])
                TT("dve", g_["yy"][:], g_["yy"][:], zz[:], ALU.mult, ["yy", zz.name], ["yy"])
                ACT(g_["t1"][:], g_["yy"][:], AF.Square, ["yy"], ["t1"])
                T.op("dve", lambda e, o=ss1[:], i=g_["t1"][:]: e.tensor_reduce(out=o, in_=i, axis=AX.X, op=ALU.add), reads=["t1"], writes=[ss1.name])
                ACT(ss1[:], ss1[:], AF.Ln, [ss1.name], [ss1.name], bias=RMS_EPS, scale=1.0 / 512)
                ACT(ss1[:], ss1[:], AF.Exp, [ss1.name], [ss1.name], scale=-0.5)
                TS("dve", g_["yy"][:], g_["yy"][:], ss1[:, 0:1], None, ALU.mult, None, ["yy", ss1.name], ["yy"])
                bk = bank()
                TR([(bk[:, j * 64:(j + 1) * 64], g_["yy"][:, j * 128:(j + 1) * 128], ident[0:64, 0:64]) for j in range(4)], ["yy", ident.name], [bk.name])
                TT("dve", yT2[:], bk[:, 0:256].rearrange("p (j t) -> p j t", j=4), snw[:, 4 * g:4 * g + 4].unsqueeze(2).to_broadcast([128, 4, 64]),
                   ALU.mult, [bk.name, snw.name], [yT2.name])
                T.dma("pool", yT2.name, yssT[:, 4 * g:4 * g + 4, t0:t0 + 64], yT2[:], reads=[yT2.name])
                bst = bank()
                MM([(bst[:, :], Btm[:, g * 128:(g + 1) * 128], g_["xdtw"][:], True, True)], [Btm.name, "xdtw"], [bst.name])
                TT("pool", HTg.rearrange("p (h k) -> p h k", h=8), HTg.rearrange("p (h k) -> p h k", h=8),
                   cdec[:, hs].unsqueeze(2).to_broadcast([128, 8, 64]), ALU.mult, [HTk, cdec.name], [HTk])
                TT("dve", HTg, HTg, bst[:, :], ALU.add, [HTk, bst.name], [HTk])
            if c in (0, 1, cfg.nchunks - 1):
                store_states_ssd(seq)

        def rw_stream(sid):
            bpool[0] = [3 * sid, 3 * sid + 1, 3 * sid + 2]
            kpref[0] = f"s{sid}:"
            for c in range(cfg.nchunks):
                rwkv_chunk(c, sid)
            kpref[0] = ""

        def ssd_stream():
            bpool[0] = [6, 7]
            for c in range(cfg.nchunks):
                ssd_chunk(c)

        streams = [T.capture(lambda i=i: rw_stream(i)) for i in range(NSTR)]
        streams.append(T.capture(ssd_stream))
        T.replay_merged(streams)
        T.flush()


def phase_outp(nc, T, cfg, pb, S, W, name):
    TB = cfg.tile_blocks
    TT = TB * 128
    npc = (TT + 511) // 512
    pw = TT // npc
    GATE0 = 8672
    with ExitStack() as st:
        sb = _sbuf(nc, st, name)
        yrT = sb("yrT", [128, 8, TT], BF16)
        ysT = sb("ysT", [128, NKC, TT], BF16)
        x1T = sb("x1T", [128, NKC, TT], BF16)
        mT = sb("mT", [128, NKC, TT], BF16)
        bg = sb("bg", [128, 32])
        wra = [sb(f"wra{i}", [128, 8, 256], BF16) for i in range(2)]
        wsa = [sb(f"wsa{i}", [128, NKC, 256], BF16) for i in range(2)]
        wga = [sb(f"wga{i}", [128, NKC, 256], BF16) for i in range(2)]
        wgb = [sb(f"wgb{i}", [128, NKC, 256], BF16) for i in range(2)]
        wo = [sb(f"wo{i}", [128, NKC, 512], BF16) for i in range(2)]
        sga = [sb(f"sga{i}", [128, pw]) for i in range(2)]
        sgb = [sb(f"sgb{i}", [128, pw]) for i in range(2)]
        xr = sb("xr", [128, TB, 512])
        so = [sb(f"so{i}", [128, 512]) for i in range(2)]
        T.dma("sp", bg.name, bg[:], W["b_gate"][:, :], writes=[bg.name])
        yrv = S["YRWT"].rearrange("(c p) r -> p c r", p=128)
        ysv = S["YSSMT"].rearrange("(c p) r -> p c r", p=128)
        x1v = S["X1T"].rearrange("(c p) r -> p c r", p=128)
        wrv = W["w_rw_out"].rearrange("(c p) n -> p c n", p=128)
        wsv = W["w_ssm_out"].rearrange("(c p) n -> p c n", p=128)
        wiv = W["w_in"].rearrange("(c p) n -> p c n", p=128)
        wov = W["w_out"].rearrange("(c p) n -> p c n", p=128)
        q = 0
        xq = 0
        for t in range(cfg.ntiles):
            r0 = t * TT
            T.dma("sp", yrT.name, yrT[:], yrv[:, :, r0:r0 + TT], writes=[yrT.name])
            T.dma("sp", ysT.name, ysT[:], ysv[:, :, r0:r0 + TT], writes=[ysT.name])
            T.dma("sp", x1T.name, x1T[:], x1v[:, :, r0:r0 + TT], writes=[x1T.name])
            for cc in range(NKC):
                s = (cc // 2) % 2
                cj = cc % 2
                if cj == 0:
                    c0 = (cc // 2) * 256
                    cg_ = cc // 2
                    srcs = ((wra[s], wrv[:, :, c0:c0 + 256], 0, 8), (wsa[s], wsv[:, :, c0:c0 + 256], 8, NKC),
                            (wga[s], wiv[:, :, GATE0 + c0:GATE0 + c0 + 256], 24, NKC),
                            (wgb[s], wiv[:, :, GATE0 + D + c0:GATE0 + D + c0 + 256], 40, NKC))
                    for wt, src_ap, k0, nk in srcs:
                        cview = S["WOC_B"][cg_].rearrange("p (k n) -> p k n", n=256)[:, k0:k0 + nk, :]
                        ck = ("woc", cg_, k0)
                        if t > 0:
                            T.dma("pool", wt.name, wt[:], cview, reads=[ck], writes=[wt.name])
                        else:
                            T.dma("pool", wt.name, wt[:], src_ap, writes=[wt.name])
                            if cfg.ntiles > 1:
                                T.dma("sp", wt.name + "_wb", cview, wt[:], reads=[wt.name], writes=[ck])
                for pc in range(npc):
                    ps = slice(pc * pw, (pc + 1) * pw)
                    banks = [pb[(4 * q + i) % 8] for i in range(4)]
                    sa, sb2 = sga[q % 2], sgb[q % 2]
                    q += 1
                    for bank, w, act, nk in ((banks[0], wra[s], yrT, 8), (banks[1], wsa[s], ysT, NKC),
                                             (banks[2], wga[s], x1T, NKC), (banks[3], wgb[s], x1T, NKC)):
                        T.op("pe", [lambda e, kc=kc, bank=bank, w=w, act=act, ps=ps, nk=nk, cj=cj: e.matmul(
                            bank[:, 0:pw], lhsT=w[:, kc, cj * 128:(cj + 1) * 128], rhs=act[:, kc, ps], start=(kc == 0), stop=(kc == nk - 1)) for kc in range(nk)],
                            reads=[w.name, act.name], writes=[bank.name])
                    T.op("act", lambda e, sa=sa, bank=banks[2], cc=cc: e.activation(out=sa[:], in_=bank[:, 0:pw], func=AF.Sigmoid, bias=bg[:, cc:cc + 1]),
                         reads=[banks[2].name, bg.name], writes=[sa.name])
                    T.op("act", lambda e, sb2=sb2, bank=banks[3], cc=cc: e.activation(out=sb2[:], in_=bank[:, 0:pw], func=AF.Sigmoid, bias=bg[:, 16 + cc:17 + cc]),
                         reads=[banks[3].name, bg.name], writes=[sb2.name])
                    T.op("dve", lambda e, sa=sa, bank=banks[0]: e.tensor_tensor(out=sa[:], in0=sa[:], in1=bank[:, 0:pw], op=ALU.mult),
                         reads=[sa.name, banks[0].name], writes=[sa.name])
                    T.op("dve", lambda e, sb2=sb2, bank=banks[1]: e.tensor_tensor(out=sb2[:], in0=sb2[:], in1=bank[:, 0:pw], op=ALU.mult),
                         reads=[sb2.name, banks[1].name], writes=[sb2.name])
                    T.op("pool", lambda e, sa=sa, sb2=sb2, cc=cc, ps=ps: e.tensor_tensor(out=mT[:, cc, ps], in0=sa[:], in1=sb2[:], op=ALU.add),
                         reads=[sa.name, sb2.name], writes=[(mT.name, cc, pc)])
            mkeys = [(mT.name, cc, pc) for cc in range(NKC) for pc in range(npc)]
            for cb in range(4):
                w = wo[cb % 2]
                T.dma("pool", w.name, w[:], wov[:, :, cb * 512:(cb + 1) * 512], writes=[w.name])
                T.dma("sp", xr.name, xr[:], S["X1"][r0:r0 + TT, cb * 512:(cb + 1) * 512].rearrange("(b p) n -> p b n", p=128), writes=[xr.name])
                for b in range(TB):
                    bank = pb[q % 8]
                    q += 1
                    T.op("pe", [lambda e, kc=kc, bank=bank, w=w, b=b: e.matmul(
                        bank[:, :], lhsT=mT[:, kc, b * 128:(b + 1) * 128], rhs=w[:, kc, :], start=(kc == 0), stop=(kc == NKC - 1))
                        for kc in range(NKC)], reads=[w.name] + mkeys, writes=[bank.name])
                    rows = r0 + b * 128
                    o = so[xq % 2]
                    xq += 1
                    T.op("dve", lambda e, bank=bank, b=b, o=o: e.scalar_tensor_tensor(
                        out=o[:], in0=xr[:, b, :], scalar=ALPHA, in1=bank[:, :], op0=ALU.mult, op1=ALU.add),
                        reads=[bank.name, xr.name], writes=[o.name])
                    T.dma("sp", o.name, S["XPRE"][rows:rows + 128, cb * 512:(cb + 1) * 512], o[:], reads=[o.name])
        T.flush()


WEIGHT_SPECS = [
    ("ffn1_gu", [D, 2 * DFF]), ("ffn1_dn", [DFF, D]), ("ln1_g", [1, D]), ("ln1_b", [1, D]),
    ("w_in", [D, N_IN]), ("b_gate", [128, 32]),
    ("rw_mu", [1, RW_COLS]), ("rw_w0", [1, RW_DIM]), ("rw_w2", [96, RW_DIM]), ("rw_a0", [1, RW_DIM]),
    ("rw_a2", [96, RW_DIM]), ("rw_g2", [256, RW_DIM]), ("rw_kk", [1, RW_DIM]), ("rw_ka", [1, RW_DIM]),
    ("rw_rk", [1, RW_DIM]), ("rw_gn_w", [1, RW_DIM]), ("rw_gn_b", [1, RW_DIM]),
    ("conv_w", [128, 24 * 4]), ("conv_b", [128, 24]), ("dt_bias", [1, SSM_H]), ("a_log", [1, SSM_H]),
    ("d_skip", [1, SSM_H]), ("ssm_norm_w", [128, 16]),
    ("w_rw_out", [RW_DIM, D]), ("w_ssm_out", [D, D]), ("w_out", [D, D]), ("ln2_g", [1, D]), ("ln2_b", [1, D]),
    ("ffn2_gu", [D, 2 * DFF]), ("ffn2_dn", [DFF, D]), ("ln3_g", [1, D]), ("ln3_b", [1, D]),
]


def build_program(cfg):
    nc = bass.Bass("TRN2", target_bir_lowering=False)
    R = cfg.R
    dbg = "ExternalOutput" if cfg.debug else None

    def din(name, shape, dt=F32):
        return nc.dram_tensor(name, shape, dt, kind="ExternalInput").ap()

    def dout(name, shape, dt=F32):
        return nc.dram_tensor(name, shape, dt, kind="ExternalOutput").ap()

    def dscr(name, shape, dt=F32):
        if name in cfg.scratch_in:
            return nc.dram_tensor(name, shape, dt, kind="ExternalInput").ap()
        if dbg:
            return nc.dram_tensor(name, shape, dt, kind=dbg).ap()
        return nc.dram_tensor(name, shape, dt).ap()

    I = {}
    I["xin"] = din("xin", [R, D])
    I["xinT"] = din("xinT", [D, R])
    I["hist_rw"] = din("hist_rw", [3, RW_COLS])
    I["hist_conv"] = din("hist_conv", [9, CONV_DIM])
    I["wkv0"] = din("wkv0", [3 * RW_H * 64, 64])
    I["ssm0"] = din("ssm0", [3 * SSM_H * 64, SSM_N])
    W = {n: din(n, shp) for n, shp in WEIGHT_SPECS}
    O = {}
    O["yout"] = dout("yout", [R, D])
    O["o_shift"] = dout("o_shift", [3, RW_COLS])
    O["o_conv"] = dout("o_conv", [9, CONV_DIM])
    O["o_wkv"] = dout("o_wkv", [3 * RW_H * 64, 64])
    O["o_ssm"] = dout("o_ssm", [3 * SSM_H * 64, SSM_N])
    S = {}
    S["X0T"] = dscr("X0T", [D, R], BF16)
    S["XPRE"] = dscr("XPRE", [R, D])
    S["X1"] = dscr("X1", [R, D])
    S["X1T"] = dscr("X1T", [D, R], BF16)
    S["X2"] = dscr("X2", [R, D])
    S["X2T"] = dscr("X2T", [D, R], BF16)
    S["P_RW"] = dscr("P_RW", [1 + R, RW_COLS])
    S["P_Z"] = dscr("P_Z", [R, D])
    S["P_DT"] = dscr("P_DT", [R, SSM_H])
    S["PT_XBC"] = dscr("PT_XBC", [CONV_DIM, 4 + R])
    S["YRWT"] = dscr("YRWT", [RW_DIM, R], BF16)
    S["YSSMT"] = dscr("YSSMT", [D, R], BF16)
    S["WDN_B"] = nc.dram_tensor("WDN_B", [DFF, D], BF16).ap()
    S["WOC_B"] = nc.dram_tensor("WOC_B", [8, 128, 56 * 256], BF16).ap()

    with ExitStack() as st:
        T = Tracker(nc, st)
        pb = [st.enter_context(nc.psum_tensor(f"pb{i}", [128, 512], F32)) for i in range(8)]
        T.excl = set(b.name for b in pb)

        def run(name, kind, fn, *a):
            T.rename = (name, kind)
            fn(*a)
            return cfg.stop_after == name

        phases = [
            ("ffn1", "ffn", phase_ffn, (nc, T, cfg, pb, I["xinT"], I["xin"], W["ffn1_gu"], W["ffn1_dn"], S["XPRE"], "ffn1", S["WDN_B"])),
            ("ln1", "ln", phase_ln, (nc, T, cfg, pb, S["XPRE"], S["X1"], S["X1T"], W["ln1_g"], W["ln1_b"], "ln1")),
            ("win", "win", phase_win, (nc, T, cfg, pb, S, W, O, "win")),
            ("mix", "mix", phase_mix, (nc, T, cfg, pb, S, W, I, O, "mix")),
            ("outp", "outp", phase_outp, (nc, T, cfg, pb, S, W, "outp")),
            ("ln2", "ln", phase_ln, (nc, T, cfg, pb, S["XPRE"], S["X2"], S["X2T"], W["ln2_g"], W["ln2_b"], "ln2")),
            ("ffn2", "ffn", phase_ffn, (nc, T, cfg, pb, S["X2T"], S["X2"], W["ffn2_gu"], W["ffn2_dn"], S["XPRE"], "ffn2", S["WDN_B"])),
            ("ln3", "ln", phase_ln, (nc, T, cfg, pb, S["XPRE"], O["yout"], None, W["ln3_g"], W["ln3_b"], "ln3")),
        ]
        for name, kind, fn, a in phases:
            if cfg.only is not None and name not in cfg.only:
                continue
            if run(name, kind, fn, *a):
                break
    return nc


def _pp(v, nchunk):
    return np.ascontiguousarray(np.asarray(v, np.float32).reshape(nchunk, 128).T)


def prep_weights(inp):
    f = lambda a: np.ascontiguousarray(np.asarray(a, np.float32))
    Wn = {}
    for n in ("ffn1_gu", "ffn1_dn", "w_in", "rw_w2", "rw_a2", "rw_g2", "w_rw_out", "w_ssm_out", "w_out", "ffn2_gu", "ffn2_dn"):
        Wn[n] = f(inp[n][0])
    for n in ("ln1_g", "ln1_b", "rw_mu", "rw_w0", "rw_a0", "dt_bias", "a_log", "d_skip", "ln2_g", "ln2_b", "ln3_g", "ln3_b"):
        Wn[n] = f(inp[n][0]).reshape(1, -1)
    for n in ("rw_kk", "rw_ka", "rw_rk", "rw_gn_w", "rw_gn_b"):
        Wn[n] = f(inp[n][0]).reshape(1, -1)
    Wn["b_gate"] = _pp(inp["b_gate"][0], 32)
    cw = np.asarray(inp["conv_w"][0], np.float32)
    Wn["conv_w"] = np.ascontiguousarray(cw.reshape(4, 24, 128).transpose(2, 1, 0).reshape(128, 96))
    Wn["conv_b"] = _pp(inp["conv_b"][0], 24)
    Wn["ssm_norm_w"] = _pp(inp["ssm_norm_w"][0], 16)
    return Wn


def core_inputs(inp, cfg, core, Wn):
    R = cfg.R
    b = core % 4
    sa, sb_ = 2 * core, 2 * core + 1
    xin = np.zeros((R, D), np.float32)
    xin[48:64] = inp["x_sample"][sa]
    xin[64 + 48:128] = inp["x_sample"][sb_]
    xin[128 + 48:192] = inp["meta_tokens"]
    nl = cfg.n_long * 64
    xin[192:192 + nl] = inp["x_prompt"][b][:nl]
    m = {"xin": xin, "xinT": np.ascontiguousarray(xin.T)}
    z = np.zeros
    m["hist_rw"] = np.ascontiguousarray(np.concatenate(
        [inp["state_rwkv_shift"][0, sa], inp["state_rwkv_shift"][0, sb_], z((1, RW_COLS), np.float32)], 0).astype(np.float32))
    m["hist_conv"] = np.ascontiguousarray(np.concatenate(
        [inp["state_conv"][0, sa], inp["state_conv"][0, sb_], z((3, CONV_DIM), np.float32)], 0).astype(np.float32))
    m["wkv0"] = np.ascontiguousarray(np.concatenate(
        [inp["state_wkv"][0, sa].reshape(-1, 64), inp["state_wkv"][0, sb_].reshape(-1, 64), z((RW_H * 64, 64), np.float32)], 0).astype(np.float32))
    m["ssm0"] = np.ascontiguousarray(np.concatenate(
        [inp["state_ssm"][0, sa].reshape(-1, SSM_N), inp["state_ssm"][0, sb_].reshape(-1, SSM_N), z((SSM_H * 64, SSM_N), np.float32)], 0).astype(np.float32))
    m.update(Wn)
    return m


def assemble(results, cfg):
    nl = cfg.n_long * 64
    f = np.float32
    y_prompt = np.zeros((4, nl, D), f)
    y_sample = np.zeros((16, 16, D), f)
    p_shift = np.zeros((1, 4, 1, RW_COLS), f)
    p_wkv = np.zeros((1, 4, RW_H, 64, 64), f)
    p_conv = np.zeros((1, 4, 3, CONV_DIM), f)
    p_ssm = np.zeros((1, 4, SSM_H, 64, SSM_N), f)
    s_shift = np.zeros((1, 16, 1, RW_COLS), f)
    s_wkv = np.zeros((1, 16, RW_H, 64, 64), f)
    s_conv = np.zeros((1, 16, 3, CONV_DIM), f)
    s_ssm = np.zeros((1, 16, SSM_H, 64, SSM_N), f)
    for c, r in enumerate(results):
        yo = np.asarray(r["yout"])
        osh = np.asarray(r["o_shift"])
        ocv = np.asarray(r["o_conv"])
        owk = np.asarray(r["o_wkv"]).reshape(3, RW_H, 64, 64)
        osm = np.asarray(r["o_ssm"]).reshape(3, SSM_H, 64, SSM_N)
        for i in range(2):
            s = 2 * c + i
            y_sample[s] = yo[i * 64 + 48:i * 64 + 64]
            s_shift[0, s, 0] = osh[i]
            s_conv[0, s] = ocv[i * 3:(i + 1) * 3]
            s_wkv[0, s] = owk[i]
            s_ssm[0, s] = osm[i]
        if c < 4:
            y_prompt[c] = yo[192:192 + nl]
            p_shift[0, c, 0] = osh[2]
            p_conv[0, c] = ocv[6:9]
            p_wkv[0, c] = owk[2]
            p_ssm[0, c] = osm[2]
    return (y_prompt, y_sample, p_shift, p_wkv, p_conv, p_ssm, s_shift, s_wkv, s_conv, s_ssm)


def kernel(**inputs):
    cfg = Cfg()
    inp = {k: np.asarray(v) for k, v in inputs.items()}
    nc = build_program(cfg)
    Wn = prep_weights(inp)
    maps = [core_inputs(inp, cfg, c, Wn) for c in range(8)]
    res = run_bass_kernel_spmd(nc, maps, core_ids=list(range(8)))
    return assemble(res.results, cfg)
```

```python
from contextlib import ExitStack
import numpy as np
import concourse.bass as bass
import concourse.mybir as mybir
from concourse.bass_utils import run_bass_kernel_spmd

F32 = mybir.dt.float32
BF16 = mybir.dt.bfloat16
AF = mybir.ActivationFunctionType
ALU = mybir.AluOpType
AX = mybir.AxisListType

D = 2048
DFF = 5632
NKC = D // 128
NFC = DFF // 128
RW_DIM = 1024
RW_H = 16
RW_COLS = 3520
SSM_H = 32
SSM_G = 4
SSM_N = 128
CONV_DIM = 3072
N_IN = 12768
ALPHA = 2.0 ** 0.25
LN_EPS = 1e-5
RW_GN_EPS = 64e-5
RMS_EPS = 1e-5
C = 64


class Tracker:
    ENGS = ("pe", "act", "dve", "pool", "sp")

    def __init__(self, nc, stack):
        self.nc = nc
        self.stack = stack
        self.sem = {e: stack.enter_context(nc.semaphore("s_" + e)) for e in self.ENGS}
        self.cnt = {e: 0 for e in self.ENGS}
        self.chan_sem = {}
        self.chan_cnt = {}
        self.seen = {}
        self.lastw = {}
        self.readers = {}
        self.prog = {e: [] for e in self.ENGS}
        self.nsem = len(self.ENGS)
        self.rename = None
        self.defer = None
        self._grp = None
        self.excl = set()

    def _split(self, reads, writes):
        if not self.excl:
            return list(reads), list(writes)
        r = [k for k in reads if k not in self.excl]
        w = list(writes) + [k for k in reads if k in self.excl]
        return r, w

    def _deps(self, reads, writes):
        deps = {}
        def add(d):
            if d is None:
                return
            s, v = d
            k = id(s)
            if k not in deps or deps[k][1] < v:
                deps[k] = (s, v)
        for k in reads:
            add(self.lastw.get(k))
        for k in writes:
            add(self.lastw.get(k))
            for d in self.readers.get(k, ()):
                add(d)
        return list(deps.values())

    def _emit_waits(self, eng, deps):
        for s, v in deps:
            key = (eng, id(s))
            if self.seen.get(key, 0) >= v:
                continue
            self.seen[key] = v
            self.prog[eng].append(lambda e, s=s, v=v: e.wait_ge(s, v))

    def _commit(self, dep, reads, writes):
        for k in reads:
            self.readers.setdefault(k, []).append(dep)
        for k in writes:
            self.lastw[k] = dep
            self.readers[k] = []

    def op(self, eng, fns, reads=(), writes=()):
        if self.defer is not None:
            self.defer.append(("op", (eng, fns, reads, writes), {}))
            return
        if not isinstance(fns, (list, tuple)):
            fns = [fns]
        reads, writes = self._split(reads, writes)
        self._emit_waits(eng, self._deps(reads, writes))
        self.cnt[eng] += 1
        n = self.cnt[eng]
        sem = self.sem[eng]
        for f in fns[:-1]:
            self.prog[eng].append(lambda e, f=f: f(e))
        last = fns[-1]
        self.prog[eng].append(lambda e, f=last, sem=sem: f(e).then_inc(sem, 1))
        self._commit((sem, n), reads, writes)

    def chan(self, name):
        if name not in self.chan_sem:
            self.chan_sem[name] = self.stack.enter_context(self.nc.semaphore("c_" + name))
            self.chan_cnt[name] = 0
            self.nsem += 1
        return self.chan_sem[name]

    def begin_group(self, chan):
        self._grp = (chan, [])

    def end_group(self):
        chan, keys = self._grp
        self._grp = None
        if chan in self.chan_sem:
            dep = (self.chan_sem[chan], self.chan_cnt[chan])
            for k in keys:
                self.lastw[k] = dep

    def dma(self, eng, chan, out, in_, reads=(), writes=(), **kw):
        if self.defer is not None:
            self.defer.append(("dma", (eng, chan, out, in_, reads, writes), kw))
            return
        if self._grp is not None:
            chan = self._grp[0]
            self._grp[1].extend(writes)
        if self.rename is not None:
            chan = chan.replace(self.rename[0], self.rename[1])
        sem = self.chan(chan)
        self._emit_waits(eng, self._deps(reads, writes))
        self.chan_cnt[chan] += 16
        n = self.chan_cnt[chan]
        self.prog[eng].append(lambda e, o=out, i=in_, sem=sem, kw=kw: e.dma_start(out=o, in_=i, **kw).then_inc(sem, 16))
        self._commit((sem, n), reads, writes)

    def capture(self, fn):
        assert self.defer is None
        self.defer = []
        try:
            fn()
        finally:
            lst, self.defer = self.defer, None
        return lst

    def replay(self, item):
        kind, a, kw = item
        if kind == "op":
            self.op(*a)
        else:
            self.dma(*a, **kw)

    def replay_merged(self, streams):
        pos = [0] * len(streams)
        tot = [max(len(x), 1) for x in streams]
        while True:
            best, bi = None, -1
            for i, x in enumerate(streams):
                if pos[i] < len(x):
                    f = pos[i] / tot[i]
                    if best is None or f < best:
                        best, bi = f, i
            if bi < 0:
                break
            self.replay(streams[bi][pos[bi]])
            pos[bi] += 1

    def drain_dmas(self, eng="sp"):
        for name, sem in self.chan_sem.items():
            v = self.chan_cnt[name]
            if v and self.seen.get((eng, id(sem)), 0) < v:
                self.seen[(eng, id(sem))] = v
                self.prog[eng].append(lambda e, s=sem, v=v: e.wait_ge(s, v))

    def flush(self):
        self.drain_dmas("sp")
        nc = self.nc
        prog = self.prog
        with nc.Block() as block:
            @block.tensor
            def _(e):
                for f in prog["pe"]:
                    f(e)

            @block.scalar
            def _(e):
                for f in prog["act"]:
                    f(e)

            @block.vector
            def _(e):
                for f in prog["dve"]:
                    f(e)

            @block.gpsimd
            def _(e):
                for f in prog["pool"]:
                    f(e)

            @block.sync
            def _(e):
                for f in prog["sp"]:
                    f(e)
        self.prog = {e: [] for e in self.ENGS}
        self.lastw = {}
        self.readers = {}


class Cfg:
    def __init__(self, n_long=32, tile_blocks=6, debug=False, stop_after=None, only=None, scratch_in=()):
        self.only = only
        self.mix_stop = None
        self.scratch_in = tuple(scratch_in)
        self.n_long = n_long
        self.nchunks = 3 + n_long
        self.nblocks = (self.nchunks + 1) // 2
        assert self.nblocks % tile_blocks == 0
        self.tile_blocks = tile_blocks
        self.ntiles = self.nblocks // tile_blocks
        self.R = self.nblocks * 128
        self.debug = debug
        self.stop_after = stop_after
        self.seq_end = [64, 128, self.nchunks * 64]


def _sbuf(nc, st, prefix):
    def sb(n, shape, dt=F32):
        return st.enter_context(nc.sbuf_tensor(f"{prefix}_{n}", shape, dt))
    return sb


def make_ident(T, ident, n=128):
    T.op("pool", lambda e: e.memset(ident[:], 0.0), writes=[ident.name])
    T.op("pool", lambda e: e.affine_select(out=ident[:], in_=ident[:], pattern=[[-1, n]], compare_op=ALU.not_equal,
                                           fill=1.0, base=0, channel_multiplier=1),
         reads=[ident.name], writes=[ident.name])


def phase_ln(nc, T, cfg, pb, src, dst_x, dst_xT, g_d, b_d, name):
    nb = cfg.nblocks
    GB = 3
    norm = g_d is not None
    with ExitStack() as st:
        sb = _sbuf(nc, st, name)
        NBUF = 3
        xin = [sb(f"xin{i}", [128, D]) for i in range(NBUF)]
        if norm:
            gbc = sb("gbc", [128, D])
            bbc = sb("bbc", [128, D])
            T.dma("sp", gbc.name, gbc[:], g_d.partition_broadcast(128), writes=[gbc.name])
            T.dma("sp", bbc.name, bbc[:], b_d.partition_broadcast(128), writes=[bbc.name])
            yv = [sb(f"y{i}", [128, D]) for i in range(NBUF)]
            stats = [sb(f"stats{i}", [128, 4, 6]) for i in range(NBUF)]
            mv = [sb(f"mv{i}", [128, 2]) for i in range(NBUF)]
            rstd = [sb(f"rstd{i}", [128, 1]) for i in range(NBUF)]
        if dst_xT is not None:
            ident = sb("ident", [128, 128])
            make_ident(T, ident)
            xTb = [sb(f"xTb{i}", [128, NKC, GB * 128], BF16) for i in range(2)]
            xTv = dst_xT.rearrange("(kc p) r -> p kc r", p=128)
        def head(b):
            s = b % NBUF
            x = xin[s]
            T.dma("sp", x.name, x[:], src[b * 128:(b + 1) * 128, :], writes=[x.name])
            y = x
            if norm:
                y = yv[s]
                for i in range(4):
                    T.op("dve", lambda e, i=i, x=x, s=s: e.bn_stats(out=stats[s][:, i, :], in_=x[:, i * 512:(i + 1) * 512]),
                         reads=[x.name], writes=[(stats[s].name, i)])
                T.op("dve", lambda e, s=s: e.bn_aggr(out=mv[s][:], in_=stats[s][:]),
                     reads=[(stats[s].name, i) for i in range(4)], writes=[mv[s].name])
                T.op("act", lambda e, s=s: e.activation(out=rstd[s][:], in_=mv[s][:, 1:2], func=AF.Sqrt, bias=LN_EPS),
                     reads=[mv[s].name], writes=[rstd[s].name])
                T.op("dve", lambda e, s=s: e.reciprocal(out=rstd[s][:], in_=rstd[s][:]),
                     reads=[rstd[s].name], writes=[rstd[s].name])
                T.op("dve", lambda e, s=s, x=x, y=y: e.tensor_scalar(out=y[:], in0=x[:], scalar1=mv[s][:, 0:1], scalar2=rstd[s][:, 0:1],
                                                                     op0=ALU.subtract, op1=ALU.mult),
                     reads=[x.name, mv[s].name, rstd[s].name], writes=[y.name])
                T.op("dve", lambda e, y=y: e.tensor_tensor(out=y[:], in0=y[:], in1=gbc[:], op=ALU.mult),
                     reads=[y.name, gbc.name], writes=[y.name])
                T.op("pool", lambda e, y=y: e.tensor_tensor(out=y[:], in0=y[:], in1=bbc[:], op=ALU.add),
                     reads=[y.name, bbc.name], writes=[y.name])
            return None

        def tail(b):
            s = b % NBUF
            y = yv[s] if norm else xin[s]
            if dst_x is not None:
                T.dma("act", y.name + "_st", dst_x[b * 128:(b + 1) * 128, :], y[:], reads=[y.name])
            if dst_xT is not None:
                g, gi = divmod(b, GB)
                gs = g % 2
                for q in range(4):
                    bank = pb[(b % 2) * 4 + q]
                    T.op("pe", [lambda e, kc=4 * q + j, j=j, bank=bank, y=y: e.transpose(
                        out=bank[:, j * 128:(j + 1) * 128], in_=y[:, kc * 128:(kc + 1) * 128], identity=ident[:]) for j in range(4)],
                        reads=[y.name, ident.name], writes=[bank.name])
                    eng = "act" if q % 2 == 0 else "dve"
                    outap = xTb[gs][:, 4 * q:4 * q + 4, gi * 128:(gi + 1) * 128]
                    inap = bank[:, :].rearrange("p (j t) -> p j t", j=4)
                    if eng == "act":
                        T.op("act", lambda e, o=outap, i=inap: e.copy(out=o, in_=i), reads=[bank.name], writes=[(xTb[gs].name, gi, q)])
                    else:
                        T.op("dve", lambda e, o=outap, i=inap: e.tensor_copy(out=o, in_=i), reads=[bank.name], writes=[(xTb[gs].name, gi, q)])
                if gi == GB - 1 or b == nb - 1:
                    nbk = gi + 1
                    r0 = g * GB * 128
                    keys = [(xTb[gs].name, a, q) for a in range(nbk) for q in range(4)]
                    T.dma("act", xTb[gs].name + "_st", xTv[:, :, r0:r0 + nbk * 128], xTb[gs][:, :, 0:nbk * 128], reads=keys)

        for b in range(nb):
            head(b)
            if b > 0:
                tail(b - 1)
        tail(nb - 1)
        T.flush()


def phase_ffn(nc, T, cfg, pb, xT_d, xres_d, wgu_d, wdn_d, dst_pre, name, wdn_cache=None):
    TB = cfg.tile_blocks
    TT = TB * 128
    npc = (TT + 511) // 512
    pw = TT // npc
    with ExitStack() as st:
        sb = _sbuf(nc, st, name)
        xT = sb("xT", [128, NKC, TT], BF16)
        hT = sb("hT", [128, NFC, TT], BF16)
        wgu = [[sb(f"wgu{i}{p}", [128, NKC, 512], BF16) for p in "gu"] for i in range(2)]
        wdn = [sb(f"wdn{i}", [128, 4, 512], BF16) for i in range(3)]
        sg = [sb(f"sg{i}", [128, pw]) for i in range(4)]
        xr = sb("xr", [128, TB, 512])
        so = sb("so", [128, TB, 512])
        xTv = xT_d.rearrange("(kc p) r -> p kc r", p=128)
        wguv = wgu_d.rearrange("(kc p) n -> p kc n", p=128)
        q = 0
        wq = 0
        xq = 0
        for t in range(cfg.ntiles):
            r0 = t * TT
            T.dma("pool" if xT_d.dtype == F32 else "sp", xT.name, xT[:], xTv[:, :, r0:r0 + TT], writes=[xT.name])
            for jg in range(NFC // 4):
                s = jg % 2
                for gu in range(2):
                    c0 = gu * DFF + jg * 512
                    w = wgu[s][gu]
                    T.dma("pool", w.name, w[:], wguv[:, :, c0:c0 + 512], writes=[w.name])
                for jj in range(4):
                    j = jg * 4 + jj
                    for pc in range(npc):
                        bg = pb[(2 * q) % 8]
                        bu = pb[(2 * q + 1) % 8]
                        sgt = sg[q % 4]
                        q += 1
                        for bank, w in ((bg, wgu[s][0]), (bu, wgu[s][1])):
                            T.op("pe", [lambda e, kc=kc, bank=bank, w=w, jj=jj, pc=pc: e.matmul(
                                bank[:, 0:pw], lhsT=w[:, kc, jj * 128:(jj + 1) * 128], rhs=xT[:, kc, pc * pw:(pc + 1) * pw],
                                start=(kc == 0), stop=(kc == NKC - 1)) for kc in range(NKC)],
                                reads=[w.name, xT.name], writes=[bank.name])
                        T.op("act", lambda e, bg=bg, sgt=sgt: e.activation(out=sgt[:], in_=bg[:, 0:pw], func=AF.Silu),
                             reads=[bg.name], writes=[sgt.name])
                        T.op("dve", lambda e, bu=bu, sgt=sgt, j=j, pc=pc: e.tensor_tensor(
                            out=hT[:, j, pc * pw:(pc + 1) * pw], in0=sgt[:], in1=bu[:, 0:pw], op=ALU.mult),
                            reads=[sgt.name, bu.name], writes=[(hT.name, j, pc)])
            for cb in range(4):
                T.dma("sp", xr.name, xr[:], xres_d[r0:r0 + TT, cb * 512:(cb + 1) * 512].rearrange("(b p) n -> p b n", p=128), writes=[xr.name])
                T.op("act", lambda e: e.mul(out=xr[:], in_=xr[:], mul=ALPHA), reads=[xr.name], writes=[xr.name])
                for j4 in range(NFC // 4):
                    w = wdn[wq % 3]
                    wq += 1
                    jr = slice(j4 * 512, (j4 + 1) * 512)
                    cs_ = slice(cb * 512, (cb + 1) * 512)
                    if wdn_cache is not None and t > 0:
                        T.dma("pool", w.name, w[:], wdn_cache[jr, cs_].rearrange("(a p) n -> p a n", p=128),
                              reads=[("wdnc", cb, j4)], writes=[w.name])
                    else:
                        T.dma("pool", w.name, w[:], wdn_d[jr, cs_].rearrange("(a p) n -> p a n", p=128), writes=[w.name])
                        if wdn_cache is not None and cfg.ntiles > 1:
                            T.dma("sp", w.name + "_wb", wdn_cache[jr, cs_].rearrange("(a p) n -> p a n", p=128), w[:],
                                  reads=[w.name], writes=[("wdnc", cb, j4)])
                    first, last = (j4 == 0), (j4 == NFC // 4 - 1)
                    wr = [pb[b].name for b in range(TB)] if (first or last) else []
                    T.op("pe", [lambda e, b=b, j=j4 * 4 + jj, jj=jj, w=w: e.matmul(
                        pb[b][:, :], lhsT=hT[:, j, b * 128:(b + 1) * 128], rhs=w[:, jj, :], start=(j == 0), stop=(j == NFC - 1))
                        for jj in range(4) for b in range(TB)],
                        reads=[w.name] + [(hT.name, j4 * 4 + jj, pc) for jj in range(4) for pc in range(npc)], writes=wr)
                for b in range(TB):
                    T.op("dve", lambda e, b=b: e.scalar_tensor_tensor(
                        out=so[:, b, :], in0=pb[b][:, :], scalar=0.5, in1=xr[:, b, :], op0=ALU.mult, op1=ALU.add),
                        reads=[pb[b].name, xr.name], writes=[(so.name, b)])
                T.dma("sp", so.name, dst_pre[r0:r0 + TT, cb * 512:(cb + 1) * 512].rearrange("(b p) n -> p b n", p=128), so[:],
                      reads=[(so.name, b) for b in range(TB)])
        T.flush()


def phase_win(nc, T, cfg, pb, S, W, O, name):
    R = cfg.R
    nb = cfg.nblocks
    with ExitStack() as st:
        sb = _sbuf(nc, st, name)
        x1T = sb("x1T", [128, NKC, R], BF16)
        wsl = [sb(f"w{i}", [128, NKC, 512], BF16) for i in range(2)]
        stg = [sb(f"stg{i}", [128, 512]) for i in range(4)]
        fstg = [sb(f"fstg{i}", [128, R]) for i in range(2)]
        tl = [sb(f"tl{i}", [3, 512]) for i in range(2)]
        zt = sb("zt", [128, RW_COLS])
        x1Tv = S["X1T"].rearrange("(kc p) r -> p kc r", p=128)
        wv = W["w_in"].rearrange("(kc p) n -> p kc n", p=128)
        TT = cfg.tile_blocks * 128
        for t in range(cfg.ntiles):
            T.dma("sp", f"{name}_x1T{t % 2}", x1T[:, :, t * TT:(t + 1) * TT], x1Tv[:, :, t * TT:(t + 1) * TT], writes=[(x1T.name, t)])
        x1keys = [(x1T.name, t) for t in range(cfg.ntiles)]
        T.op("pool", lambda e: e.memset(zt[:], 0.0), writes=[zt.name])
        T.dma("sp", f"{name}_z0", S["P_RW"][0:1, :], zt[0:1, :], reads=[zt.name])
        T.dma("sp", f"{name}_z1", S["PT_XBC"].rearrange("(c p) r -> p c r", p=128)[:, :, 0:4],
              zt[:, 0:96].rearrange("p (c r) -> p c r", r=4), reads=[zt.name])
        groups = []
        for c0 in range(0, RW_COLS, 512):
            groups.append((c0, min(512, RW_COLS - c0), S["P_RW"], c0, 1))
        for c0 in range(0, D, 512):
            groups.append((RW_COLS + c0, 512, S["P_Z"], c0, 0))
        groups.append((8640, SSM_H, S["P_DT"], 0, 0))
        q = 0
        wq = 0
        sq = 0
        for (c0, n, dst, dc0, roff) in groups:
            w = wsl[wq % 2]
            wq += 1
            T.dma("pool", w.name, w[:, :, 0:n], wv[:, :, c0:c0 + n], writes=[w.name])
            for b in range(nb):
                bank = pb[q % 8]
                q += 1
                T.op("pe", [lambda e, kc=kc, bank=bank, w=w, b=b, n=n: e.matmul(
                    bank[:, 0:n], lhsT=x1T[:, kc, b * 128:(b + 1) * 128], rhs=w[:, kc, 0:n],
                    start=(kc == 0), stop=(kc == NKC - 1)) for kc in range(NKC)],
                    reads=[w.name] + x1keys, writes=[bank.name])
                sg = stg[sq % 4]
                if sq % 2 == 0:
                    T.op("act", lambda e, sg=sg, bank=bank, n=n: e.copy(out=sg[:, 0:n], in_=bank[:, 0:n]), reads=[bank.name], writes=[sg.name])
                else:
                    T.op("dve", lambda e, sg=sg, bank=bank, n=n: e.tensor_copy(out=sg[:, 0:n], in_=bank[:, 0:n]), reads=[bank.name], writes=[sg.name])
                sq += 1
                T.dma("sp", sg.name, dst[roff + b * 128:roff + (b + 1) * 128, dc0:dc0 + n], sg[:, 0:n], reads=[sg.name])
        PW = 384
        npc = R // PW
        fq = 0
        tq = 0
        for cg in range(6):
            c0 = 5568 + cg * 512
            w = wsl[wq % 2]
            wq += 1
            T.dma("pool", w.name, w[:], wv[:, :, c0:c0 + 512], writes=[w.name])
            for cc in range(4):
                fs = fstg[fq % 2]
                fq += 1
                for pc in range(npc):
                    bank = pb[q % 8]
                    q += 1
                    T.op("pe", [lambda e, kc=kc, bank=bank, w=w, cc=cc, pc=pc: e.matmul(
                        bank[:, 0:PW], lhsT=w[:, kc, cc * 128:(cc + 1) * 128], rhs=x1T[:, kc, pc * PW:(pc + 1) * PW],
                        start=(kc == 0), stop=(kc == NKC - 1)) for kc in range(NKC)],
                        reads=[w.name] + x1keys, writes=[bank.name])
                    if sq % 2 == 0:
                        T.op("act", lambda e, fs=fs, bank=bank, pc=pc: e.copy(out=fs[:, pc * PW:(pc + 1) * PW], in_=bank[:, 0:PW]),
                             reads=[bank.name], writes=[(fs.name, pc)])
                    else:
                        T.op("dve", lambda e, fs=fs, bank=bank, pc=pc: e.tensor_copy(out=fs[:, pc * PW:(pc + 1) * PW], in_=bank[:, 0:PW]),
                             reads=[bank.name], writes=[(fs.name, pc)])
                    sq += 1
                ch = cg * 4 + cc
                T.dma("sp", fs.name, S["PT_XBC"][ch * 128:(ch + 1) * 128, 4:4 + R], fs[:], reads=[(fs.name, pc) for pc in range(npc)])
            for si, rend in enumerate(cfg.seq_end):
                bank = pb[q % 8]
                q += 1
                T.op("pe", [lambda e, kc=kc, bank=bank, w=w, rend=rend: e.matmul(
                    bank[0:3, :], lhsT=x1T[:, kc, rend - 3:rend], rhs=w[:, kc, :],
                    start=(kc == 0), stop=(kc == NKC - 1)) for kc in range(NKC)],
                    reads=[w.name] + x1keys, writes=[bank.name])
                tt = tl[tq % 2]
                tq += 1
                T.op("dve", lambda e, tt=tt, bank=bank: e.tensor_copy(out=tt[:], in_=bank[0:3, :]), reads=[bank.name], writes=[tt.name])
                T.dma("sp", tt.name, O["o_conv"][si * 3:(si + 1) * 3, cg * 512:(cg + 1) * 512], tt[:], reads=[tt.name])
        T.flush()


class _Stop(Exception):
    pass


def phase_mix(nc, T, cfg, pb, S, W, I, O, name):
    try:
        _phase_mix(nc, T, cfg, pb, S, W, I, O, name)
    except _Stop:
        pass


def _phase_mix(nc, T, cfg, pb, S, W, I, O, name):
    R = cfg.R

    def chk(n):
        if cfg.mix_stop == n:
            T.flush()
            raise _Stop()
    NH = 4
    HW = NH * 64
    NSTR = 2
    with ExitStack() as st:
        sb = _sbuf(nc, st, name)
        bq = [0]
        bpool = [list(range(8))]

        def bank():
            p = bpool[0]
            b = pb[p[bq[0] % len(p)]]
            bq[0] += 1
            return b

        kpref = [""]
        LOCALK = set(["ew", "av", "kkn", "kp", "Ep", "Em", "Wp", "rt", "kt", "bt", "at", "gg", "tA", "tB", "Us", "Ys", "yo", "vb", "btb", "ktb",
                      "n2", "rn", "s1", "s2", "mean", "var", "bs"])

        class _TP:
            @staticmethod
            def _m(keys):
                return [kpref[0] + k if (isinstance(k, str) and k in LOCALK) else k for k in keys]

            @staticmethod
            def op(eng, fns, reads=(), writes=()):
                T.op(eng, fns, reads=_TP._m(reads), writes=_TP._m(writes))
        TP = _TP

        def TT(eng, out, in0, in1, op, r, w):
            TP.op(eng, lambda e: e.tensor_tensor(out=out, in0=in0, in1=in1, op=op), reads=r, writes=w)

        def TS(eng, out, in0, s1, s2, op0, op1, r, w):
            if s2 is None:
                TP.op(eng, lambda e: e.tensor_scalar(out=out, in0=in0, scalar1=s1, scalar2=None, op0=op0), reads=r, writes=w)
            else:
                TP.op(eng, lambda e: e.tensor_scalar(out=out, in0=in0, scalar1=s1, scalar2=s2, op0=op0, op1=op1), reads=r, writes=w)

        def STT(out, in0, sc, in1, op0, op1, r, w):
            TP.op("dve", lambda e: e.scalar_tensor_tensor(out=out, in0=in0, scalar=sc, in1=in1, op0=op0, op1=op1), reads=r, writes=w)

        def ACT(out, in_, func, r, w, bias=0.0, scale=1.0):
            TP.op("act", lambda e: e.activation(out=out, in_=in_, func=func, bias=bias, scale=scale), reads=r, writes=w)

        def CP(eng, out, in_, r, w):
            if eng == "act":
                TP.op("act", lambda e: e.copy(out=out, in_=in_), reads=r, writes=w)
            else:
                TP.op(eng, lambda e: e.tensor_copy(out=out, in_=in_), reads=r, writes=w)

        def RED(out, in_, r, w):
            TP.op("dve", lambda e: e.tensor_reduce(out=out, in_=in_, axis=AX.X, op=ALU.add), reads=r, writes=w)

        def RECIP(out, in_, r, w):
            TP.op("dve", lambda e: e.reciprocal(out=out, in_=in_), reads=r, writes=w)

        def MM(specs, r, w):
            TP.op("pe", [lambda e, sp=sp: e.matmul(sp[0], lhsT=sp[1], rhs=sp[2], start=sp[3], stop=sp[4]) for sp in specs], reads=r, writes=w)

        def TR(specs, r, w):
            TP.op("pe", [lambda e, sp=sp: e.transpose(out=sp[0], in_=sp[1], identity=sp[2]) for sp in specs], reads=r, writes=w)

        def LD(t, out, in_, w, r=()):
            T.dma("sp", t, out, in_, reads=r, writes=w)

        ident = sb("ident", [128, 128])
        make_ident(T, ident)
        tri = sb("tri", [64, 64])
        msl = sb("msl", [64, 2, 64])
        mgt = sb("mgt", [64, 64])
        ones = sb("ones", [64, 128])
        rowmask = sb("rowmask", [64, 1])

        def sel(t, ap, pattern, base, cm):
            T.op("pool", lambda e: e.memset(ap, 1.0), writes=[t.name])
            T.op("pool", lambda e: e.affine_select(out=ap, in_=ap, pattern=pattern, compare_op=ALU.is_ge, fill=0.0,
                                                   base=base, channel_multiplier=cm), reads=[t.name], writes=[t.name])
        sel(tri, tri[:], [[1, 64]], 0, -1)
        sel(msl, msl[:, 0, :], [[1, 64]], -1, -1)
        sel(msl, msl[:, 1, :], [[1, 64]], 0, -1)
        sel(mgt, mgt[:], [[-1, 64]], -1, 1)
        sel(rowmask, rowmask[:], [[0, 1]], -48, 1)
        T.op("pool", lambda e: e.memset(ones[:], 1.0), writes=[ones.name])

        T.begin_group("mix_setup")

        def bc_load(n, src, cols):
            t = sb(n, [64, cols])
            LD(t.name, t[:], src.partition_broadcast(64), [t.name])
            return t
        mu_bc = bc_load("mu_bc", W["rw_mu"], RW_COLS)
        w0_bc = bc_load("w0_bc", W["rw_w0"], RW_DIM)
        a0_bc = bc_load("a0_bc", W["rw_a0"], RW_DIM)
        kk_bc = bc_load("kk_bc", W["rw_kk"], RW_DIM)
        ka_bc = bc_load("ka_bc", W["rw_ka"], RW_DIM)
        rk_bc = bc_load("rk_bc", W["rw_rk"], RW_DIM)
        dtb_bc = bc_load("dtb_bc", W["dt_bias"], SSM_H)
        aneg_bc = bc_load("aneg_bc", W["a_log"], SSM_H)
        dsk_bc = bc_load("dsk_bc", W["d_skip"], SSM_H)
        gnw_bc = bc_load("gnw_bc", W["rw_gn_w"], RW_DIM)
        gnb_bc = bc_load("gnb_bc", W["rw_gn_b"], RW_DIM)
        w2 = sb("w2", [128, RW_DIM])
        a2 = sb("a2", [128, RW_DIM], BF16)
        g2 = sb("g2", [128, 2, RW_DIM], BF16)
        T.op("pool", lambda e: e.memset(w2[:], 0.0), writes=[w2.name])
        T.op("pool", lambda e: e.memset(a2[:], 0.0), writes=[a2.name])
        LD(w2.name, w2[0:96, :], W["rw_w2"][:, :], [w2.name])
        T.dma("pool", a2.name, a2[0:96, :], W["rw_a2"][:, :], writes=[a2.name])
        T.dma("pool", g2.name, g2[:], W["rw_g2"].rearrange("(c p) n -> p c n", p=128), writes=[g2.name])
        cw = sb("cw", [128, 24, 4])
        cb = sb("cb", [128, 24])
        snw = sb("snw", [128, 16])
        LD(cw.name, cw[:], W["conv_w"].rearrange("p (c i) -> p c i", i=4), [cw.name])
        LD(cb.name, cb[:], W["conv_b"][:, :], [cb.name])
        LD(snw.name, snw[:], W["ssm_norm_w"][:, :], [snw.name])
        T.end_group()
        ACT(aneg_bc[:], aneg_bc[:], AF.Exp, [aneg_bc.name], [aneg_bc.name])
        T.op("act", lambda e: e.mul(out=aneg_bc[:], in_=aneg_bc[:], mul=-1.0), reads=[aneg_bc.name], writes=[aneg_bc.name])
        XAB = sb("XAB", [128, 2, 24, 64])
        XA = XAB[:, 0]
        XB2 = XAB[:, 1]
        XAk, XBk = "XA", "XB2"
        histT = sb("histT", [128, 24, 9])
        hrow = XAB[0:9, 0].rearrange("p c t -> p (c t)")
        for half in range(2):
            LD("mix_hrow", hrow, I["hist_conv"][:, half * 1536:(half + 1) * 1536], [XAk])
            for q4 in range(3):
                bk = bank()
                TR([(bk[:, j * 9:(j + 1) * 9], hrow[:, (q4 * 4 + j) * 128:(q4 * 4 + j + 1) * 128], ident[0:9, 0:9]) for j in range(4)],
                   [XAk, ident.name], [bk.name])
                c0_ = half * 12 + q4 * 4
                CP("dve", histT[:, c0_:c0_ + 4, :], bk[:, 0:36].rearrange("p (j s) -> p j s", j=4), [bk.name], [histT.name])

        chk(1)
        ST = sb("ST", [64, RW_H, 64])
        STb = sb("STb", [64, RW_H, 64], BF16)
        HT = sb("HT", [128, SSM_H * 64])
        stg_ws = [sb(f"stg_w{i}", [64, 8, 64]) for i in range(NSTR)]
        stg_h = XAB[:].rearrange("p a c t -> p (a c t)")[:, 0:2048].rearrange("p (c n) -> p c n", c=16)
        SHk = [XAk, XBk]

        def load_states_rw(seq, sid):
            stg_w = stg_ws[sid]
            r0_ = seq * 1024 + sid * 512
            LD(stg_w.name, stg_w[:], I["wkv0"][r0_:r0_ + 512, :].rearrange("(h v) k -> v h k", v=64), [stg_w.name])
            bk = bank()
            TR([(bk[0:64, j * 64:(j + 1) * 64], stg_w[:, j, :], ident[0:64, 0:64]) for j in range(8)],
               [stg_w.name, ident.name], [bk.name])
            CP("act", ST[:, sid * 8:(sid + 1) * 8, :], bk[0:64, :].rearrange("p (h v) -> p h v", h=8), [bk.name],
               [(ST.name, 2 * sid), (ST.name, 2 * sid + 1)])
            CP("dve", STb[:, sid * 8:(sid + 1) * 8, :], bk[0:64, :].rearrange("p (h v) -> p h v", h=8), [bk.name],
               [(STb.name, 2 * sid), (STb.name, 2 * sid + 1)])

        def load_states_ssd(seq):
            LD("mix_stg_h", stg_h, I["ssm0"][seq * 2048:(seq + 1) * 2048, :].rearrange("(c p) n -> p c n", p=128), SHk)
            for g in range(4):
                bk = bank()
                TR([(bk[:, j * 128:(j + 1) * 128], stg_h[:, g * 4 + j, :], ident[:]) for j in range(4)],
                   SHk + [ident.name], [bk.name])
                CP("dve", HT[:, g * 512:(g + 1) * 512], bk[:, :], [bk.name], [(HT.name, g)])

        def store_states_rw(seq, sid):
            stg_w = stg_ws[sid]
            r0_ = seq * 1024 + sid * 512
            bk = bank()
            TR([(bk[0:64, j * 64:(j + 1) * 64], ST[:, sid * 8 + j, :], ident[0:64, 0:64]) for j in range(8)],
               [(ST.name, 2 * sid), (ST.name, 2 * sid + 1), ident.name], [bk.name])
            CP("act", stg_w[:], bk[0:64, :].rearrange("p (h k) -> p h k", h=8), [bk.name], [stg_w.name])
            T.dma("sp", stg_w.name + "_st", O["o_wkv"][r0_:r0_ + 512, :].rearrange("(h v) k -> v h k", v=64), stg_w[:],
                  reads=[stg_w.name])

        def store_states_ssd(seq):
            for g in range(4):
                bk = bank()
                TR([(bk[:, j * 128:(j + 1) * 128], HT[:, (g * 4 + j) * 128:(g * 4 + j + 1) * 128], ident[:]) for j in range(4)],
                   [(HT.name, g), ident.name], [bk.name])
                CP("dve", stg_h[:, g * 4:(g + 1) * 4, :], bk[:, :].rearrange("p (j n) -> p j n", j=4), [bk.name], SHk)
            T.dma("sp", "mix_stg_h_st", O["o_ssm"][seq * 2048:(seq + 1) * 2048, :].rearrange("(c p) n -> p c n", p=128), stg_h,
                  reads=SHk)

        for si, rend in enumerate(cfg.seq_end):
            T.dma("sp", f"{name}_shift", O["o_shift"][si:si + 1, :], S["P_RW"][rend:rend + 1, :])

        identb = sb("identb", [64, 64], BF16)
        CP("pool", identb[:], ident[0:64, 0:64], [ident.name], [identb.name])
        names = ["ew", "av", "kkn", "kp", "Ep", "Em", "Wp", "rt", "kt", "bt", "at", "gg", "tA", "tB", "Us", "Ys", "yo"]

        def mk_rw(i):
            p = f"r{i}_"
            return dict(
                cur_l=sb(p + "cur_l", [64, 448]), prev_l=sb(p + "prev_l", [64, 448]), lT=sb(p + "lT", [128, 64]),
                lTb=sb(p + "lTb", [128, 3, 64], BF16),
                cur3=[sb(p + f"cur3{j}", [64, 3, HW]) for j in range(2)], prev3=[sb(p + f"prev3{j}", [64, 3, HW]) for j in range(2)],
                W_={n: sb(p + n, [64, HW], BF16 if n in ("Us", "vb", "btb", "ktb") else F32) for n in names + ["vb", "btb", "ktb"]},
                ARTb=sb(p + "ARTb", [64, NH, 2, 64], BF16),
                btT=sb(p + "btT", [64, NH, 64], BF16), ktT=sb(p + "ktT", [64, NH, 64], BF16),
                AB=sb(p + "AB", [64, NH, 2, 64], BF16), AK=sb(p + "AK", [64, NH, 2, 64], BF16),
                Pm=[sb(p + f"Pm{j}", [64, NH, 64], BF16) for j in range(2)],
                Qm=[sb(p + f"Qm{j}", [64, NH, 64], BF16) for j in range(2)],
                XT=sb(p + "XT", [64, NH, 64], BF16), RHSb=sb(p + "RHSb", [64, HW], BF16), WC=sb(p + "WC", [64, NH]),
                sm={n: sb(p + n, [64, NH]) for n in ["n2", "rn", "s1", "s2", "mean", "var", "bs"]},
                yT=[sb(p + f"yT{j}", [128, HW // 128, 64], BF16) for j in range(2)])
        RWB = [mk_rw(i) for i in range(NSTR)]
        Fin = sb("Fin", [128, 24, 67])
        Btm = sb("Btm", [64, 512], BF16)
        pdt = sb("pdt", [64, SSM_H])
        dts = {n: sb(n, [64, SSM_H]) for n in ["dt", "adt", "acs", "eacs", "toend"]}
        cdec = sb("cdec", [128, SSM_H])
        gn = ["xs", "zz", "yy", "xdt", "xdtw", "ML", "EX", "MT", "t1"]
        G_ = {n: sb("g_" + n, [64, 512], BF16 if n in ("xdt", "xdtw", "MT") else F32) for n in gn}
        cbm = sb("cbm", [64, 64])
        ss1 = sb("ss1", [64, 1])
        yT2s = [sb(f"yT2{j}", [128, 4, 64], BF16) for j in range(2)]
        ZZ = [G_["zz"], sb("g_zz1", [64, 512])]
        prw3 = S["P_RW"][:, 0:3072].rearrange("t (j c) -> t j c", j=3)
        hrw3 = I["hist_rw"][:, 0:3072].rearrange("t (j c) -> t j c", j=3)
        yrwT = S["YRWT"].rearrange("(c p) r -> p c r", p=128)
        yssT = S["YSSMT"].rearrange("(c p) r -> p c r", p=128)
        xbcv = S["PT_XBC"].rearrange("(c p) r -> p c r", p=128)

        def h3(ap):
            return ap.rearrange("p (h k) -> p h k", h=NH)

        def h3s(ap):
            return ap.rearrange("p (h k) -> p h k", h=8)

        def bc8s(ap):
            return ap.unsqueeze(2).to_broadcast([64, 8, 64])

        def bc8(ap):
            return ap.unsqueeze(2).to_broadcast([64, NH, 64])

        def rwkv_chunk(c, sid):
            Bf = RWB[sid]
            cur_l, prev_l, lT, W_ = Bf["cur_l"], Bf["prev_l"], Bf["lT"], Bf["W_"]
            lTb = Bf["lTb"]
            btT, ktT, AB, AK, Pm, Qm = Bf["btT"], Bf["ktT"], Bf["AB"], Bf["AK"], Bf["Pm"], Bf["Qm"]
            ARTb = Bf["ARTb"]
            XT, RHSb, WC, sm = Bf["XT"], Bf["RHSb"], Bf["WC"], Bf["sm"]
            t0 = c * 64
            seq = min(c, 2)
            short = c < 3
            if c < 3:
                load_states_rw(seq, sid)
            LD(cur_l.name, cur_l[:], S["P_RW"][1 + t0:1 + t0 + 64, 3072:3520], [cur_l.name])
            LD(prev_l.name, prev_l[:], S["P_RW"][t0:t0 + 64, 3072:3520], [prev_l.name])
            if short:
                LD(prev_l.name, prev_l[48:49, :], I["hist_rw"][seq:seq + 1, 3072:3520], [prev_l.name])
            TT("dve", prev_l[:], prev_l[:], cur_l[:], ALU.subtract, [prev_l.name, cur_l.name], [prev_l.name])
            TT("pool", prev_l[:], prev_l[:], mu_bc[:, 3072:3520], ALU.mult, [prev_l.name, mu_bc.name], [prev_l.name])
            TT("dve", cur_l[:], cur_l[:], prev_l[:], ALU.add, [prev_l.name, cur_l.name], [cur_l.name])
            ACT(cur_l[:, 0:96], cur_l[:, 0:96], AF.Tanh, [cur_l.name], [cur_l.name])
            ACT(cur_l[:, 192:448], cur_l[:, 192:448], AF.Sigmoid, [cur_l.name], [cur_l.name])
            bk = bank()
            TR([(bk[:, j * 64:(j + 1) * 64], cur_l[:, o_:o_ + 128], ident[0:64, 0:64]) for j, o_ in enumerate((0, 96, 192, 320))],
               [cur_l.name, ident.name], [bk.name])
            CP("act", lT[:], bk[:, 0:64], [bk.name], [lT.name])
            CP("dve", lTb[:], bk[:, 64:256].rearrange("p (a t) -> p a t", a=3), [bk.name], [lTb.name])
            for qq in range(2):
                hh = 2 * sid + qq
                cur3, prev3, yT = Bf["cur3"][qq], Bf["prev3"][qq], Bf["yT"][qq]
                cs = hh * HW
                h0 = hh * NH
                w = W_
                LD(cur3.name, cur3[:], prw3[1 + t0:1 + t0 + 64, :, cs:cs + HW], [cur3.name])
                LD(prev3.name, prev3[:], prw3[t0:t0 + 64, :, cs:cs + HW], [prev3.name])
                if short:
                    LD(prev3.name, prev3[48:49, :, :], hrw3[seq:seq + 1, :, cs:cs + HW], [prev3.name])
                mu3 = mu_bc[:, 0:3072].rearrange("p (j c) -> p j c", j=3)[:, :, cs:cs + HW]
                TT("dve", prev3[:], prev3[:], cur3[:], ALU.subtract, [prev3.name, cur3.name], [prev3.name])
                TT("pool", prev3[:], prev3[:], mu3, ALU.mult, [prev3.name, mu_bc.name], [prev3.name])
                TT("dve", cur3[:], cur3[:], prev3[:], ALU.add, [prev3.name, cur3.name], [cur3.name])
                r_, k_, v_ = cur3[:, 0, :], cur3[:, 1, :], cur3[:, 2, :]
                c3 = [cur3.name]
                b_lw, b_la, b_lg = bank(), bank(), bank()
                MM([(b_lw[0:64, 0:HW], lT[:], w2[:, cs:cs + HW], True, True)], [lT.name, w2.name], [b_lw.name])
                MM([(b_la[0:64, 0:HW], lTb[:, 0, :], a2[:, cs:cs + HW], True, True)], [lTb.name, a2.name], [b_la.name])
                MM([(b_lg[0:64, 0:HW], lTb[:, 1, :], g2[:, 0, cs:cs + HW], True, False),
                    (b_lg[0:64, 0:HW], lTb[:, 2, :], g2[:, 1, cs:cs + HW], False, True)], [lTb.name, g2.name], [b_lg.name])
                CP("act", w["gg"][:], b_lg[0:64, 0:HW], [b_lg.name], ["gg"])
                TT("dve", w["tA"][:], b_lw[0:64, 0:HW], w0_bc[:, cs:cs + HW], ALU.add, [b_lw.name, w0_bc.name], ["tA"])
                ACT(w["tA"][:], w["tA"][:], AF.Exp, ["tA"], ["tA"], scale=-1.0)
                ACT(w["tA"][:], w["tA"][:], AF.Ln, ["tA"], ["tA"], bias=1.0)
                ACT(w["ew"][:], w["tA"][:], AF.Exp, ["tA"], ["ew"], bias=-0.5, scale=-1.0)
                if short:
                    TS("dve", w["ew"][:], w["ew"][:], rowmask[:, 0:1], None, ALU.mult, None, ["ew", rowmask.name], ["ew"])
                TT("dve", w["av"][:], b_la[0:64, 0:HW], a0_bc[:, cs:cs + HW], ALU.add, [b_la.name, a0_bc.name], ["av"])
                ACT(w["av"][:], w["av"][:], AF.Sigmoid, ["av"], ["av"])
                TT("pool", w["kkn"][:], k_, kk_bc[:, cs:cs + HW], ALU.mult, c3 + [kk_bc.name], ["kkn"])
                ACT(w["tB"][:], w["kkn"][:], AF.Square, ["kkn"], ["tB"])
                RED(sm["n2"][:], h3(w["tB"][:]), ["tB"], ["n2"])
                TS("dve", sm["n2"][:], sm["n2"][:], 1e-24, None, ALU.max, None, ["n2"], ["n2"])
                ACT(sm["n2"][:], sm["n2"][:], AF.Ln, ["n2"], ["n2"])
                ACT(sm["rn"][:], sm["n2"][:], AF.Exp, ["n2"], ["rn"], scale=-0.5)
                for j_ in range(NH):
                    ACT(w["kkn"][:, j_ * 64:(j_ + 1) * 64], w["kkn"][:, j_ * 64:(j_ + 1) * 64], AF.Identity, ["kkn", "rn"], ["kkn"],
                        scale=sm["rn"][:, j_:j_ + 1])
                if short:
                    TS("dve", w["kkn"][:], w["kkn"][:], rowmask[:, 0:1], None, ALU.mult, None, ["kkn", rowmask.name], ["kkn"])
                STT(w["tB"][:], w["av"][:], -1.0, ka_bc[:, cs:cs + HW], ALU.add, ALU.mult, ["av", ka_bc.name], ["tB"])
                STT(w["kp"][:], w["tB"][:], 1.0, k_, ALU.add, ALU.mult, ["tB"] + c3, ["kp"])
                if short:
                    TS("dve", w["kp"][:], w["kp"][:], rowmask[:, 0:1], None, ALU.mult, None, ["kp", rowmask.name], ["kp"])
                b_cn = bank()
                MM([(b_cn[0:64, 0:HW], tri[:], w["ew"][:], True, True)], [tri.name, "ew"], [b_cn.name])
                b_wc = bank()
                MM([(b_wc[0:64, j:j + 1], w["ew"][:, j * 64:(j + 1) * 64], ones[:, 0:1], True, True) for j in range(NH)],
                   ["ew", ones.name], [b_wc.name])
                ACT(WC[:], b_wc[0:64, 0:NH], AF.Exp, [b_wc.name], [WC.name], scale=-1.0)
                ACT(w["Ep"][:], b_cn[0:64, 0:HW], AF.Exp, [b_cn.name], ["Ep"], scale=-1.0)
                ACT(w["Em"][:], b_cn[0:64, 0:HW], AF.Exp, [b_cn.name], ["Em"])
                TT("dve", w["tA"][:], b_cn[0:64, 0:HW], w["ew"][:], ALU.subtract, [b_cn.name, "ew"], ["tA"])
                ACT(w["Wp"][:], w["tA"][:], AF.Exp, ["tA"], ["Wp"], scale=-1.0)
                TT("dve", w["rt"][:], r_, w["Ep"][:], ALU.mult, c3 + ["Ep"], ["rt"])
                TT("pool", w["kt"][:], w["kp"][:], w["Em"][:], ALU.mult, ["kp", "Em"], ["kt"])
                TT("pool", w["bt"][:], w["kkn"][:], w["av"][:], ALU.mult, ["kkn", "av"], ["bt"])
                TT("pool", w["bt"][:], w["bt"][:], w["Em"][:], ALU.mult, ["bt", "Em"], ["bt"])
                STT(w["at"][:], w["kkn"][:], -1.0, w["Wp"][:], ALU.mult, ALU.mult, ["kkn", "Wp"], ["at"])
                i64 = ident[0:64, 0:64]
                for src, dst, dkey, eng in (("at", ARTb[:, :, 0, :], (ARTb.name, 0), "act"), ("rt", ARTb[:, :, 1, :], (ARTb.name, 1), "dve"),
                                            ("bt", btT[:], btT.name, "act"), ("kt", ktT[:], ktT.name, "dve")):
                    bk = bank()
                    TR([(bk[0:64, j * 64:(j + 1) * 64], w[src][:, j * 64:(j + 1) * 64], i64) for j in range(NH)], [src, ident.name], [bk.name])
                    CP(eng, dst, bk[0:64, 0:HW].rearrange("p (h t) -> p h t", h=NH), [bk.name], [dkey])
                ARTbk = [(ARTb.name, 0), (ARTb.name, 1)]
                CP("act", w["vb"][:], v_, c3, ["vb"])
                CP("act", w["btb"][:], w["bt"][:], ["bt"], ["btb"])
                CP("act", w["ktb"][:], w["kt"][:], ["kt"], ["ktb"])
                for lhs, lkey, dstt in ((btT, btT.name, AB), (ktT, ktT.name, AK)):
                    for hb in range(NH // 4):
                        bk = bank()
                        MM([(bk[0:64, j * 128:(j + 1) * 128], lhs[:, hb * 4 + j, :], ARTb[:, hb * 4 + j, :, :].rearrange("p a t -> p (a t)"), True, True)
                            for j in range(4)], [lkey] + ARTbk, [bk.name])
                        TT("dve", dstt[:, hb * 4:(hb + 1) * 4, :, :], bk[0:64, :].rearrange("p (h a t) -> p h a t", h=4, a=2),
                           msl[:].unsqueeze(1).to_broadcast([64, 4, 2, 64]), ALU.mult, [bk.name, msl.name], [(dstt.name, hb)])
                ABk = [(AB.name, i_) for i_ in range(NH // 4)]
                AKk = [(AK.name, i_) for i_ in range(NH // 4)]
                bk = bank()
                MM([(bk[0:64, j * 64:(j + 1) * 64], ARTb[:, j, 0, :], btT[:, j, :], True, True) for j in range(NH)], ARTbk + [btT.name], [bk.name])
                TT("dve", Pm[0][:], bk[0:64, 0:HW].rearrange("p (h s) -> p h s", h=NH), mgt[:].unsqueeze(1).to_broadcast([64, NH, 64]), ALU.mult,
                   [bk.name, mgt.name], [Pm[0].name])
                Q0 = AB[:, :, 0, :]
                TT("pool", XT[:], Q0, identb[:].unsqueeze(1).to_broadcast([64, NH, 64]), ALU.add, ABk + [identb.name], [XT.name])
                pi = 0
                for lvl in range(1, 6):
                    Pc, Qc, Pn, Qn = Pm[pi], Qm[pi], Pm[1 - pi], Qm[1 - pi]
                    Qcv = Q0 if lvl == 1 else Qc[:]
                    Qck = ABk if lvl == 1 else [Qc.name]
                    bp = bank()
                    MM([(bp[0:64, j * 64:(j + 1) * 64], Qcv[:, j, :], Pc[:, j, :], True, True) for j in range(NH)], [Pc.name] + Qck, [bp.name])
                    CP("act", Pn[:], bp[0:64, 0:HW].rearrange("p (h s) -> p h s", h=NH), [bp.name], [Pn.name])
                    if lvl < 5:
                        bq_ = bank()
                        MM([(bq_[0:64, j * 64:(j + 1) * 64], Pc[:, j, :], Qcv[:, j, :], True, True) for j in range(NH)], [Pc.name] + Qck, [bq_.name])
                        CP("dve", Qn[:], bq_[0:64, 0:HW].rearrange("p (h s) -> p h s", h=NH), [bq_.name], [Qn.name])
                    bz = bank()
                    MM([(bz[0:64, j * 64:(j + 1) * 64], Pn[:, j, :], XT[:, j, :], True, True) for j in range(NH)], [Pn.name, XT.name], [bz.name])
                    TT("dve", XT[:], XT[:], bz[0:64, 0:HW].rearrange("p (h s) -> p h s", h=NH), ALU.add, [XT.name, bz.name], [XT.name])
                    pi = 1 - pi
                STk = (ST.name, hh)
                STbk = (STb.name, hh)
                bk = bank()
                sp_ = []
                for j in range(NH):
                    o = bk[0:64, j * 64:(j + 1) * 64]
                    sp_.append((o, ARTb[:, j, 0, :], STb[:, h0 + j, :], True, False))
                    sp_.append((o, AK[:, j, 0, :], w["vb"][:, j * 64:(j + 1) * 64], False, True))
                MM(sp_, ARTbk + AKk + ["vb", STbk], [bk.name])
                CP("act", RHSb[:], bk[0:64, 0:HW], [bk.name], [RHSb.name])
                bk = bank()
                MM([(bk[0:64, j * 64:(j + 1) * 64], XT[:, j, :], RHSb[:, j * 64:(j + 1) * 64], True, True) for j in range(NH)],
                   [XT.name, RHSb.name], [bk.name])
                CP("dve", w["Us"][:], bk[0:64, 0:HW], [bk.name], ["Us"])
                by = bank()
                sp_ = []
                for j in range(NH):
                    o = by[0:64, j * 64:(j + 1) * 64]
                    sp_.append((o, ARTb[:, j, 1, :], STb[:, h0 + j, :], True, False))
                    sp_.append((o, AB[:, j, 1, :], w["Us"][:, j * 64:(j + 1) * 64], False, False))
                    sp_.append((o, AK[:, j, 1, :], w["vb"][:, j * 64:(j + 1) * 64], False, True))
                MM(sp_, ARTbk + ABk + AKk + ["vb", STbk, "Us"], [by.name])
                bs_ = bank()
                sp_ = []
                for j in range(NH):
                    o = bs_[0:64, j * 64:(j + 1) * 64]
                    sp_.append((o, w["btb"][:, j * 64:(j + 1) * 64], w["Us"][:, j * 64:(j + 1) * 64], True, False))
                    sp_.append((o, w["ktb"][:, j * 64:(j + 1) * 64], w["vb"][:, j * 64:(j + 1) * 64], False, True))
                MM(sp_, ["btb", "ktb", "Us", "vb"], [bs_.name])
                STh = ST[:, h0:h0 + NH, :]
                TT("dve", STh, STh, bs_[0:64, 0:HW].rearrange("p (h v) -> p h v", h=NH), ALU.add, [STk, bs_.name], [STk])
                TT("pool", STh, STh, bc8(WC[:]), ALU.mult, [STk, WC.name], [STk])
                CP("act", STb[:, h0:h0 + NH, :], STh, [STk], [STbk])
                CP("act", w["Ys"][:], by[0:64, 0:HW], [by.name], ["Ys"])
                RED(sm["s1"][:], h3(w["Ys"][:]), ["Ys"], ["s1"])
                ACT(w["tA"][:], w["Ys"][:], AF.Square, ["Ys"], ["tA"])
                RED(sm["s2"][:], h3(w["tA"][:]), ["tA"], ["s2"])
                TS("dve", sm["mean"][:], sm["s1"][:], 1.0 / 64, None, ALU.mult, None, ["s1"], ["mean"])
                TT("dve", sm["var"][:], sm["mean"][:], sm["mean"][:], ALU.mult, ["mean"], ["var"])
                STT(sm["var"][:], sm["s2"][:], 1.0 / 64, sm["var"][:], ALU.mult, ALU.subtract, ["s2", "var"], ["var"])
                ACT(sm["var"][:], sm["var"][:], AF.Ln, ["var"], ["var"], bias=RW_GN_EPS)
                ACT(sm["var"][:], sm["var"][:], AF.Exp, ["var"], ["var"], scale=-0.5)
                STT(sm["s1"][:], sm["mean"][:], -1.0, sm["var"][:], ALU.mult, ALU.mult, ["mean", "var"], ["s1"])
                for j_ in range(NH):
                    ACT(w["Ys"][:, j_ * 64:(j_ + 1) * 64], w["Ys"][:, j_ * 64:(j_ + 1) * 64], AF.Identity, ["Ys", "var", "s1"], ["Ys"],
                        scale=sm["var"][:, j_:j_ + 1], bias=sm["s1"][:, j_:j_ + 1])
                TT("pool", w["Ys"][:], w["Ys"][:], gnw_bc[:, cs:cs + HW], ALU.mult, ["Ys", gnw_bc.name], ["Ys"])
                TT("pool", w["Ys"][:], w["Ys"][:], gnb_bc[:, cs:cs + HW], ALU.add, ["Ys", gnb_bc.name], ["Ys"])
                TT("pool", w["tB"][:], r_, w["kp"][:], ALU.mult, c3 + ["kp"], ["tB"])
                TT("pool", w["tB"][:], w["tB"][:], rk_bc[:, cs:cs + HW], ALU.mult, ["tB", rk_bc.name], ["tB"])
                RED(sm["bs"][:], h3(w["tB"][:]), ["tB"], ["bs"])
                TT("dve", h3(w["tB"][:]), h3(v_), bc8(sm["bs"][:]), ALU.mult, c3 + ["bs"], ["tB"])
                TT("pool", w["Ys"][:], w["Ys"][:], w["tB"][:], ALU.add, ["Ys", "tB"], ["Ys"])
                TT("dve", w["yo"][:], w["Ys"][:], w["gg"][:], ALU.mult, ["Ys", "gg"], ["yo"])
                bk = bank()
                NJ = HW // 128
                TR([(bk[:, j * 64:(j + 1) * 64], w["yo"][:, j * 128:(j + 1) * 128], i64) for j in range(NJ)], ["yo", ident.name], [bk.name])
                CP("act", yT[:], bk[:, 0:NJ * 64].rearrange("p (j t) -> p j t", j=NJ), [bk.name], [yT.name])
                T.dma("act", yT.name, yrwT[:, hh * NJ:(hh + 1) * NJ, t0:t0 + 64], yT[:], reads=[yT.name])
            if c in (0, 1, cfg.nchunks - 1):
                store_states_rw(seq, sid)

        def ssd_chunk(c):
            t0 = c * 64
            seq = min(c, 2)
            short = c < 3
            if c < 3:
                load_states_ssd(seq)
            LD(Fin.name, Fin[:], xbcv[:, :, 4 + t0 - 3:4 + t0 + 64], [Fin.name])
            if short:
                CP("pool", Fin[:, :, 48:51], histT[:, :, seq * 3:(seq + 1) * 3], [histT.name, Fin.name], [Fin.name])
            LD(pdt.name, pdt[:], S["P_DT"][t0:t0 + 64, :], [pdt.name])

            def cwb(i):
                return cw[:, :, i:i + 1].to_broadcast([128, 24, 64])
            TT("pool", XA, Fin[:, :, 0:64], cwb(0), ALU.mult, [Fin.name, cw.name], [XAk])
            for i in range(1, 4):
                TT("pool", XB2, Fin[:, :, i:i + 64], cwb(i), ALU.mult, [Fin.name, cw.name], [XBk])
                TT("dve", XA, XA, XB2, ALU.add, [XAk, XBk], [XAk])
            TT("dve", XA, XA, cb[:].unsqueeze(2).to_broadcast([128, 24, 64]), ALU.add, [XAk, cb.name], [XAk])
            ACT(XB2, XA, AF.Silu, [XAk], [XBk])
            XB = XB2
            d = dts
            TT("dve", d["dt"][:], pdt[:], dtb_bc[:], ALU.add, [pdt.name, dtb_bc.name], ["dt"])
            ACT(d["dt"][:], d["dt"][:], AF.Exp, ["dt"], ["dt"])
            ACT(d["dt"][:], d["dt"][:], AF.Ln, ["dt"], ["dt"], bias=1.0)
            if short:
                TS("dve", d["dt"][:], d["dt"][:], rowmask[:, 0:1], None, ALU.mult, None, ["dt", rowmask.name], ["dt"])
            TT("dve", d["adt"][:], d["dt"][:], aneg_bc[:], ALU.mult, ["dt", aneg_bc.name], ["adt"])
            b_ac = bank()
            MM([(b_ac[0:64, 0:SSM_H], tri[:], d["adt"][:], True, True)], [tri.name, "adt"], [b_ac.name])
            b_tot = bank()
            MM([(b_tot[:, 0:SSM_H], ones[:], d["adt"][:], True, True)], [ones.name, "adt"], [b_tot.name])
            CP("dve", d["acs"][:], b_ac[0:64, 0:SSM_H], [b_ac.name], ["acs"])
            ACT(d["eacs"][:], b_ac[0:64, 0:SSM_H], AF.Exp, [b_ac.name], ["eacs"])
            ACT(cdec[:], b_tot[:, 0:SSM_H], AF.Exp, [b_tot.name], [cdec.name])
            TT("dve", d["toend"][:], b_tot[0:64, 0:SSM_H], d["acs"][:], ALU.subtract, [b_tot.name, "acs"], ["toend"])
            ACT(d["toend"][:], d["toend"][:], AF.Exp, ["toend"], ["toend"])
            bk = bank()
            TR([(bk[0:64, j * 128:(j + 1) * 128], XB[:, 16 + j, :], ident[:]) for j in range(4)], [XBk, ident.name], [bk.name])
            CP("act", Btm[:], bk[0:64, :], [bk.name], [Btm.name])
            g_ = G_
            for g in range(4):
                hs = slice(8 * g, 8 * g + 8)
                yT2 = yT2s[g % 2]
                zz = ZZ[g % 2]
                HTk = (HT.name, g)
                HTg = HT[:, g * 512:(g + 1) * 512]
                bk = bank()
                TR([(bk[0:64, j * 128:(j + 1) * 128], XB[:, 4 * g + j, :], ident[:]) for j in range(4)], [XBk, ident.name], [bk.name])
                CP("act", g_["xs"][:], bk[0:64, :], [bk.name], ["xs"])
                LD(zz.name, zz[:], S["P_Z"][t0:t0 + 64, g * 512:(g + 1) * 512], [zz.name])
                ACT(zz[:], zz[:], AF.Silu, [zz.name], [zz.name])
                TT("dve", h3s(g_["xdt"][:]), h3s(g_["xs"][:]), bc8s(d["dt"][:, hs]), ALU.mult, ["xs", "dt"], ["xdt"])
                TT("pool", h3s(g_["xdtw"][:]), h3s(g_["xdt"][:]), bc8s(d["toend"][:, hs]), ALU.mult, ["xdt", "toend"], ["xdtw"])
                bcb = bank()
                MM([(bcb[0:64, 0:64], XB[:, 16 + g, :], XB[:, 20 + g, :], True, True)], [XBk], [bcb.name])
                TT("dve", cbm[:], bcb[0:64, 0:64], msl[:, 1, :], ALU.mult, [bcb.name, msl.name], [cbm.name])
                TT("pool", h3s(g_["ML"][:]), bc8s(d["adt"][:, hs]), mgt[:].unsqueeze(1).to_broadcast([64, 8, 64]), ALU.mult,
                   ["adt", mgt.name], ["ML"])
                bsg = bank()
                MM([(bsg[0:64, j * 64:(j + 1) * 64], g_["ML"][:, j * 64:(j + 1) * 64], tri[:], True, True) for j in range(8)],
                   ["ML", tri.name], [bsg.name])
                ACT(g_["EX"][:], bsg[0:64, :], AF.Exp, [bsg.name], ["EX"])
                TT("pool", h3s(g_["MT"][:]), h3s(g_["EX"][:]), cbm[:].unsqueeze(1).to_broadcast([64, 8, 64]), ALU.mult, ["EX", cbm.name], ["MT"])
                byd = bank()
                MM([(byd[0:64, j * 64:(j + 1) * 64], g_["MT"][:, j * 64:(j + 1) * 64], g_["xdt"][:, j * 64:(j + 1) * 64], True, True)
                    for j in range(8)], ["MT", "xdt"], [byd.name])
                byo = bank()
                MM([(byo[0:64, :], XB[:, 20 + g, :], HTg, True, True)], [XBk, HTk], [byo.name])
                TT("dve", h3s(g_["yy"][:]), byo[0:64, :].rearrange("p (h k) -> p h k", h=8), bc8s(d["eacs"][:, hs]), ALU.mult,
                   [byo.name, "eacs"], ["yy"])
                TT("dve", g_["yy"][:], g_["yy"][:], byd[0:64, :], ALU.add, ["yy", byd.name], ["yy"])
                TT("pool", h3s(g_["t1"][:]), h3s(g_["xs"][:]), bc8s(dsk_bc[:, hs]), ALU.mult, ["xs", dsk_bc.name], ["t1"])
                TT("pool", g_["yy"][:], g_["yy"][:], g_["t1"][:], ALU.add, ["yy", "t1"], ["yy"])
                TT("dve", g_["yy"][:], g_["yy"][:], zz[:], ALU.mult, ["yy", zz.name], ["yy"])
                ACT(g_["t1"][:], g_["yy"][:], AF.Square, ["yy"], ["t1"])
                T.op("dve", lambda e, o=ss1[:], i=g_["t1"][:]: e.tensor_reduce(out=o, in_=i, axis=AX.X, op=ALU.add), reads=["t1"], writes=[ss1.name])
                ACT(ss1[:], ss1[:], AF.Ln, [ss1.name], [ss1.name], bias=RMS_EPS, scale=1.0 / 512)
                ACT(ss1[:], ss1[:], AF.Exp, [ss1.name], [ss1.name], scale=-0.5)
                TS("dve", g_["yy"][:], g_["yy"][:], ss1[:, 0:1], None, ALU.mult, None, ["yy", ss1.name], ["yy"])
                bk = bank()
                TR([(bk[:, j * 64:(j + 1) * 64], g_["yy"][:, j * 128:(j + 1) * 128], ident[0:64, 0:64]) for j in range(4)], ["yy", ident.name], [bk.name])
                TT("dve", yT2[:], bk[:, 0:256].rearrange("p (j t) -> p j t", j=4), snw[:, 4 * g:4 * g + 4].unsqueeze(2).to_broadcast([128, 4, 64]),
                   ALU.mult, [bk.name, snw.name], [yT2.name])
                T.dma("pool", yT2.name, yssT[:, 4 * g:4 * g + 4, t0:t0 + 64], yT2[:], reads=[yT2.name])
                bst = bank()
                MM([(bst[:, :], Btm[:, g * 128:(g + 1) * 128], g_["xdtw"][:], True, True)], [Btm.name, "xdtw"], [bst.name])
                TT("pool", HTg.rearrange("p (h k) -> p h k", h=8), HTg.rearrange("p (h k) -> p h k", h=8),
                   cdec[:, hs].unsqueeze(2).to_broadcast([128, 8, 64]), ALU.mult, [HTk, cdec.name], [HTk])
                TT("dve", HTg, HTg, bst[:, :], ALU.add, [HTk, bst.name], [HTk])
            if c in (0, 1, cfg.nchunks - 1):
                store_states_ssd(seq)

        def rw_stream(sid):
            bpool[0] = [3 * sid, 3 * sid + 1, 3 * sid + 2]
            kpref[0] = f"s{sid}:"
            for c in range(cfg.nchunks):
                rwkv_chunk(c, sid)
            kpref[0] = ""

        def ssd_stream():
            bpool[0] = [6, 7]
            for c in range(cfg.nchunks):
                ssd_chunk(c)

        streams = [T.capture(lambda i=i: rw_stream(i)) for i in range(NSTR)]
        streams.append(T.capture(ssd_stream))
        T.replay_merged(streams)
        T.flush()


def phase_outp(nc, T, cfg, pb, S, W, name):
    TB = cfg.tile_blocks
    TT = TB * 128
    npc = (TT + 511) // 512
    pw = TT // npc
    GATE0 = 8672
    with ExitStack() as st:
        sb = _sbuf(nc, st, name)
        yrT = sb("yrT", [128, 8, TT], BF16)
        ysT = sb("ysT", [128, NKC, TT], BF16)
        x1T = sb("x1T", [128, NKC, TT], BF16)
        mT = sb("mT", [128, NKC, TT], BF16)
        bg = sb("bg", [128, 32])
        wra = [sb(f"wra{i}", [128, 8, 256], BF16) for i in range(2)]
        wsa = [sb(f"wsa{i}", [128, NKC, 256], BF16) for i in range(2)]
        wga = [sb(f"wga{i}", [128, NKC, 256], BF16) for i in range(2)]
        wgb = [sb(f"wgb{i}", [128, NKC, 256], BF16) for i in range(2)]
        wo = [sb(f"wo{i}", [128, NKC, 512], BF16) for i in range(2)]
        sga = [sb(f"sga{i}", [128, pw]) for i in range(2)]
        sgb = [sb(f"sgb{i}", [128, pw]) for i in range(2)]
        xr = sb("xr", [128, TB, 512])
        so = [sb(f"so{i}", [128, 512]) for i in range(2)]
        T.dma("sp", bg.name, bg[:], W["b_gate"][:, :], writes=[bg.name])
        yrv = S["YRWT"].rearrange("(c p) r -> p c r", p=128)
        ysv = S["YSSMT"].rearrange("(c p) r -> p c r", p=128)
        x1v = S["X1T"].rearrange("(c p) r -> p c r", p=128)
        wrv = W["w_rw_out"].rearrange("(c p) n -> p c n", p=128)
        wsv = W["w_ssm_out"].rearrange("(c p) n -> p c n", p=128)
        wiv = W["w_in"].rearrange("(c p) n -> p c n", p=128)
        wov = W["w_out"].rearrange("(c p) n -> p c n", p=128)
        q = 0
        xq = 0
        for t in range(cfg.ntiles):
            r0 = t * TT
            T.dma("sp", yrT.name, yrT[:], yrv[:, :, r0:r0 + TT], writes=[yrT.name])
            T.dma("sp", ysT.name, ysT[:], ysv[:, :, r0:r0 + TT], writes=[ysT.name])
            T.dma("sp", x1T.name, x1T[:], x1v[:, :, r0:r0 + TT], writes=[x1T.name])
            for cc in range(NKC):
                s = (cc // 2) % 2
                cj = cc % 2
                if cj == 0:
                    c0 = (cc // 2) * 256
                    cg_ = cc // 2
                    srcs = ((wra[s], wrv[:, :, c0:c0 + 256], 0, 8), (wsa[s], wsv[:, :, c0:c0 + 256], 8, NKC),
                            (wga[s], wiv[:, :, GATE0 + c0:GATE0 + c0 + 256], 24, NKC),
                            (wgb[s], wiv[:, :, GATE0 + D + c0:GATE0 + D + c0 + 256], 40, NKC))
                    for wt, src_ap, k0, nk in srcs:
                        cview = S["WOC_B"][cg_].rearrange("p (k n) -> p k n", n=256)[:, k0:k0 + nk, :]
                        ck = ("woc", cg_, k0)
                        if t > 0:
                            T.dma("pool", wt.name, wt[:], cview, reads=[ck], writes=[wt.name])
                        else:
                            T.dma("pool", wt.name, wt[:], src_ap, writes=[wt.name])
                            if cfg.ntiles > 1:
                                T.dma("sp", wt.name + "_wb", cview, wt[:], reads=[wt.name], writes=[ck])
                for pc in range(npc):
                    ps = slice(pc * pw, (pc + 1) * pw)
                    banks = [pb[(4 * q + i) % 8] for i in range(4)]
                    sa, sb2 = sga[q % 2], sgb[q % 2]
                    q += 1
                    for bank, w, act, nk in ((banks[0], wra[s], yrT, 8), (banks[1], wsa[s], ysT, NKC),
                                             (banks[2], wga[s], x1T, NKC), (banks[3], wgb[s], x1T, NKC)):
                        T.op("pe", [lambda e, kc=kc, bank=bank, w=w, act=act, ps=ps, nk=nk, cj=cj: e.matmul(
                            bank[:, 0:pw], lhsT=w[:, kc, cj * 128:(cj + 1) * 128], rhs=act[:, kc, ps], start=(kc == 0), stop=(kc == nk - 1)) for kc in range(nk)],
                            reads=[w.name, act.name], writes=[bank.name])
                    T.op("act", lambda e, sa=sa, bank=banks[2], cc=cc: e.activation(out=sa[:], in_=bank[:, 0:pw], func=AF.Sigmoid, bias=bg[:, cc:cc + 1]),
                         reads=[banks[2].name, bg.name], writes=[sa.name])
                    T.op("act", lambda e, sb2=sb2, bank=banks[3], cc=cc: e.activation(out=sb2[:], in_=bank[:, 0:pw], func=AF.Sigmoid, bias=bg[:, 16 + cc:17 + cc]),
                         reads=[banks[3].name, bg.name], writes=[sb2.name])
                    T.op("dve", lambda e, sa=sa, bank=banks[0]: e.tensor_tensor(out=sa[:], in0=sa[:], in1=bank[:, 0:pw], op=ALU.mult),
                         reads=[sa.name, banks[0].name], writes=[sa.name])
                    T.op("dve", lambda e, sb2=sb2, bank=banks[1]: e.tensor_tensor(out=sb2[:], in0=sb2[:], in1=bank[:, 0:pw], op=ALU.mult),
                         reads=[sb2.name, banks[1].name], writes=[sb2.name])
                    T.op("pool", lambda e, sa=sa, sb2=sb2, cc=cc, ps=ps: e.tensor_tensor(out=mT[:, cc, ps], in0=sa[:], in1=sb2[:], op=ALU.add),
                         reads=[sa.name, sb2.name], writes=[(mT.name, cc, pc)])
            mkeys = [(mT.name, cc, pc) for cc in range(NKC) for pc in range(npc)]
            for cb in range(4):
                w = wo[cb % 2]
                T.dma("pool", w.name, w[:], wov[:, :, cb * 512:(cb + 1) * 512], writes=[w.name])
                T.dma("sp", xr.name, xr[:], S["X1"][r0:r0 + TT, cb * 512:(cb + 1) * 512].rearrange("(b p) n -> p b n", p=128), writes=[xr.name])
                for b in range(TB):
                    bank = pb[q % 8]
                    q += 1
                    T.op("pe", [lambda e, kc=kc, bank=bank, w=w, b=b: e.matmul(
                        bank[:, :], lhsT=mT[:, kc, b * 128:(b + 1) * 128], rhs=w[:, kc, :], start=(kc == 0), stop=(kc == NKC - 1))
                        for kc in range(NKC)], reads=[w.name] + mkeys, writes=[bank.name])
                    rows = r0 + b * 128
                    o = so[xq % 2]
                    xq += 1
                    T.op("dve", lambda e, bank=bank, b=b, o=o: e.scalar_tensor_tensor(
                        out=o[:], in0=xr[:, b, :], scalar=ALPHA, in1=bank[:, :], op0=ALU.mult, op1=ALU.add),
                        reads=[bank.name, xr.name], writes=[o.name])
                    T.dma("sp", o.name, S["XPRE"][rows:rows + 128, cb * 512:(cb + 1) * 512], o[:], reads=[o.name])
        T.flush()


WEIGHT_SPECS = [
    ("ffn1_gu", [D, 2 * DFF]), ("ffn1_dn", [DFF, D]), ("ln1_g", [1, D]), ("ln1_b", [1, D]),
    ("w_in", [D, N_IN]), ("b_gate", [128, 32]),
    ("rw_mu", [1, RW_COLS]), ("rw_w0", [1, RW_DIM]), ("rw_w2", [96, RW_DIM]), ("rw_a0", [1, RW_DIM]),
    ("rw_a2", [96, RW_DIM]), ("rw_g2", [256, RW_DIM]), ("rw_kk", [1, RW_DIM]), ("rw_ka", [1, RW_DIM]),
    ("rw_rk", [1, RW_DIM]), ("rw_gn_w", [1, RW_DIM]), ("rw_gn_b", [1, RW_DIM]),
    ("conv_w", [128, 24 * 4]), ("conv_b", [128, 24]), ("dt_bias", [1, SSM_H]), ("a_log", [1, SSM_H]),
    ("d_skip", [1, SSM_H]), ("ssm_norm_w", [128, 16]),
    ("w_rw_out", [RW_DIM, D]), ("w_ssm_out", [D, D]), ("w_out", [D, D]), ("ln2_g", [1, D]), ("ln2_b", [1, D]),
    ("ffn2_gu", [D, 2 * DFF]), ("ffn2_dn", [DFF, D]), ("ln3_g", [1, D]), ("ln3_b", [1, D]),
]


def build_program(cfg):
    nc = bass.Bass("TRN2", target_bir_lowering=False)
    R = cfg.R
    dbg = "ExternalOutput" if cfg.debug else None

    def din(name, shape, dt=F32):
        return nc.dram_tensor(name, shape, dt, kind="ExternalInput").ap()

    def dout(name, shape, dt=F32):
        return nc.dram_tensor(name, shape, dt, kind="ExternalOutput").ap()

    def dscr(name, shape, dt=F32):
        if name in cfg.scratch_in:
            return nc.dram_tensor(name, shape, dt, kind="ExternalInput").ap()
        if dbg:
            return nc.dram_tensor(name, shape, dt, kind=dbg).ap()
        return nc.dram_tensor(name, shape, dt).ap()

    I = {}
    I["xin"] = din("xin", [R, D])
    I["xinT"] = din("xinT", [D, R])
    I["hist_rw"] = din("hist_rw", [3, RW_COLS])
    I["hist_conv"] = din("hist_conv", [9, CONV_DIM])
    I["wkv0"] = din("wkv0", [3 * RW_H * 64, 64])
    I["ssm0"] = din("ssm0", [3 * SSM_H * 64, SSM_N])
    W = {n: din(n, shp) for n, shp in WEIGHT_SPECS}
    O = {}
    O["yout"] = dout("yout", [R, D])
    O["o_shift"] = dout("o_shift", [3, RW_COLS])
    O["o_conv"] = dout("o_conv", [9, CONV_DIM])
    O["o_wkv"] = dout("o_wkv", [3 * RW_H * 64, 64])
    O["o_ssm"] = dout("o_ssm", [3 * SSM_H * 64, SSM_N])
    S = {}
    S["X0T"] = dscr("X0T", [D, R], BF16)
    S["XPRE"] = dscr("XPRE", [R, D])
    S["X1"] = dscr("X1", [R, D])
    S["X1T"] = dscr("X1T", [D, R], BF16)
    S["X2"] = dscr("X2", [R, D])
    S["X2T"] = dscr("X2T", [D, R], BF16)
    S["P_RW"] = dscr("P_RW", [1 + R, RW_COLS])
    S["P_Z"] = dscr("P_Z", [R, D])
    S["P_DT"] = dscr("P_DT", [R, SSM_H])
    S["PT_XBC"] = dscr("PT_XBC", [CONV_DIM, 4 + R])
    S["YRWT"] = dscr("YRWT", [RW_DIM, R], BF16)
    S["YSSMT"] = dscr("YSSMT", [D, R], BF16)
    S["WDN_B"] = nc.dram_tensor("WDN_B", [DFF, D], BF16).ap()
    S["WOC_B"] = nc.dram_tensor("WOC_B", [8, 128, 56 * 256], BF16).ap()

    with ExitStack() as st:
        T = Tracker(nc, st)
        pb = [st.enter_context(nc.psum_tensor(f"pb{i}", [128, 512], F32)) for i in range(8)]
        T.excl = set(b.name for b in pb)

        def run(name, kind, fn, *a):
            T.rename = (name, kind)
            fn(*a)
            return cfg.stop_after == name

        phases = [
            ("ffn1", "ffn", phase_ffn, (nc, T, cfg, pb, I["xinT"], I["xin"], W["ffn1_gu"], W["ffn1_dn"], S["XPRE"], "ffn1", S["WDN_B"])),
            ("ln1", "ln", phase_ln, (nc, T, cfg, pb, S["XPRE"], S["X1"], S["X1T"], W["ln1_g"], W["ln1_b"], "ln1")),
            ("win", "win", phase_win, (nc, T, cfg, pb, S, W, O, "win")),
            ("mix", "mix", phase_mix, (nc, T, cfg, pb, S, W, I, O, "mix")),
            ("outp", "outp", phase_outp, (nc, T, cfg, pb, S, W, "outp")),
            ("ln2", "ln", phase_ln, (nc, T, cfg, pb, S["XPRE"], S["X2"], S["X2T"], W["ln2_g"], W["ln2_b"], "ln2")),
            ("ffn2", "ffn", phase_ffn, (nc, T, cfg, pb, S["X2T"], S["X2"], W["ffn2_gu"], W["ffn2_dn"], S["XPRE"], "ffn2", S["WDN_B"])),
            ("ln3", "ln", phase_ln, (nc, T, cfg, pb, S["XPRE"], O["yout"], None, W["ln3_g"], W["ln3_b"], "ln3")),
        ]
        for name, kind, fn, a in phases:
            if cfg.only is not None and name not in cfg.only:
                continue
            if run(name, kind, fn, *a):
                break
    return nc


def _pp(v, nchunk):
    return np.ascontiguousarray(np.asarray(v, np.float32).reshape(nchunk, 128).T)


def prep_weights(inp):
    f = lambda a: np.ascontiguousarray(np.asarray(a, np.float32))
    Wn = {}
    for n in ("ffn1_gu", "ffn1_dn", "w_in", "rw_w2", "rw_a2", "rw_g2", "w_rw_out", "w_ssm_out", "w_out", "ffn2_gu", "ffn2_dn"):
        Wn[n] = f(inp[n][0])
    for n in ("ln1_g", "ln1_b", "rw_mu", "rw_w0", "rw_a0", "dt_bias", "a_log", "d_skip", "ln2_g", "ln2_b", "ln3_g", "ln3_b"):
        Wn[n] = f(inp[n][0]).reshape(1, -1)
    for n in ("rw_kk", "rw_ka", "rw_rk", "rw_gn_w", "rw_gn_b"):
        Wn[n] = f(inp[n][0]).reshape(1, -1)
    Wn["b_gate"] = _pp(inp["b_gate"][0], 32)
    cw = np.asarray(inp["conv_w"][0], np.float32)
    Wn["conv_w"] = np.ascontiguousarray(cw.reshape(4, 24, 128).transpose(2, 1, 0).reshape(128, 96))
    Wn["conv_b"] = _pp(inp["conv_b"][0], 24)
    Wn["ssm_norm_w"] = _pp(inp["ssm_norm_w"][0], 16)
    return Wn


def core_inputs(inp, cfg, core, Wn):
    R = cfg.R
    b = core % 4
    sa, sb_ = 2 * core, 2 * core + 1
    xin = np.zeros((R, D), np.float32)
    xin[48:64] = inp["x_sample"][sa]
    xin[64 + 48:128] = inp["x_sample"][sb_]
    xin[128 + 48:192] = inp["meta_tokens"]
    nl = cfg.n_long * 64
    xin[192:192 + nl] = inp["x_prompt"][b][:nl]
    m = {"xin": xin, "xinT": np.ascontiguousarray(xin.T)}
    z = np.zeros
    m["hist_rw"] = np.ascontiguousarray(np.concatenate(
        [inp["state_rwkv_shift"][0, sa], inp["state_rwkv_shift"][0, sb_], z((1, RW_COLS), np.float32)], 0).astype(np.float32))
    m["hist_conv"] = np.ascontiguousarray(np.concatenate(
        [inp["state_conv"][0, sa], inp["state_conv"][0, sb_], z((3, CONV_DIM), np.float32)], 0).astype(np.float32))
    m["wkv0"] = np.ascontiguousarray(np.concatenate(
        [inp["state_wkv"][0, sa].reshape(-1, 64), inp["state_wkv"][0, sb_].reshape(-1, 64), z((RW_H * 64, 64), np.float32)], 0).astype(np.float32))
    m["ssm0"] = np.ascontiguousarray(np.concatenate(
        [inp["state_ssm"][0, sa].reshape(-1, SSM_N), inp["state_ssm"][0, sb_].reshape(-1, SSM_N), z((SSM_H * 64, SSM_N), np.float32)], 0).astype(np.float32))
    m.update(Wn)
    return m


def assemble(results, cfg):
    nl = cfg.n_long * 64
    f = np.float32
    y_prompt = np.zeros((4, nl, D), f)
    y_sample = np.zeros((16, 16, D), f)
    p_shift = np.zeros((1, 4, 1, RW_COLS), f)
    p_wkv = np.zeros((1, 4, RW_H, 64, 64), f)
    p_conv = np.zeros((1, 4, 3, CONV_DIM), f)
    p_ssm = np.zeros((1, 4, SSM_H, 64, SSM_N), f)
    s_shift = np.zeros((1, 16, 1, RW_COLS), f)
    s_wkv = np.zeros((1, 16, RW_H, 64, 64), f)
    s_conv = np.zeros((1, 16, 3, CONV_DIM), f)
    s_ssm = np.zeros((1, 16, SSM_H, 64, SSM_N), f)
    for c, r in enumerate(results):
        yo = np.asarray(r["yout"])
        osh = np.asarray(r["o_shift"])
        ocv = np.asarray(r["o_conv"])
        owk = np.asarray(r["o_wkv"]).reshape(3, RW_H, 64, 64)
        osm = np.asarray(r["o_ssm"]).reshape(3, SSM_H, 64, SSM_N)
        for i in range(2):
            s = 2 * c + i
            y_sample[s] = yo[i * 64 + 48:i * 64 + 64]
            s_shift[0, s, 0] = osh[i]
            s_conv[0, s] = ocv[i * 3:(i + 1) * 3]
            s_wkv[0, s] = owk[i]
            s_ssm[0, s] = osm[i]
        if c < 4:
            y_prompt[c] = yo[192:192 + nl]
            p_shift[0, c, 0] = osh[2]
            p_conv[0, c] = ocv[6:9]
            p_wkv[0, c] = owk[2]
            p_ssm[0, c] = osm[2]
    return (y_prompt, y_sample, p_shift, p_wkv, p_conv, p_ssm, s_shift, s_wkv, s_conv, s_ssm)


def kernel(**inputs):
    cfg = Cfg()
    inp = {k: np.asarray(v) for k, v in inputs.items()}
    nc = build_program(cfg)
    Wn = prep_weights(inp)
    maps = [core_inputs(inp, cfg, c, Wn) for c in range(8)]
    res = run_bass_kernel_spmd(nc, maps, core_ids=list(range(8)))
    return assemble(res.results, cfg)
```

```python
from contextlib import ExitStack
import numpy as np
import concourse.bass as bass
import concourse.mybir as mybir
from concourse.bass_utils import run_bass_kernel_spmd

F32 = mybir.dt.float32
BF16 = mybir.dt.bfloat16
AF = mybir.ActivationFunctionType
ALU = mybir.AluOpType
AX = mybir.AxisListType

D = 2048
DFF = 5632
NKC = D // 128
NFC = DFF // 128
RW_DIM = 1024
RW_H = 16
RW_COLS = 3520
SSM_H = 32
SSM_G = 4
SSM_N = 128
CONV_DIM = 3072
N_IN = 12768
ALPHA = 2.0 ** 0.25
LN_EPS = 1e-5
RW_GN_EPS = 64e-5
RMS_EPS = 1e-5
C = 64


class Tracker:
    ENGS = ("pe", "act", "dve", "pool", "sp")

    def __init__(self, nc, stack):
        self.nc = nc
        self.stack = stack
        self.sem = {e: stack.enter_context(nc.semaphore("s_" + e)) for e in self.ENGS}
        self.cnt = {e: 0 for e in self.ENGS}
        self.chan_sem = {}
        self.chan_cnt = {}
        self.seen = {}
        self.lastw = {}
        self.readers = {}
        self.prog = {e: [] for e in self.ENGS}
        self.nsem = len(self.ENGS)
        self.rename = None
        self.defer = None
        self._grp = None
        self.excl = set()

    def _split(self, reads, writes):
        if not self.excl:
            return list(reads), list(writes)
        r = [k for k in reads if k not in self.excl]
        w = list(writes) + [k for k in reads if k in self.excl]
        return r, w

    def _deps(self, reads, writes):
        deps = {}
        def add(d):
            if d is None:
                return
            s, v = d
            k = id(s)
            if k not in deps or deps[k][1] < v:
                deps[k] = (s, v)
        for k in reads:
            add(self.lastw.get(k))
        for k in writes:
            add(self.lastw.get(k))
            for d in self.readers.get(k, ()):
                add(d)
        return list(deps.values())

    def _emit_waits(self, eng, deps):
        for s, v in deps:
            key = (eng, id(s))
            if self.seen.get(key, 0) >= v:
                continue
            self.seen[key] = v
            self.prog[eng].append(lambda e, s=s, v=v: e.wait_ge(s, v))

    def _commit(self, dep, reads, writes):
        for k in reads:
            self.readers.setdefault(k, []).append(dep)
        for k in writes:
            self.lastw[k] = dep
            self.readers[k] = []

    def op(self, eng, fns, reads=(), writes=()):
        if self.defer is not None:
            self.defer.append(("op", (eng, fns, reads, writes), {}))
            return
        if not isinstance(fns, (list, tuple)):
            fns = [fns]
        reads, writes = self._split(reads, writes)
        self._emit_waits(eng, self._deps(reads, writes))
        self.cnt[eng] += 1
        n = self.cnt[eng]
        sem = self.sem[eng]
        for f in fns[:-1]:
            self.prog[eng].append(lambda e, f=f: f(e))
        last = fns[-1]
        self.prog[eng].append(lambda e, f=last, sem=sem: f(e).then_inc(sem, 1))
        self._commit((sem, n), reads, writes)

    def chan(self, name):
        if name not in self.chan_sem:
            self.chan_sem[name] = self.stack.enter_context(self.nc.semaphore("c_" + name))
            self.chan_cnt[name] = 0
            self.nsem += 1
        return self.chan_sem[name]

    def begin_group(self, chan):
        self._grp = (chan, [])

    def end_group(self):
        chan, keys = self._grp
        self._grp = None
        if chan in self.chan_sem:
            dep = (self.chan_sem[chan], self.chan_cnt[chan])
            for k in keys:
                self.lastw[k] = dep

    def dma(self, eng, chan, out, in_, reads=(), writes=(), **kw):
        if self.defer is not None:
            self.defer.append(("dma", (eng, chan, out, in_, reads, writes), kw))
            return
        if self._grp is not None:
            chan = self._grp[0]
            self._grp[1].extend(writes)
        if self.rename is not None:
            chan = chan.replace(self.rename[0], self.rename[1])
        sem = self.chan(chan)
        self._emit_waits(eng, self._deps(reads, writes))
        self.chan_cnt[chan] += 16
        n = self.chan_cnt[chan]
        self.prog[eng].append(lambda e, o=out, i=in_, sem=sem, kw=kw: e.dma_start(out=o, in_=i, **kw).then_inc(sem, 16))
        self._commit((sem, n), reads, writes)

    def capture(self, fn):
        assert self.defer is None
        self.defer = []
        try:
            fn()
        finally:
            lst, self.defer = self.defer, None
        return lst

    def replay(self, item):
        kind, a, kw = item
        if kind == "op":
            self.op(*a)
        else:
            self.dma(*a, **kw)

    def replay_merged(self, streams):
        pos = [0] * len(streams)
        tot = [max(len(x), 1) for x in streams]
        while True:
            best, bi = None, -1
            for i, x in enumerate(streams):
                if pos[i] < len(x):
                    f = pos[i] / tot[i]
                    if best is None or f < best:
                        best, bi = f, i
            if bi < 0:
                break
            self.replay(streams[bi][pos[bi]])
            pos[bi] += 1

    def drain_dmas(self, eng="sp"):
        for name, sem in self.chan_sem.items():
            v = self.chan_cnt[name]
            if v and self.seen.get((eng, id(sem)), 0) < v:
                self.seen[(eng, id(sem))] = v
                self.prog[eng].append(lambda e, s=sem, v=v: e.wait_ge(s, v))

    def flush(self):
        self.drain_dmas("sp")
        nc = self.nc
        prog = self.prog
        with nc.Block() as block:
            @block.tensor
            def _(e):
                for f in prog["pe"]:
                    f(e)

            @block.scalar
            def _(e):
                for f in prog["act"]:
                    f(e)

            @block.vector
            def _(e):
                for f in prog["dve"]:
                    f(e)

            @block.gpsimd
            def _(e):
                for f in prog["pool"]:
                    f(e)

            @block.sync
            def _(e):
                for f in prog["sp"]:
                    f(e)
        self.prog = {e: [] for e in self.ENGS}
        self.lastw = {}
        self.readers = {}


class Cfg:
    def __init__(self, n_long=32, tile_blocks=6, debug=False, stop_after=None, only=None, scratch_in=()):
        self.only = only
        self.mix_stop = None
        self.scratch_in = tuple(scratch_in)
        self.n_long = n_long
        self.nchunks = 3 + n_long
        self.nblocks = (self.nchunks + 1) // 2
        assert self.nblocks % tile_blocks == 0
        self.tile_blocks = tile_blocks
        self.ntiles = self.nblocks // tile_blocks
        self.R = self.nblocks * 128
        self.debug = debug
        self.stop_after = stop_after
        self.seq_end = [64, 128, self.nchunks * 64]


def _sbuf(nc, st, prefix):
    def sb(n, shape, dt=F32):
        return st.enter_context(nc.sbuf_tensor(f"{prefix}_{n}", shape, dt))
    return sb


def make_ident(T, ident, n=128):
    T.op("pool", lambda e: e.memset(ident[:], 0.0), writes=[ident.name])
    T.op("pool", lambda e: e.affine_select(out=ident[:], in_=ident[:], pattern=[[-1, n]], compare_op=ALU.not_equal,
                                           fill=1.0, base=0, channel_multiplier=1),
         reads=[ident.name], writes=[ident.name])


def phase_ln(nc, T, cfg, pb, src, dst_x, dst_xT, g_d, b_d, name):
    nb = cfg.nblocks
    GB = 3
    norm = g_d is not None
    with ExitStack() as st:
        sb = _sbuf(nc, st, name)
        NBUF = 3
        xin = [sb(f"xin{i}", [128, D]) for i in range(NBUF)]
        if norm:
            gbc = sb("gbc", [128, D])
            bbc = sb("bbc", [128, D])
            T.dma("sp", gbc.name, gbc[:], g_d.partition_broadcast(128), writes=[gbc.name])
            T.dma("sp", bbc.name, bbc[:], b_d.partition_broadcast(128), writes=[bbc.name])
            yv = [sb(f"y{i}", [128, D]) for i in range(NBUF)]
            stats = [sb(f"stats{i}", [128, 4, 6]) for i in range(NBUF)]
            mv = [sb(f"mv{i}", [128, 2]) for i in range(NBUF)]
            rstd = [sb(f"rstd{i}", [128, 1]) for i in range(NBUF)]
        if dst_xT is not None:
            ident = sb("ident", [128, 128])
            make_ident(T, ident)
            xTb = [sb(f"xTb{i}", [128, NKC, GB * 128], BF16) for i in range(2)]
            xTv = dst_xT.rearrange("(kc p) r -> p kc r", p=128)
        def head(b):
            s = b % NBUF
            x = xin[s]
            T.dma("sp", x.name, x[:], src[b * 128:(b + 1) * 128, :], writes=[x.name])
            y = x
            if norm:
                y = yv[s]
                for i in range(4):
                    T.op("dve", lambda e, i=i, x=x, s=s: e.bn_stats(out=stats[s][:, i, :], in_=x[:, i * 512:(i + 1) * 512]),
                         reads=[x.name], writes=[(stats[s].name, i)])
                T.op("dve", lambda e, s=s: e.bn_aggr(out=mv[s][:], in_=stats[s][:]),
                     reads=[(stats[s].name, i) for i in range(4)], writes=[mv[s].name])
                T.op("act", lambda e, s=s: e.activation(out=rstd[s][:], in_=mv[s][:, 1:2], func=AF.Sqrt, bias=LN_EPS),
                     reads=[mv[s].name], writes=[rstd[s].name])
                T.op("dve", lambda e, s=s: e.reciprocal(out=rstd[s][:], in_=rstd[s][:]),
                     reads=[rstd[s].name], writes=[rstd[s].name])
                T.op("dve", lambda e, s=s, x=x, y=y: e.tensor_scalar(out=y[:], in0=x[:], scalar1=mv[s][:, 0:1], scalar2=rstd[s][:, 0:1],
                                                                     op0=ALU.subtract, op1=ALU.mult),
                     reads=[x.name, mv[s].name, rstd[s].name], writes=[y.name])
                T.op("dve", lambda e, y=y: e.tensor_tensor(out=y[:], in0=y[:], in1=gbc[:], op=ALU.mult),
                     reads=[y.name, gbc.name], writes=[y.name])
                T.op("pool", lambda e, y=y: e.tensor_tensor(out=y[:], in0=y[:], in1=bbc[:], op=ALU.add),
                     reads=[y.name, bbc.name], writes=[y.name])
            return None

        def tail(b):
            s = b % NBUF
            y = yv[s] if norm else xin[s]
            if dst_x is not None:
                T.dma("act", y.name + "_st", dst_x[b * 128:(b + 1) * 128, :], y[:], reads=[y.name])
            if dst_xT is not None:
                g, gi = divmod(b, GB)
                gs = g % 2
                for q in range(4):
                    bank = pb[(b % 2) * 4 + q]
                    T.op("pe", [lambda e, kc=4 * q + j, j=j, bank=bank, y=y: e.transpose(
                        out=bank[:, j * 128:(j + 1) * 128], in_=y[:, kc * 128:(kc + 1) * 128], identity=ident[:]) for j in range(4)],
                        reads=[y.name, ident.name], writes=[bank.name])
                    eng = "act" if q % 2 == 0 else "dve"
                    outap = xTb[gs][:, 4 * q:4 * q + 4, gi * 128:(gi + 1) * 128]
                    inap = bank[:, :].rearrange("p (j t) -> p j t", j=4)
                    if eng == "act":
                        T.op("act", lambda e, o=outap, i=inap: e.copy(out=o, in_=i), reads=[bank.name], writes=[(xTb[gs].name, gi, q)])
                    else:
                        T.op("dve", lambda e, o=outap, i=inap: e.tensor_copy(out=o, in_=i), reads=[bank.name], writes=[(xTb[gs].name, gi, q)])
                if gi == GB - 1 or b == nb - 1:
                    nbk = gi + 1
                    r0 = g * GB * 128
                    keys = [(xTb[gs].name, a, q) for a in range(nbk) for q in range(4)]
                    T.dma("act", xTb[gs].name + "_st", xTv[:, :, r0:r0 + nbk * 128], xTb[gs][:, :, 0:nbk * 128], reads=keys)

        for b in range(nb):
            head(b)
            if b > 0:
                tail(b - 1)
        tail(nb - 1)
        T.flush()


def phase_ffn(nc, T, cfg, pb, xT_d, xres_d, wgu_d, wdn_d, dst_pre, name, wdn_cache=None):
    TB = cfg.tile_blocks
    TT = TB * 128
    npc = (TT + 511) // 512
    pw = TT // npc
    with ExitStack() as st:
        sb = _sbuf(nc, st, name)
        xT = sb("xT", [128, NKC, TT], BF16)
        hT = sb("hT", [128, NFC, TT], BF16)
        wgu = [[sb(f"wgu{i}{p}", [128, NKC, 512], BF16) for p in "gu"] for i in range(2)]
        wdn = [sb(f"wdn{i}", [128, 4, 512], BF16) for i in range(3)]
        sg = [sb(f"sg{i}", [128, pw]) for i in range(4)]
        xr = sb("xr", [128, TB, 512])
        so = sb("so", [128, TB, 512])
        xTv = xT_d.rearrange("(kc p) r -> p kc r", p=128)
        wguv = wgu_d.rearrange("(kc p) n -> p kc n", p=128)
        q = 0
        wq = 0
        xq = 0
        for t in range(cfg.ntiles):
            r0 = t * TT
            T.dma("pool" if xT_d.dtype == F32 else "sp", xT.name, xT[:], xTv[:, :, r0:r0 + TT], writes=[xT.name])
            for jg in range(NFC // 4):
                s = jg % 2
                for gu in range(2):
                    c0 = gu * DFF + jg * 512
                    w = wgu[s][gu]
                    T.dma("pool", w.name, w[:], wguv[:, :, c0:c0 + 512], writes=[w.name])
                for jj in range(4):
                    j = jg * 4 + jj
                    for pc in range(npc):
                        bg = pb[(2 * q) % 8]
                        bu = pb[(2 * q + 1) % 8]
                        sgt = sg[q % 4]
                        q += 1
                        for bank, w in ((bg, wgu[s][0]), (bu, wgu[s][1])):
                            T.op("pe", [lambda e, kc=kc, bank=bank, w=w, jj=jj, pc=pc: e.matmul(
                                bank[:, 0:pw], lhsT=w[:, kc, jj * 128:(jj + 1) * 128], rhs=xT[:, kc, pc * pw:(pc + 1) * pw],
                                start=(kc == 0), stop=(kc == NKC - 1)) for kc in range(NKC)],
                                reads=[w.name, xT.name], writes=[bank.name])
                        T.op("act", lambda e, bg=bg, sgt=sgt: e.activation(out=sgt[:], in_=bg[:, 0:pw], func=AF.Silu),
                             reads=[bg.name], writes=[sgt.name])
                        T.op("dve", lambda e, bu=bu, sgt=sgt, j=j, pc=pc: e.tensor_tensor(
                            out=hT[:, j, pc * pw:(pc + 1) * pw], in0=sgt[:], in1=bu[:, 0:pw], op=ALU.mult),
                            reads=[sgt.name, bu.name], writes=[(hT.name, j, pc)])
            for cb in range(4):
                T.dma("sp", xr.name, xr[:], xres_d[r0:r0 + TT, cb * 512:(cb + 1) * 512].rearrange("(b p) n -> p b n", p=128), writes=[xr.name])
                T.op("act", lambda e: e.mul(out=xr[:], in_=xr[:], mul=ALPHA), reads=[xr.name], writes=[xr.name])
                for j4 in range(NFC // 4):
                    w = wdn[wq % 3]
                    wq += 1
                    jr = slice(j4 * 512, (j4 + 1) * 512)
                    cs_ = slice(cb * 512, (cb + 1) * 512)
                    if wdn_cache is not None and t > 0:
                        T.dma("pool", w.name, w[:], wdn_cache[jr, cs_].rearrange("(a p) n -> p a n", p=128),
                              reads=[("wdnc", cb, j4)], writes=[w.name])
                    else:
                        T.dma("pool", w.name, w[:], wdn_d[jr, cs_].rearrange("(a p) n -> p a n", p=128), writes=[w.name])
                        if wdn_cache is not None and cfg.ntiles > 1:
                            T.dma("sp", w.name + "_wb", wdn_cache[jr, cs_].rearrange("(a p) n -> p a n", p=128), w[:],
                                  reads=[w.name], writes=[("wdnc", cb, j4)])
                    first, last = (j4 == 0), (j4 == NFC // 4 - 1)
                    wr = [pb[b].name for b in range(TB)] if (first or last) else []
                    T.op("pe", [lambda e, b=b, j=j4 * 4 + jj, jj=jj, w=w: e.matmul(
                        pb[b][:, :], lhsT=hT[:, j, b * 128:(b + 1) * 128], rhs=w[:, jj, :], start=(j == 0), stop=(j == NFC - 1))
                        for jj in range(4) for b in range(TB)],
                        reads=[w.name] + [(hT.name, j4 * 4 + jj, pc) for jj in range(4) for pc in range(npc)], writes=wr)
                for b in range(TB):
                    T.op("dve", lambda e, b=b: e.scalar_tensor_tensor(
                        out=so[:, b, :], in0=pb[b][:, :], scalar=0.5, in1=xr[:, b, :], op0=ALU.mult, op1=ALU.add),
                        reads=[pb[b].name, xr.name], writes=[(so.name, b)])
                T.dma("sp", so.name, dst_pre[r0:r0 + TT, cb * 512:(cb + 1) * 512].rearrange("(b p) n -> p b n", p=128), so[:],
                      reads=[(so.name, b) for b in range(TB)])
        T.flush()


def phase_win(nc, T, cfg, pb, S, W, O, name, I_hist=None):
    R = cfg.R
    nb = cfg.nblocks
    with ExitStack() as st:
        sb = _sbuf(nc, st, name)
        x1T = sb("x1T", [128, NKC, R], BF16)
        wsl = [sb(f"w{i}", [128, NKC, 512], BF16) for i in range(2)]
        stg = [sb(f"stg{i}", [128, 512]) for i in range(4)]
        fstg = [sb(f"fstg{i}", [128, 3 + R]) for i in range(2)]
        cacc = [sb(f"cacc{i}", [128, R]) for i in range(2)]
        cw = sb("cw", [128, 24, 4])
        cb = sb("cb", [128, 24])
        hrow = sb("hrow", [9, CONV_DIM])
        histT = sb("histT", [128, 24, 9])
        ident = sb("ident", [128, 128])
        make_ident(T, ident)
        T.dma("sp", cw.name, cw[:], W["conv_w"].rearrange("p (c i) -> p c i", i=4), writes=[cw.name])
        T.dma("sp", cb.name, cb[:], W["conv_b"][:, :], writes=[cb.name])
        T.dma("sp", hrow.name, hrow[:], I_hist[:, :], writes=[hrow.name])
        for q4 in range(6):
            bk = pb[q4 % 8]
            T.op("pe", [lambda e, j=j, bk=bk, q4=q4: e.transpose(out=bk[:, j * 9:(j + 1) * 9], in_=hrow[:, (q4 * 4 + j) * 128:(q4 * 4 + j + 1) * 128],
                                                          identity=ident[0:9, 0:9]) for j in range(4)], reads=[hrow.name, ident.name], writes=[bk.name])
            T.op("dve", lambda e, bk=bk, q4=q4: e.tensor_copy(out=histT[:, q4 * 4:(q4 + 1) * 4, :], in_=bk[:, 0:36].rearrange("p (j s) -> p j s", j=4)),
                 reads=[bk.name], writes=[histT.name])
        for f_ in fstg:
            T.op("pool", lambda e, f_=f_: e.memset(f_[:], 0.0), writes=[(f_.name, pc) for pc in range(R // 384)])
        tl = [sb(f"tl{i}", [3, 512]) for i in range(2)]
        zt = sb("zt", [128, RW_COLS])
        x1Tv = S["X1T"].rearrange("(kc p) r -> p kc r", p=128)
        wv = W["w_in"].rearrange("(kc p) n -> p kc n", p=128)
        TT = cfg.tile_blocks * 128
        for t in range(cfg.ntiles):
            T.dma("sp", f"{name}_x1T{t % 2}", x1T[:, :, t * TT:(t + 1) * TT], x1Tv[:, :, t * TT:(t + 1) * TT], writes=[(x1T.name, t)])
        x1keys = [(x1T.name, t) for t in range(cfg.ntiles)]
        T.op("pool", lambda e: e.memset(zt[:], 0.0), writes=[zt.name])
        T.dma("sp", f"{name}_z0", S["P_RW"][0:1, :], zt[0:1, :], reads=[zt.name])
        T.dma("sp", f"{name}_z1", S["PT_XBC"].rearrange("(c p) r -> p c r", p=128)[:, :, 0:4],
              zt[:, 0:96].rearrange("p (c r) -> p c r", r=4), reads=[zt.name])
        groups = []
        for c0 in range(0, RW_COLS, 512):
            groups.append((c0, min(512, RW_COLS - c0), S["P_RW"], c0, 1))
        for c0 in range(0, D, 512):
            groups.append((RW_COLS + c0, 512, S["P_Z"], c0, 0))
        groups.append((8640, SSM_H, S["P_DT"], 0, 0))
        q = 0
        wq = 0
        sq = 0
        for (c0, n, dst, dc0, roff) in groups:
            w = wsl[wq % 2]
            wq += 1
            T.dma("pool", w.name, w[:, :, 0:n], wv[:, :, c0:c0 + n], writes=[w.name])
            for b in range(nb):
                bank = pb[q % 8]
                q += 1
                T.op("pe", [lambda e, kc=kc, bank=bank, w=w, b=b, n=n: e.matmul(
                    bank[:, 0:n], lhsT=x1T[:, kc, b * 128:(b + 1) * 128], rhs=w[:, kc, 0:n],
                    start=(kc == 0), stop=(kc == NKC - 1)) for kc in range(NKC)],
                    reads=[w.name] + x1keys, writes=[bank.name])
                sg = stg[sq % 4]
                if sq % 2 == 0:
                    T.op("act", lambda e, sg=sg, bank=bank, n=n: e.copy(out=sg[:, 0:n], in_=bank[:, 0:n]), reads=[bank.name], writes=[sg.name])
                else:
                    T.op("dve", lambda e, sg=sg, bank=bank, n=n: e.tensor_copy(out=sg[:, 0:n], in_=bank[:, 0:n]), reads=[bank.name], writes=[sg.name])
                sq += 1
                T.dma("sp", sg.name, dst[roff + b * 128:roff + (b + 1) * 128, dc0:dc0 + n], sg[:, 0:n], reads=[sg.name])
        PW = 384
        npc = R // PW
        fq = 0
        tq = 0
        for cg in range(6):
            c0 = 5568 + cg * 512
            w = wsl[wq % 2]
            wq += 1
            T.dma("pool", w.name, w[:], wv[:, :, c0:c0 + 512], writes=[w.name])
            for cc in range(4):
                fs = fstg[fq % 2]
                fq += 1
                for pc in range(npc):
                    bank = pb[q % 8]
                    q += 1
                    T.op("pe", [lambda e, kc=kc, bank=bank, w=w, cc=cc, pc=pc: e.matmul(
                        bank[:, 0:PW], lhsT=w[:, kc, cc * 128:(cc + 1) * 128], rhs=x1T[:, kc, pc * PW:(pc + 1) * PW],
                        start=(kc == 0), stop=(kc == NKC - 1)) for kc in range(NKC)],
                        reads=[w.name] + x1keys, writes=[bank.name])
                    T.op("act", lambda e, fs=fs, bank=bank, pc=pc: e.copy(out=fs[:, 3 + pc * PW:3 + (pc + 1) * PW], in_=bank[:, 0:PW]),
                         reads=[bank.name], writes=[(fs.name, pc)])
                    sq += 1
                ch = cg * 4 + cc
                for seq in range(3):
                    c0_ = 3 + seq * 64 + 45
                    T.op("pool", lambda e, fs=fs, ch=ch, seq=seq, c0_=c0_: e.tensor_copy(out=fs[:, c0_:c0_ + 3], in_=histT[:, ch, 3 * seq:3 * seq + 3]),
                         reads=[histT.name], writes=[(fs.name, 0)])
                ca = cacc[(fq - 1) % 2]
                fkeys = [(fs.name, pc) for pc in range(npc)]
                T.op("dve", lambda e, fs=fs, ca=ca, ch=ch: e.tensor_scalar(out=ca[:], in0=fs[:, 0:R], scalar1=cw[:, ch, 0:1], scalar2=None, op0=ALU.mult),
                     reads=fkeys + [cw.name], writes=[ca.name])
                for i_ in range(1, 4):
                    T.op("dve", lambda e, fs=fs, ca=ca, ch=ch, i_=i_: e.scalar_tensor_tensor(
                        out=ca[:], in0=fs[:, i_:i_ + R], scalar=cw[:, ch, i_:i_ + 1], in1=ca[:], op0=ALU.mult, op1=ALU.add),
                        reads=fkeys + [cw.name, ca.name], writes=[ca.name])
                T.op("act", lambda e, ca=ca, ch=ch: e.activation(out=ca[:], in_=ca[:], func=AF.Silu, bias=cb[:, ch:ch + 1]),
                     reads=[ca.name, cb.name], writes=[ca.name])
                T.dma("sp", ca.name, S["PT_XBC"][ch * 128:(ch + 1) * 128, 4:4 + R], ca[:], reads=[ca.name])
            for si, rend in enumerate(cfg.seq_end):
                bank = pb[q % 8]
                q += 1
                T.op("pe", [lambda e, kc=kc, bank=bank, w=w, rend=rend: e.matmul(
                    bank[0:3, :], lhsT=x1T[:, kc, rend - 3:rend], rhs=w[:, kc, :],
                    start=(kc == 0), stop=(kc == NKC - 1)) for kc in range(NKC)],
                    reads=[w.name] + x1keys, writes=[bank.name])
                tt = tl[tq % 2]
                tq += 1
                T.op("dve", lambda e, tt=tt, bank=bank: e.tensor_copy(out=tt[:], in_=bank[0:3, :]), reads=[bank.name], writes=[tt.name])
                T.dma("sp", tt.name, O["o_conv"][si * 3:(si + 1) * 3, cg * 512:(cg + 1) * 512], tt[:], reads=[tt.name])
        T.flush()


class _Stop(Exception):
    pass


def phase_mix(nc, T, cfg, pb, S, W, I, O, name):
    try:
        _phase_mix(nc, T, cfg, pb, S, W, I, O, name)
    except _Stop:
        pass


def _phase_mix(nc, T, cfg, pb, S, W, I, O, name):
    R = cfg.R

    def chk(n):
        if cfg.mix_stop == n:
            T.flush()
            raise _Stop()
    NH = 4
    HW = NH * 64
    NSTR = 2
    with ExitStack() as st:
        sb = _sbuf(nc, st, name)
        bq = [0]
        bpool = [list(range(8))]

        def bank():
            p = bpool[0]
            b = pb[p[bq[0] % len(p)]]
            bq[0] += 1
            return b

        kpref = [""]
        LOCALK = set(["ew", "av", "kkn", "kp", "Ep", "Em", "Wp", "rt", "kt", "bt", "at", "gg", "tA", "tB", "Us", "Ys", "yo", "vb", "btb", "ktb",
                      "n2", "rn", "s1", "s2", "mean", "var", "bs"])

        class _TP:
            @staticmethod
            def _m(keys):
                return [kpref[0] + k if (isinstance(k, str) and k in LOCALK) else k for k in keys]

            @staticmethod
            def op(eng, fns, reads=(), writes=()):
                T.op(eng, fns, reads=_TP._m(reads), writes=_TP._m(writes))
        TP = _TP

        def TT(eng, out, in0, in1, op, r, w):
            TP.op(eng, lambda e: e.tensor_tensor(out=out, in0=in0, in1=in1, op=op), reads=r, writes=w)

        def TS(eng, out, in0, s1, s2, op0, op1, r, w):
            if s2 is None:
                TP.op(eng, lambda e: e.tensor_scalar(out=out, in0=in0, scalar1=s1, scalar2=None, op0=op0), reads=r, writes=w)
            else:
                TP.op(eng, lambda e: e.tensor_scalar(out=out, in0=in0, scalar1=s1, scalar2=s2, op0=op0, op1=op1), reads=r, writes=w)

        def STT(out, in0, sc, in1, op0, op1, r, w):
            TP.op("dve", lambda e: e.scalar_tensor_tensor(out=out, in0=in0, scalar=sc, in1=in1, op0=op0, op1=op1), reads=r, writes=w)

        def ACT(out, in_, func, r, w, bias=0.0, scale=1.0):
            TP.op("act", lambda e: e.activation(out=out, in_=in_, func=func, bias=bias, scale=scale), reads=r, writes=w)

        def CP(eng, out, in_, r, w):
            if eng == "act":
                TP.op("act", lambda e: e.copy(out=out, in_=in_), reads=r, writes=w)
            else:
                TP.op(eng, lambda e: e.tensor_copy(out=out, in_=in_), reads=r, writes=w)

        def RED(out, in_, r, w):
            TP.op("dve", lambda e: e.tensor_reduce(out=out, in_=in_, axis=AX.X, op=ALU.add), reads=r, writes=w)

        def RECIP(out, in_, r, w):
            TP.op("dve", lambda e: e.reciprocal(out=out, in_=in_), reads=r, writes=w)

        def MM(specs, r, w):
            TP.op("pe", [lambda e, sp=sp: e.matmul(sp[0], lhsT=sp[1], rhs=sp[2], start=sp[3], stop=sp[4]) for sp in specs], reads=r, writes=w)

        def TR(specs, r, w):
            TP.op("pe", [lambda e, sp=sp: e.transpose(out=sp[0], in_=sp[1], identity=sp[2]) for sp in specs], reads=r, writes=w)

        def LD(t, out, in_, w, r=()):
            T.dma("sp", t, out, in_, reads=r, writes=w)

        ident = sb("ident", [128, 128])
        make_ident(T, ident)
        tri = sb("tri", [64, 64])
        msl = sb("msl", [64, 2, 64])
        mgt = sb("mgt", [64, 64])
        ones = sb("ones", [64, 128])
        rowmask = sb("rowmask", [64, 1])

        def sel(t, ap, pattern, base, cm):
            T.op("pool", lambda e: e.memset(ap, 1.0), writes=[t.name])
            T.op("pool", lambda e: e.affine_select(out=ap, in_=ap, pattern=pattern, compare_op=ALU.is_ge, fill=0.0,
                                                   base=base, channel_multiplier=cm), reads=[t.name], writes=[t.name])
        sel(tri, tri[:], [[1, 64]], 0, -1)
        sel(msl, msl[:, 0, :], [[1, 64]], -1, -1)
        sel(msl, msl[:, 1, :], [[1, 64]], 0, -1)
        sel(mgt, mgt[:], [[-1, 64]], -1, 1)
        sel(rowmask, rowmask[:], [[0, 1]], -48, 1)
        T.op("pool", lambda e: e.memset(ones[:], 1.0), writes=[ones.name])

        T.begin_group("mix_setup")

        def bc_load(n, src, cols):
            t = sb(n, [64, cols])
            LD(t.name, t[:], src.partition_broadcast(64), [t.name])
            return t
        mu_bc = bc_load("mu_bc", W["rw_mu"], RW_COLS)
        w0_bc = bc_load("w0_bc", W["rw_w0"], RW_DIM)
        a0_bc = bc_load("a0_bc", W["rw_a0"], RW_DIM)
        kk_bc = bc_load("kk_bc", W["rw_kk"], RW_DIM)
        ka_bc = bc_load("ka_bc", W["rw_ka"], RW_DIM)
        rk_bc = bc_load("rk_bc", W["rw_rk"], RW_DIM)
        dtb_bc = bc_load("dtb_bc", W["dt_bias"], SSM_H)
        aneg_bc = bc_load("aneg_bc", W["a_log"], SSM_H)
        dsk_bc = bc_load("dsk_bc", W["d_skip"], SSM_H)
        gnw_bc = bc_load("gnw_bc", W["rw_gn_w"], RW_DIM)
        gnb_bc = bc_load("gnb_bc", W["rw_gn_b"], RW_DIM)
        w2 = sb("w2", [128, RW_DIM])
        a2 = sb("a2", [128, RW_DIM], BF16)
        g2 = sb("g2", [128, 2, RW_DIM], BF16)
        T.op("pool", lambda e: e.memset(w2[:], 0.0), writes=[w2.name])
        T.op("pool", lambda e: e.memset(a2[:], 0.0), writes=[a2.name])
        LD(w2.name, w2[0:96, :], W["rw_w2"][:, :], [w2.name])
        T.dma("pool", a2.name, a2[0:96, :], W["rw_a2"][:, :], writes=[a2.name])
        T.dma("pool", g2.name, g2[:], W["rw_g2"].rearrange("(c p) n -> p c n", p=128), writes=[g2.name])
        cw = sb("cw", [128, 24, 4])
        cb = sb("cb", [128, 24])
        snw = sb("snw", [128, 16])
        LD(cw.name, cw[:], W["conv_w"].rearrange("p (c i) -> p c i", i=4), [cw.name])
        LD(cb.name, cb[:], W["conv_b"][:, :], [cb.name])
        LD(snw.name, snw[:], W["ssm_norm_w"][:, :], [snw.name])
        T.end_group()
        ACT(aneg_bc[:], aneg_bc[:], AF.Exp, [aneg_bc.name], [aneg_bc.name])
        T.op("act", lambda e: e.mul(out=aneg_bc[:], in_=aneg_bc[:], mul=-1.0), reads=[aneg_bc.name], writes=[aneg_bc.name])
        XAB = sb("XAB", [128, 2, 24, 64])
        XA = XAB[:, 0]
        XB2 = XAB[:, 1]
        XAk, XBk = "XA", "XB2"
        histT = sb("histT", [128, 24, 9])
        hrow = XAB[0:9, 0].rearrange("p c t -> p (c t)")
        for half in range(2):
            LD("mix_hrow", hrow, I["hist_conv"][:, half * 1536:(half + 1) * 1536], [XAk])
            for q4 in range(3):
                bk = bank()
                TR([(bk[:, j * 9:(j + 1) * 9], hrow[:, (q4 * 4 + j) * 128:(q4 * 4 + j + 1) * 128], ident[0:9, 0:9]) for j in range(4)],
                   [XAk, ident.name], [bk.name])
                c0_ = half * 12 + q4 * 4
                CP("dve", histT[:, c0_:c0_ + 4, :], bk[:, 0:36].rearrange("p (j s) -> p j s", j=4), [bk.name], [histT.name])

        chk(1)
        ST = sb("ST", [64, RW_H, 64])
        STb = sb("STb", [64, RW_H, 64], BF16)
        HT = sb("HT", [128, SSM_H * 64])
        stg_ws = [sb(f"stg_w{i}", [64, 8, 64]) for i in range(NSTR)]
        stg_h = XAB[:].rearrange("p a c t -> p (a c t)")[:, 0:2048].rearrange("p (c n) -> p c n", c=16)
        SHk = [XAk, XBk]

        def load_states_rw(seq, sid):
            stg_w = stg_ws[sid]
            r0_ = seq * 1024 + sid * 512
            LD(stg_w.name, stg_w[:], I["wkv0"][r0_:r0_ + 512, :].rearrange("(h v) k -> v h k", v=64), [stg_w.name])
            bk = bank()
            TR([(bk[0:64, j * 64:(j + 1) * 64], stg_w[:, j, :], ident[0:64, 0:64]) for j in range(8)],
               [stg_w.name, ident.name], [bk.name])
            CP("act", ST[:, sid * 8:(sid + 1) * 8, :], bk[0:64, :].rearrange("p (h v) -> p h v", h=8), [bk.name],
               [(ST.name, 2 * sid), (ST.name, 2 * sid + 1)])
            CP("dve", STb[:, sid * 8:(sid + 1) * 8, :], bk[0:64, :].rearrange("p (h v) -> p h v", h=8), [bk.name],
               [(STb.name, 2 * sid), (STb.name, 2 * sid + 1)])

        def load_states_ssd(seq):
            LD("mix_stg_h", stg_h, I["ssm0"][seq * 2048:(seq + 1) * 2048, :].rearrange("(c p) n -> p c n", p=128), SHk)
            for g in range(4):
                bk = bank()
                TR([(bk[:, j * 128:(j + 1) * 128], stg_h[:, g * 4 + j, :], ident[:]) for j in range(4)],
                   SHk + [ident.name], [bk.name])
                CP("dve", HT[:, g * 512:(g + 1) * 512], bk[:, :], [bk.name], [(HT.name, g)])

        def store_states_rw(seq, sid):
            stg_w = stg_ws[sid]
            r0_ = seq * 1024 + sid * 512
            bk = bank()
            TR([(bk[0:64, j * 64:(j + 1) * 64], ST[:, sid * 8 + j, :], ident[0:64, 0:64]) for j in range(8)],
               [(ST.name, 2 * sid), (ST.name, 2 * sid + 1), ident.name], [bk.name])
            CP("act", stg_w[:], bk[0:64, :].rearrange("p (h k) -> p h k", h=8), [bk.name], [stg_w.name])
            T.dma("sp", stg_w.name + "_st", O["o_wkv"][r0_:r0_ + 512, :].rearrange("(h v) k -> v h k", v=64), stg_w[:],
                  reads=[stg_w.name])

        def store_states_ssd(seq):
            for g in range(4):
                bk = bank()
                TR([(bk[:, j * 128:(j + 1) * 128], HT[:, (g * 4 + j) * 128:(g * 4 + j + 1) * 128], ident[:]) for j in range(4)],
                   [(HT.name, g), ident.name], [bk.name])
                CP("dve", stg_h[:, g * 4:(g + 1) * 4, :], bk[:, :].rearrange("p (j n) -> p j n", j=4), [bk.name], SHk)
            T.dma("sp", "mix_stg_h_st", O["o_ssm"][seq * 2048:(seq + 1) * 2048, :].rearrange("(c p) n -> p c n", p=128), stg_h,
                  reads=SHk)

        for si, rend in enumerate(cfg.seq_end):
            T.dma("sp", f"{name}_shift", O["o_shift"][si:si + 1, :], S["P_RW"][rend:rend + 1, :])

        identb = sb("identb", [64, 64], BF16)
        CP("pool", identb[:], ident[0:64, 0:64], [ident.name], [identb.name])
        names = ["ew", "av", "kkn", "kp", "Ep", "Em", "Wp", "rt", "kt", "bt", "at", "gg", "tA", "tB", "Us", "Ys", "yo"]

        def mk_rw(i):
            p = f"r{i}_"
            return dict(
                cur_l=sb(p + "cur_l", [64, 448]), prev_l=sb(p + "prev_l", [64, 448]), lT=sb(p + "lT", [128, 64]),
                lTb=sb(p + "lTb", [128, 3, 64], BF16),
                cur3=[sb(p + f"cur3{j}", [64, 3, HW]) for j in range(2)], prev3=[sb(p + f"prev3{j}", [64, 3, HW]) for j in range(2)],
                W_={n: sb(p + n, [64, HW], BF16 if n in ("Us", "vb", "btb", "ktb") else F32) for n in names + ["vb", "btb", "ktb"]},
                ARTb=sb(p + "ARTb", [64, NH, 2, 64], BF16),
                btT=sb(p + "btT", [64, NH, 64], BF16), ktT=sb(p + "ktT", [64, NH, 64], BF16),
                AB=sb(p + "AB", [64, NH, 2, 64], BF16), AK=sb(p + "AK", [64, NH, 2, 64], BF16),
                Pm=[sb(p + f"Pm{j}", [64, NH, 64], BF16) for j in range(2)],
                Qm=[sb(p + f"Qm{j}", [64, NH, 64], BF16) for j in range(2)],
                XT=sb(p + "XT", [64, NH, 64], BF16), RHSb=sb(p + "RHSb", [64, HW], BF16), WC=sb(p + "WC", [64, NH]),
                sm={n: sb(p + n, [64, NH]) for n in ["n2", "rn", "s1", "s2", "mean", "var", "bs"]},
                yT=[sb(p + f"yT{j}", [128, HW // 128, 64], BF16) for j in range(2)])
        RWB = [mk_rw(i) for i in range(NSTR)]
        Btm = sb("Btm", [64, 512], BF16)
        pdt = sb("pdt", [64, SSM_H])
        dts = {n: sb(n, [64, SSM_H]) for n in ["dt", "adt", "acs", "eacs", "toend"]}
        cdec = sb("cdec", [128, SSM_H])
        gn = ["xs", "zz", "yy", "xdt", "xdtw", "ML", "EX", "MT", "t1"]
        G_ = {n: sb("g_" + n, [64, 512], BF16 if n in ("xdt", "xdtw", "MT") else F32) for n in gn}
        cbm = sb("cbm", [64, 64])
        ss1 = sb("ss1", [64, 1])
        yT2s = [sb(f"yT2{j}", [128, 4, 64], BF16) for j in range(2)]
        ZZ = [G_["zz"], sb("g_zz1", [64, 512])]
        prw3 = S["P_RW"][:, 0:3072].rearrange("t (j c) -> t j c", j=3)
        hrw3 = I["hist_rw"][:, 0:3072].rearrange("t (j c) -> t j c", j=3)
        yrwT = S["YRWT"].rearrange("(c p) r -> p c r", p=128)
        yssT = S["YSSMT"].rearrange("(c p) r -> p c r", p=128)
        xbcv = S["PT_XBC"].rearrange("(c p) r -> p c r", p=128)

        def h3(ap):
            return ap.rearrange("p (h k) -> p h k", h=NH)

        def h3s(ap):
            return ap.rearrange("p (h k) -> p h k", h=8)

        def bc8s(ap):
            return ap.unsqueeze(2).to_broadcast([64, 8, 64])

        def bc8(ap):
            return ap.unsqueeze(2).to_broadcast([64, NH, 64])

        def rwkv_chunk(c, sid):
            Bf = RWB[sid]
            cur_l, prev_l, lT, W_ = Bf["cur_l"], Bf["prev_l"], Bf["lT"], Bf["W_"]
            lTb = Bf["lTb"]
            btT, ktT, AB, AK, Pm, Qm = Bf["btT"], Bf["ktT"], Bf["AB"], Bf["AK"], Bf["Pm"], Bf["Qm"]
            ARTb = Bf["ARTb"]
            XT, RHSb, WC, sm = Bf["XT"], Bf["RHSb"], Bf["WC"], Bf["sm"]
            t0 = c * 64
            seq = min(c, 2)
            short = c < 3
            if c < 3:
                load_states_rw(seq, sid)
            LD(cur_l.name, cur_l[:], S["P_RW"][1 + t0:1 + t0 + 64, 3072:3520], [cur_l.name])
            LD(prev_l.name, prev_l[:], S["P_RW"][t0:t0 + 64, 3072:3520], [prev_l.name])
            if short:
                LD(prev_l.name, prev_l[48:49, :], I["hist_rw"][seq:seq + 1, 3072:3520], [prev_l.name])
            TT("dve", prev_l[:], prev_l[:], cur_l[:], ALU.subtract, [prev_l.name, cur_l.name], [prev_l.name])
            TT("pool", prev_l[:], prev_l[:], mu_bc[:, 3072:3520], ALU.mult, [prev_l.name, mu_bc.name], [prev_l.name])
            TT("dve", cur_l[:], cur_l[:], prev_l[:], ALU.add, [prev_l.name, cur_l.name], [cur_l.name])
            ACT(cur_l[:, 0:96], cur_l[:, 0:96], AF.Tanh, [cur_l.name], [cur_l.name])
            ACT(cur_l[:, 192:448], cur_l[:, 192:448], AF.Sigmoid, [cur_l.name], [cur_l.name])
            bk = bank()
            TR([(bk[:, j * 64:(j + 1) * 64], cur_l[:, o_:o_ + 128], ident[0:64, 0:64]) for j, o_ in enumerate((0, 96, 192, 320))],
               [cur_l.name, ident.name], [bk.name])
            CP("act", lT[:], bk[:, 0:64], [bk.name], [lT.name])
            CP("dve", lTb[:], bk[:, 64:256].rearrange("p (a t) -> p a t", a=3), [bk.name], [lTb.name])
            for qq in range(2):
                hh = 2 * sid + qq
                cur3, prev3, yT = Bf["cur3"][qq], Bf["prev3"][qq], Bf["yT"][qq]
                cs = hh * HW
                h0 = hh * NH
                w = W_
                LD(cur3.name, cur3[:], prw3[1 + t0:1 + t0 + 64, :, cs:cs + HW], [cur3.name])
                LD(prev3.name, prev3[:], prw3[t0:t0 + 64, :, cs:cs + HW], [prev3.name])
                if short:
                    LD(prev3.name, prev3[48:49, :, :], hrw3[seq:seq + 1, :, cs:cs + HW], [prev3.name])
                mu3 = mu_bc[:, 0:3072].rearrange("p (j c) -> p j c", j=3)[:, :, cs:cs + HW]
                TT("dve", prev3[:], prev3[:], cur3[:], ALU.subtract, [prev3.name, cur3.name], [prev3.name])
                TT("pool", prev3[:], prev3[:], mu3, ALU.mult, [prev3.name, mu_bc.name], [prev3.name])
                TT("dve", cur3[:], cur3[:], prev3[:], ALU.add, [prev3.name, cur3.name], [cur3.name])
                r_, k_, v_ = cur3[:, 0, :], cur3[:, 1, :], cur3[:, 2, :]
                c3 = [cur3.name]
                b_lw, b_la, b_lg = bank(), bank(), bank()
                MM([(b_lw[0:64, 0:HW], lT[:], w2[:, cs:cs + HW], True, True)], [lT.name, w2.name], [b_lw.name])
                MM([(b_la[0:64, 0:HW], lTb[:, 0, :], a2[:, cs:cs + HW], True, True)], [lTb.name, a2.name], [b_la.name])
                MM([(b_lg[0:64, 0:HW], lTb[:, 1, :], g2[:, 0, cs:cs + HW], True, False),
                    (b_lg[0:64, 0:HW], lTb[:, 2, :], g2[:, 1, cs:cs + HW], False, True)], [lTb.name, g2.name], [b_lg.name])
                CP("act", w["gg"][:], b_lg[0:64, 0:HW], [b_lg.name], ["gg"])
                TT("dve", w["tA"][:], b_lw[0:64, 0:HW], w0_bc[:, cs:cs + HW], ALU.add, [b_lw.name, w0_bc.name], ["tA"])
                ACT(w["tA"][:], w["tA"][:], AF.Exp, ["tA"], ["tA"], scale=-1.0)
                ACT(w["tA"][:], w["tA"][:], AF.Ln, ["tA"], ["tA"], bias=1.0)
                ACT(w["ew"][:], w["tA"][:], AF.Exp, ["tA"], ["ew"], bias=-0.5, scale=-1.0)
                if short:
                    TS("dve", w["ew"][:], w["ew"][:], rowmask[:, 0:1], None, ALU.mult, None, ["ew", rowmask.name], ["ew"])
                TT("dve", w["av"][:], b_la[0:64, 0:HW], a0_bc[:, cs:cs + HW], ALU.add, [b_la.name, a0_bc.name], ["av"])
                ACT(w["av"][:], w["av"][:], AF.Sigmoid, ["av"], ["av"])
                TT("pool", w["kkn"][:], k_, kk_bc[:, cs:cs + HW], ALU.mult, c3 + [kk_bc.name], ["kkn"])
                ACT(w["tB"][:], w["kkn"][:], AF.Square, ["kkn"], ["tB"])
                RED(sm["n2"][:], h3(w["tB"][:]), ["tB"], ["n2"])
                TS("dve", sm["n2"][:], sm["n2"][:], 1e-24, None, ALU.max, None, ["n2"], ["n2"])
                ACT(sm["n2"][:], sm["n2"][:], AF.Ln, ["n2"], ["n2"])
                ACT(sm["rn"][:], sm["n2"][:], AF.Exp, ["n2"], ["rn"], scale=-0.5)
                TT("dve", h3(w["kkn"][:]), h3(w["kkn"][:]), bc8(sm["rn"][:]), ALU.mult, ["kkn", "rn"], ["kkn"])
                if short:
                    TS("dve", w["kkn"][:], w["kkn"][:], rowmask[:, 0:1], None, ALU.mult, None, ["kkn", rowmask.name], ["kkn"])
                STT(w["tB"][:], w["av"][:], -1.0, ka_bc[:, cs:cs + HW], ALU.add, ALU.mult, ["av", ka_bc.name], ["tB"])
                STT(w["kp"][:], w["tB"][:], 1.0, k_, ALU.add, ALU.mult, ["tB"] + c3, ["kp"])
                if short:
                    TS("dve", w["kp"][:], w["kp"][:], rowmask[:, 0:1], None, ALU.mult, None, ["kp", rowmask.name], ["kp"])
                b_cn = bank()
                MM([(b_cn[0:64, 0:HW], tri[:], w["ew"][:], True, True)], [tri.name, "ew"], [b_cn.name])
                b_wc = bank()
                MM([(b_wc[0:64, j:j + 1], w["ew"][:, j * 64:(j + 1) * 64], ones[:, 0:1], True, True) for j in range(NH)],
                   ["ew", ones.name], [b_wc.name])
                ACT(WC[:], b_wc[0:64, 0:NH], AF.Exp, [b_wc.name], [WC.name], scale=-1.0)
                ACT(w["Ep"][:], b_cn[0:64, 0:HW], AF.Exp, [b_cn.name], ["Ep"], scale=-1.0)
                ACT(w["Em"][:], b_cn[0:64, 0:HW], AF.Exp, [b_cn.name], ["Em"])
                TT("dve", w["tA"][:], b_cn[0:64, 0:HW], w["ew"][:], ALU.subtract, [b_cn.name, "ew"], ["tA"])
                ACT(w["Wp"][:], w["tA"][:], AF.Exp, ["tA"], ["Wp"], scale=-1.0)
                TT("dve", w["rt"][:], r_, w["Ep"][:], ALU.mult, c3 + ["Ep"], ["rt"])
                TT("pool", w["kt"][:], w["kp"][:], w["Em"][:], ALU.mult, ["kp", "Em"], ["kt"])
                TT("pool", w["bt"][:], w["kkn"][:], w["av"][:], ALU.mult, ["kkn", "av"], ["bt"])
                TT("pool", w["bt"][:], w["bt"][:], w["Em"][:], ALU.mult, ["bt", "Em"], ["bt"])
                STT(w["at"][:], w["kkn"][:], -1.0, w["Wp"][:], ALU.mult, ALU.mult, ["kkn", "Wp"], ["at"])
                i64 = ident[0:64, 0:64]
                for src, dst, dkey, eng in (("at", ARTb[:, :, 0, :], (ARTb.name, 0), "act"), ("rt", ARTb[:, :, 1, :], (ARTb.name, 1), "dve"),
                                            ("bt", btT[:], btT.name, "act"), ("kt", ktT[:], ktT.name, "dve")):
                    bk = bank()
                    TR([(bk[0:64, j * 64:(j + 1) * 64], w[src][:, j * 64:(j + 1) * 64], i64) for j in range(NH)], [src, ident.name], [bk.name])
                    CP(eng, dst, bk[0:64, 0:HW].rearrange("p (h t) -> p h t", h=NH), [bk.name], [dkey])
                ARTbk = [(ARTb.name, 0), (ARTb.name, 1)]
                CP("act", w["vb"][:], v_, c3, ["vb"])
                CP("act", w["btb"][:], w["bt"][:], ["bt"], ["btb"])
                CP("act", w["ktb"][:], w["kt"][:], ["kt"], ["ktb"])
                for lhs, lkey, dstt in ((btT, btT.name, AB), (ktT, ktT.name, AK)):
                    for hb in range(NH // 4):
                        bk = bank()
                        MM([(bk[0:64, j * 128:(j + 1) * 128], lhs[:, hb * 4 + j, :], ARTb[:, hb * 4 + j, :, :].rearrange("p a t -> p (a t)"), True, True)
                            for j in range(4)], [lkey] + ARTbk, [bk.name])
                        TT("dve", dstt[:, hb * 4:(hb + 1) * 4, :, :], bk[0:64, :].rearrange("p (h a t) -> p h a t", h=4, a=2),
                           msl[:].unsqueeze(1).to_broadcast([64, 4, 2, 64]), ALU.mult, [bk.name, msl.name], [(dstt.name, hb)])
                ABk = [(AB.name, i_) for i_ in range(NH // 4)]
                AKk = [(AK.name, i_) for i_ in range(NH // 4)]
                bk = bank()
                MM([(bk[0:64, j * 64:(j + 1) * 64], ARTb[:, j, 0, :], btT[:, j, :], True, True) for j in range(NH)], ARTbk + [btT.name], [bk.name])
                TT("dve", Pm[0][:], bk[0:64, 0:HW].rearrange("p (h s) -> p h s", h=NH), mgt[:].unsqueeze(1).to_broadcast([64, NH, 64]), ALU.mult,
                   [bk.name, mgt.name], [Pm[0].name])
                Q0 = AB[:, :, 0, :]
                TT("pool", XT[:], Q0, identb[:].unsqueeze(1).to_broadcast([64, NH, 64]), ALU.add, ABk + [identb.name], [XT.name])
                pi = 0
                for lvl in range(1, 6):
                    Pc, Qc, Pn, Qn = Pm[pi], Qm[pi], Pm[1 - pi], Qm[1 - pi]
                    Qcv = Q0 if lvl == 1 else Qc[:]
                    Qck = ABk if lvl == 1 else [Qc.name]
                    bp = bank()
                    MM([(bp[0:64, j * 64:(j + 1) * 64], Qcv[:, j, :], Pc[:, j, :], True, True) for j in range(NH)], [Pc.name] + Qck, [bp.name])
                    CP("act", Pn[:], bp[0:64, 0:HW].rearrange("p (h s) -> p h s", h=NH), [bp.name], [Pn.name])
                    if lvl < 5:
                        bq_ = bank()
                        MM([(bq_[0:64, j * 64:(j + 1) * 64], Pc[:, j, :], Qcv[:, j, :], True, True) for j in range(NH)], [Pc.name] + Qck, [bq_.name])
                        CP("dve", Qn[:], bq_[0:64, 0:HW].rearrange("p (h s) -> p h s", h=NH), [bq_.name], [Qn.name])
                    bz = bank()
                    MM([(bz[0:64, j * 64:(j + 1) * 64], Pn[:, j, :], XT[:, j, :], True, True) for j in range(NH)], [Pn.name, XT.name], [bz.name])
                    TT("dve", XT[:], XT[:], bz[0:64, 0:HW].rearrange("p (h s) -> p h s", h=NH), ALU.add, [XT.name, bz.name], [XT.name])
                    pi = 1 - pi
                STk = (ST.name, hh)
                STbk = (STb.name, hh)
                bk = bank()
                sp_ = []
                for j in range(NH):
                    o = bk[0:64, j * 64:(j + 1) * 64]
                    sp_.append((o, ARTb[:, j, 0, :], STb[:, h0 + j, :], True, False))
                    sp_.append((o, AK[:, j, 0, :], w["vb"][:, j * 64:(j + 1) * 64], False, True))
                MM(sp_, ARTbk + AKk + ["vb", STbk], [bk.name])
                CP("act", RHSb[:], bk[0:64, 0:HW], [bk.name], [RHSb.name])
                bk = bank()
                MM([(bk[0:64, j * 64:(j + 1) * 64], XT[:, j, :], RHSb[:, j * 64:(j + 1) * 64], True, True) for j in range(NH)],
                   [XT.name, RHSb.name], [bk.name])
                CP("dve", w["Us"][:], bk[0:64, 0:HW], [bk.name], ["Us"])
                by = bank()
                sp_ = []
                for j in range(NH):
                    o = by[0:64, j * 64:(j + 1) * 64]
                    sp_.append((o, ARTb[:, j, 1, :], STb[:, h0 + j, :], True, False))
                    sp_.append((o, AB[:, j, 1, :], w["Us"][:, j * 64:(j + 1) * 64], False, False))
                    sp_.append((o, AK[:, j, 1, :], w["vb"][:, j * 64:(j + 1) * 64], False, True))
                MM(sp_, ARTbk + ABk + AKk + ["vb", STbk, "Us"], [by.name])
                bs_ = bank()
                sp_ = []
                for j in range(NH):
                    o = bs_[0:64, j * 64:(j + 1) * 64]
                    sp_.append((o, w["btb"][:, j * 64:(j + 1) * 64], w["Us"][:, j * 64:(j + 1) * 64], True, False))
                    sp_.append((o, w["ktb"][:, j * 64:(j + 1) * 64], w["vb"][:, j * 64:(j + 1) * 64], False, True))
                MM(sp_, ["btb", "ktb", "Us", "vb"], [bs_.name])
                STh = ST[:, h0:h0 + NH, :]
                TT("dve", STh, STh, bs_[0:64, 0:HW].rearrange("p (h v) -> p h v", h=NH), ALU.add, [STk, bs_.name], [STk])
                TT("pool", STh, STh, bc8(WC[:]), ALU.mult, [STk, WC.name], [STk])
                CP("act", STb[:, h0:h0 + NH, :], STh, [STk], [STbk])
                CP("act", w["Ys"][:], by[0:64, 0:HW], [by.name], ["Ys"])
                RED(sm["s1"][:], h3(w["Ys"][:]), ["Ys"], ["s1"])
                ACT(w["tA"][:], w["Ys"][:], AF.Square, ["Ys"], ["tA"])
                RED(sm["s2"][:], h3(w["tA"][:]), ["tA"], ["s2"])
                TS("dve", sm["mean"][:], sm["s1"][:], 1.0 / 64, None, ALU.mult, None, ["s1"], ["mean"])
                TT("dve", sm["var"][:], sm["mean"][:], sm["mean"][:], ALU.mult, ["mean"], ["var"])
                STT(sm["var"][:], sm["s2"][:], 1.0 / 64, sm["var"][:], ALU.mult, ALU.subtract, ["s2", "var"], ["var"])
                ACT(sm["var"][:], sm["var"][:], AF.Ln, ["var"], ["var"], bias=RW_GN_EPS)
                ACT(sm["var"][:], sm["var"][:], AF.Exp, ["var"], ["var"], scale=-0.5)
                TT("dve", h3(w["Ys"][:]), h3(w["Ys"][:]), bc8(sm["mean"][:]), ALU.subtract, ["Ys", "mean"], ["Ys"])
                TT("pool", h3(w["Ys"][:]), h3(w["Ys"][:]), bc8(sm["var"][:]), ALU.mult, ["Ys", "var"], ["Ys"])
                TT("pool", w["Ys"][:], w["Ys"][:], gnw_bc[:, cs:cs + HW], ALU.mult, ["Ys", gnw_bc.name], ["Ys"])
                TT("pool", w["Ys"][:], w["Ys"][:], gnb_bc[:, cs:cs + HW], ALU.add, ["Ys", gnb_bc.name], ["Ys"])
                TT("pool", w["tB"][:], r_, w["kp"][:], ALU.mult, c3 + ["kp"], ["tB"])
                TT("pool", w["tB"][:], w["tB"][:], rk_bc[:, cs:cs + HW], ALU.mult, ["tB", rk_bc.name], ["tB"])
                RED(sm["bs"][:], h3(w["tB"][:]), ["tB"], ["bs"])
                TT("dve", h3(w["tB"][:]), h3(v_), bc8(sm["bs"][:]), ALU.mult, c3 + ["bs"], ["tB"])
                TT("pool", w["Ys"][:], w["Ys"][:], w["tB"][:], ALU.add, ["Ys", "tB"], ["Ys"])
                TT("dve", w["yo"][:], w["Ys"][:], w["gg"][:], ALU.mult, ["Ys", "gg"], ["yo"])
                bk = bank()
                NJ = HW // 128
                TR([(bk[:, j * 64:(j + 1) * 64], w["yo"][:, j * 128:(j + 1) * 128], i64) for j in range(NJ)], ["yo", ident.name], [bk.name])
                CP("act", yT[:], bk[:, 0:NJ * 64].rearrange("p (j t) -> p j t", j=NJ), [bk.name], [yT.name])
                T.dma("act", yT.name, yrwT[:, hh * NJ:(hh + 1) * NJ, t0:t0 + 64], yT[:], reads=[yT.name])
            if c in (0, 1, cfg.nchunks - 1):
                store_states_rw(seq, sid)

        def ssd_chunk(c):
            t0 = c * 64
            seq = min(c, 2)
            short = c < 3
            if c < 3:
                load_states_ssd(seq)
            LD("mix_Fin", XB2, xbcv[:, :, 4 + t0:4 + t0 + 64], [XBk])
            LD(pdt.name, pdt[:], S["P_DT"][t0:t0 + 64, :], [pdt.name])
            XB = XB2
            d = dts
            TT("dve", d["dt"][:], pdt[:], dtb_bc[:], ALU.add, [pdt.name, dtb_bc.name], ["dt"])
            ACT(d["dt"][:], d["dt"][:], AF.Exp, ["dt"], ["dt"])
            ACT(d["dt"][:], d["dt"][:], AF.Ln, ["dt"], ["dt"], bias=1.0)
            if short:
                TS("dve", d["dt"][:], d["dt"][:], rowmask[:, 0:1], None, ALU.mult, None, ["dt", rowmask.name], ["dt"])
            TT("dve", d["adt"][:], d["dt"][:], aneg_bc[:], ALU.mult, ["dt", aneg_bc.name], ["adt"])
            b_ac = bank()
            MM([(b_ac[0:64, 0:SSM_H], tri[:], d["adt"][:], True, True)], [tri.name, "adt"], [b_ac.name])
            b_tot = bank()
            MM([(b_tot[:, 0:SSM_H], ones[:], d["adt"][:], True, True)], [ones.name, "adt"], [b_tot.name])
            CP("dve", d["acs"][:], b_ac[0:64, 0:SSM_H], [b_ac.name], ["acs"])
            ACT(d["eacs"][:], b_ac[0:64, 0:SSM_H], AF.Exp, [b_ac.name], ["eacs"])
            ACT(cdec[:], b_tot[:, 0:SSM_H], AF.Exp, [b_tot.name], [cdec.name])
            TT("dve", d["toend"][:], b_tot[0:64, 0:SSM_H], d["acs"][:], ALU.subtract, [b_tot.name, "acs"], ["toend"])
            ACT(d["toend"][:], d["toend"][:], AF.Exp, ["toend"], ["toend"])
            bk = bank()
            TR([(bk[0:64, j * 128:(j + 1) * 128], XB[:, 16 + j, :], ident[:]) for j in range(4)], [XBk, ident.name], [bk.name])
            CP("act", Btm[:], bk[0:64, :], [bk.name], [Btm.name])
            g_ = G_
            for g in range(4):
                hs = slice(8 * g, 8 * g + 8)
                yT2 = yT2s[g % 2]
                zz = ZZ[g % 2]
                HTk = (HT.name, g)
                HTg = HT[:, g * 512:(g + 1) * 512]
                bk = bank()
                TR([(bk[0:64, j * 128:(j + 1) * 128], XB[:, 4 * g + j, :], ident[:]) for j in range(4)], [XBk, ident.name], [bk.name])
                CP("act", g_["xs"][:], bk[0:64, :], [bk.name], ["xs"])
                LD(zz.name, zz[:], S["P_Z"][t0:t0 + 64, g * 512:(g + 1) * 512], [zz.name])
                ACT(zz[:], zz[:], AF.Silu, [zz.name], [zz.name])
                TT("dve", h3s(g_["xdt"][:]), h3s(g_["xs"][:]), bc8s(d["dt"][:, hs]), ALU.mult, ["xs", "dt"], ["xdt"])
                TT("pool", h3s(g_["xdtw"][:]), h3s(g_["xdt"][:]), bc8s(d["toend"][:, hs]), ALU.mult, ["xdt", "toend"], ["xdtw"])
                bcb = bank()
                MM([(bcb[0:64, 0:64], XB[:, 16 + g, :], XB[:, 20 + g, :], True, True)], [XBk], [bcb.name])
                TT("dve", cbm[:], bcb[0:64, 0:64], msl[:, 1, :], ALU.mult, [bcb.name, msl.name], [cbm.name])
                TT("pool", h3s(g_["ML"][:]), bc8s(d["adt"][:, hs]), mgt[:].unsqueeze(1).to_broadcast([64, 8, 64]), ALU.mult,
                   ["adt", mgt.name], ["ML"])
                bsg = bank()
                MM([(bsg[0:64, j * 64:(j + 1) * 64], g_["ML"][:, j * 64:(j + 1) * 64], tri[:], True, True) for j in range(8)],
                   ["ML", tri.name], [bsg.name])
                ACT(g_["EX"][:], bsg[0:64, :], AF.Exp, [bsg.name], ["EX"])
                TT("pool", h3s(g_["MT"][:]), h3s(g_["EX"][:]), cbm[:].unsqueeze(1).to_broadcast([64, 8, 64]), ALU.mult, ["EX", cbm.name], ["MT"])
                byd = bank()
                MM([(byd[0:64, j * 64:(j + 1) * 64], g_["MT"][:, j * 64:(j + 1) * 64], g_["xdt"][:, j * 64:(j + 1) * 64], True, True)
                    for j in range(8)], ["MT", "xdt"], [byd.name])
                byo = bank()
                MM([(byo[0:64, :], XB[:, 20 + g, :], HTg, True, True)], [XBk, HTk], [byo.name])
                TT("dve", h3s(g_["yy"][:]), byo[0:64, :].rearrange("p (h k) -> p h k", h=8), bc8s(d["eacs"][:, hs]), ALU.mult,
                   [byo.name, "eacs"], ["yy"])
                TT("dve", g_["yy"][:], g_["yy"][:], byd[0:64, :], ALU.add, ["yy", byd.name], ["yy"])
                TT("pool", h3s(g_["t1"][:]), h3s(g_["xs"][:]), bc8s(dsk_bc[:, hs]), ALU.mult, ["xs", dsk_bc.name], ["t1"])
                TT("pool", g_["yy"][:], g_["yy"][:], g_["t1"][:], ALU.add, ["yy", "t1"], ["yy"])
                TT("dve", g_["yy"][:], g_["yy"][:], zz[:], ALU.mult, ["yy", zz.name], ["yy"])
                ACT(g_["t1"][:], g_["yy"][:], AF.Square, ["yy"], ["t1"])
                T.op("dve", lambda e, o=ss1[:], i=g_["t1"][:]: e.tensor_reduce(out=o, in_=i, axis=AX.X, op=ALU.add), reads=["t1"], writes=[ss1.name])
                ACT(ss1[:], ss1[:], AF.Ln, [ss1.name], [ss1.name], bias=RMS_EPS, scale=1.0 / 512)
                ACT(ss1[:], ss1[:], AF.Exp, [ss1.name], [ss1.name], scale=-0.5)
                TS("dve", g_["yy"][:], g_["yy"][:], ss1[:, 0:1], None, ALU.mult, None, ["yy", ss1.name], ["yy"])
                bk = bank()
                TR([(bk[:, j * 64:(j + 1) * 64], g_["yy"][:, j * 128:(j + 1) * 128], ident[0:64, 0:64]) for j in range(4)], ["yy", ident.name], [bk.name])
                TT("dve", yT2[:], bk[:, 0:256].rearrange("p (j t) -> p j t", j=4), snw[:, 4 * g:4 * g + 4].unsqueeze(2).to_broadcast([128, 4, 64]),
                   ALU.mult, [bk.name, snw.name], [yT2.name])
                T.dma("pool", yT2.name, yssT[:, 4 * g:4 * g + 4, t0:t0 + 64], yT2[:], reads=[yT2.name])
                bst = bank()
                MM([(bst[:, :], Btm[:, g * 128:(g + 1) * 128], g_["xdtw"][:], True, True)], [Btm.name, "xdtw"], [bst.name])
                TT("pool", HTg.rearrange("p (h k) -> p h k", h=8), HTg.rearrange("p (h k) -> p h k", h=8),
                   cdec[:, hs].unsqueeze(2).to_broadcast([128, 8, 64]), ALU.mult, [HTk, cdec.name], [HTk])
                TT("dve", HTg, HTg, bst[:, :], ALU.add, [HTk, bst.name], [HTk])
            if c in (0, 1, cfg.nchunks - 1):
                store_states_ssd(seq)

        def rw_stream(sid):
            bpool[0] = [3 * sid, 3 * sid + 1, 3 * sid + 2]
            kpref[0] = f"s{sid}:"
            for c in range(cfg.nchunks):
                rwkv_chunk(c, sid)
            kpref[0] = ""

        def ssd_stream():
            bpool[0] = [6, 7]
            for c in range(cfg.nchunks):
                ssd_chunk(c)

        streams = [T.capture(lambda i=i: rw_stream(i)) for i in range(NSTR)]
        streams.append(T.capture(ssd_stream))
        T.replay_merged(streams)
        T.flush()


def phase_outp(nc, T, cfg, pb, S, W, name):
    TB = cfg.tile_blocks
    TT = TB * 128
    npc = (TT + 511) // 512
    pw = TT // npc
    GATE0 = 8672
    with ExitStack() as st:
        sb = _sbuf(nc, st, name)
        yrT = sb("yrT", [128, 8, TT], BF16)
        ysT = sb("ysT", [128, NKC, TT], BF16)
        x1T = sb("x1T", [128, NKC, TT], BF16)
        mT = sb("mT", [128, NKC, TT], BF16)
        bg = sb("bg", [128, 32])
        wra = [sb(f"wra{i}", [128, 8, 256], BF16) for i in range(2)]
        wsa = [sb(f"wsa{i}", [128, NKC, 256], BF16) for i in range(2)]
        wga = [sb(f"wga{i}", [128, NKC, 256], BF16) for i in range(2)]
        wgb = [sb(f"wgb{i}", [128, NKC, 256], BF16) for i in range(2)]
        wo = [sb(f"wo{i}", [128, NKC, 512], BF16) for i in range(2)]
        sga = [sb(f"sga{i}", [128, pw]) for i in range(2)]
        sgb = [sb(f"sgb{i}", [128, pw]) for i in range(2)]
        xr = sb("xr", [128, TB, 512])
        so = [sb(f"so{i}", [128, 512]) for i in range(2)]
        T.dma("sp", bg.name, bg[:], W["b_gate"][:, :], writes=[bg.name])
        yrv = S["YRWT"].rearrange("(c p) r -> p c r", p=128)
        ysv = S["YSSMT"].rearrange("(c p) r -> p c r", p=128)
        x1v = S["X1T"].rearrange("(c p) r -> p c r", p=128)
        wrv = W["w_rw_out"].rearrange("(c p) n -> p c n", p=128)
        wsv = W["w_ssm_out"].rearrange("(c p) n -> p c n", p=128)
        wiv = W["w_in"].rearrange("(c p) n -> p c n", p=128)
        wov = W["w_out"].rearrange("(c p) n -> p c n", p=128)
        q = 0
        xq = 0
        for t in range(cfg.ntiles):
            r0 = t * TT
            T.dma("sp", yrT.name, yrT[:], yrv[:, :, r0:r0 + TT], writes=[yrT.name])
            T.dma("sp", ysT.name, ysT[:], ysv[:, :, r0:r0 + TT], writes=[ysT.name])
            T.dma("sp", x1T.name, x1T[:], x1v[:, :, r0:r0 + TT], writes=[x1T.name])
            for cc in range(NKC):
                s = (cc // 2) % 2
                cj = cc % 2
                if cj == 0:
                    c0 = (cc // 2) * 256
                    cg_ = cc // 2
                    srcs = ((wra[s], wrv[:, :, c0:c0 + 256], 0, 8), (wsa[s], wsv[:, :, c0:c0 + 256], 8, NKC),
                            (wga[s], wiv[:, :, GATE0 + c0:GATE0 + c0 + 256], 24, NKC),
                            (wgb[s], wiv[:, :, GATE0 + D + c0:GATE0 + D + c0 + 256], 40, NKC))
                    for wt, src_ap, k0, nk in srcs:
                        cview = S["WOC_B"][cg_].rearrange("p (k n) -> p k n", n=256)[:, k0:k0 + nk, :]
                        ck = ("woc", cg_, k0)
                        if t > 0:
                            T.dma("pool", wt.name, wt[:], cview, reads=[ck], writes=[wt.name])
                        else:
                            T.dma("pool", wt.name, wt[:], src_ap, writes=[wt.name])
                            if cfg.ntiles > 1:
                                T.dma("sp", wt.name + "_wb", cview, wt[:], reads=[wt.name], writes=[ck])
                for pc in range(npc):
                    ps = slice(pc * pw, (pc + 1) * pw)
                    banks = [pb[(4 * q + i) % 8] for i in range(4)]
                    sa, sb2 = sga[q % 2], sgb[q % 2]
                    q += 1
                    for bank, w, act, nk in ((banks[0], wra[s], yrT, 8), (banks[1], wsa[s], ysT, NKC),
                                             (banks[2], wga[s], x1T, NKC), (banks[3], wgb[s], x1T, NKC)):
                        T.op("pe", [lambda e, kc=kc, bank=bank, w=w, act=act, ps=ps, nk=nk, cj=cj: e.matmul(
                            bank[:, 0:pw], lhsT=w[:, kc, cj * 128:(cj + 1) * 128], rhs=act[:, kc, ps], start=(kc == 0), stop=(kc == nk - 1)) for kc in range(nk)],
                            reads=[w.name, act.name], writes=[bank.name])
                    T.op("act", lambda e, sa=sa, bank=banks[2], cc=cc: e.activation(out=sa[:], in_=bank[:, 0:pw], func=AF.Sigmoid, bias=bg[:, cc:cc + 1]),
                         reads=[banks[2].name, bg.name], writes=[sa.name])
                    T.op("act", lambda e, sb2=sb2, bank=banks[3], cc=cc: e.activation(out=sb2[:], in_=bank[:, 0:pw], func=AF.Sigmoid, bias=bg[:, 16 + cc:17 + cc]),
                         reads=[banks[3].name, bg.name], writes=[sb2.name])
                    T.op("dve", lambda e, sa=sa, bank=banks[0]: e.tensor_tensor(out=sa[:], in0=sa[:], in1=bank[:, 0:pw], op=ALU.mult),
                         reads=[sa.name, banks[0].name], writes=[sa.name])
                    T.op("dve", lambda e, sb2=sb2, bank=banks[1]: e.tensor_tensor(out=sb2[:], in0=sb2[:], in1=bank[:, 0:pw], op=ALU.mult),
                         reads=[sb2.name, banks[1].name], writes=[sb2.name])
                    T.op("pool", lambda e, sa=sa, sb2=sb2, cc=cc, ps=ps: e.tensor_tensor(out=mT[:, cc, ps], in0=sa[:], in1=sb2[:], op=ALU.add),
                         reads=[sa.name, sb2.name], writes=[(mT.name, cc, pc)])
            mkeys = [(mT.name, cc, pc) for cc in range(NKC) for pc in range(npc)]
            for cb in range(4):
                w = wo[cb % 2]
                T.dma("pool", w.name, w[:], wov[:, :, cb * 512:(cb + 1) * 512], writes=[w.name])
                T.dma("sp", xr.name, xr[:], S["X1"][r0:r0 + TT, cb * 512:(cb + 1) * 512].rearrange("(b p) n -> p b n", p=128), writes=[xr.name])
                for b in range(TB):
                    bank = pb[q % 8]
                    q += 1
                    T.op("pe", [lambda e, kc=kc, bank=bank, w=w, b=b: e.matmul(
                        bank[:, :], lhsT=mT[:, kc, b * 128:(b + 1) * 128], rhs=w[:, kc, :], start=(kc == 0), stop=(kc == NKC - 1))
                        for kc in range(NKC)], reads=[w.name] + mkeys, writes=[bank.name])
                    rows = r0 + b * 128
                    o = so[xq % 2]
                    xq += 1
                    T.op("dve", lambda e, bank=bank, b=b, o=o: e.scalar_tensor_tensor(
                        out=o[:], in0=xr[:, b, :], scalar=ALPHA, in1=bank[:, :], op0=ALU.mult, op1=ALU.add),
                        reads=[bank.name, xr.name], writes=[o.name])
                    T.dma("sp", o.name, S["XPRE"][rows:rows + 128, cb * 512:(cb + 1) * 512], o[:], reads=[o.name])
        T.flush()


WEIGHT_SPECS = [
    ("ffn1_gu", [D, 2 * DFF]), ("ffn1_dn", [DFF, D]), ("ln1_g", [1, D]), ("ln1_b", [1, D]),
    ("w_in", [D, N_IN]), ("b_gate", [128, 32]),
    ("rw_mu", [1, RW_COLS]), ("rw_w0", [1, RW_DIM]), ("rw_w2", [96, RW_DIM]), ("rw_a0", [1, RW_DIM]),
    ("rw_a2", [96, RW_DIM]), ("rw_g2", [256, RW_DIM]), ("rw_kk", [1, RW_DIM]), ("rw_ka", [1, RW_DIM]),
    ("rw_rk", [1, RW_DIM]), ("rw_gn_w", [1, RW_DIM]), ("rw_gn_b", [1, RW_DIM]),
    ("conv_w", [128, 24 * 4]), ("conv_b", [128, 24]), ("dt_bias", [1, SSM_H]), ("a_log", [1, SSM_H]),
    ("d_skip", [1, SSM_H]), ("ssm_norm_w", [128, 16]),
    ("w_rw_out", [RW_DIM, D]), ("w_ssm_out", [D, D]), ("w_out", [D, D]), ("ln2_g", [1, D]), ("ln2_b", [1, D]),
    ("ffn2_gu", [D, 2 * DFF]), ("ffn2_dn", [DFF, D]), ("ln3_g", [1, D]), ("ln3_b", [1, D]),
]


def build_program(cfg):
    nc = bass.Bass("TRN2", target_bir_lowering=False)
    R = cfg.R
    dbg = "ExternalOutput" if cfg.debug else None

    def din(name, shape, dt=F32):
        return nc.dram_tensor(name, shape, dt, kind="ExternalInput").ap()

    def dout(name, shape, dt=F32):
        return nc.dram_tensor(name, shape, dt, kind="ExternalOutput").ap()

    def dscr(name, shape, dt=F32):
        if name in cfg.scratch_in:
            return nc.dram_tensor(name, shape, dt, kind="ExternalInput").ap()
        if dbg:
            return nc.dram_tensor(name, shape, dt, kind=dbg).ap()
        return nc.dram_tensor(name, shape, dt).ap()

    I = {}
    I["xin"] = din("xin", [R, D])
    I["xinT"] = din("xinT", [D, R])
    I["hist_rw"] = din("hist_rw", [3, RW_COLS])
    I["hist_conv"] = din("hist_conv", [9, CONV_DIM])
    I["wkv0"] = din("wkv0", [3 * RW_H * 64, 64])
    I["ssm0"] = din("ssm0", [3 * SSM_H * 64, SSM_N])
    W = {n: din(n, shp) for n, shp in WEIGHT_SPECS}
    O = {}
    O["yout"] = dout("yout", [R, D])
    O["o_shift"] = dout("o_shift", [3, RW_COLS])
    O["o_conv"] = dout("o_conv", [9, CONV_DIM])
    O["o_wkv"] = dout("o_wkv", [3 * RW_H * 64, 64])
    O["o_ssm"] = dout("o_ssm", [3 * SSM_H * 64, SSM_N])
    S = {}
    S["X0T"] = dscr("X0T", [D, R], BF16)
    S["XPRE"] = dscr("XPRE", [R, D])
    S["X1"] = dscr("X1", [R, D])
    S["X1T"] = dscr("X1T", [D, R], BF16)
    S["X2"] = dscr("X2", [R, D])
    S["X2T"] = dscr("X2T", [D, R], BF16)
    S["P_RW"] = dscr("P_RW", [1 + R, RW_COLS])
    S["P_Z"] = dscr("P_Z", [R, D])
    S["P_DT"] = dscr("P_DT", [R, SSM_H])
    S["PT_XBC"] = dscr("PT_XBC", [CONV_DIM, 4 + R])
    S["YRWT"] = dscr("YRWT", [RW_DIM, R], BF16)
    S["YSSMT"] = dscr("YSSMT", [D, R], BF16)
    S["WDN_B"] = nc.dram_tensor("WDN_B", [DFF, D], BF16).ap()
    S["WOC_B"] = nc.dram_tensor("WOC_B", [8, 128, 56 * 256], BF16).ap()

    with ExitStack() as st:
        T = Tracker(nc, st)
        pb = [st.enter_context(nc.psum_tensor(f"pb{i}", [128, 512], F32)) for i in range(8)]
        T.excl = set(b.name for b in pb)

        def run(name, kind, fn, *a):
            T.rename = (name, kind)
            fn(*a)
            return cfg.stop_after == name

        phases = [
            ("ffn1", "ffn", phase_ffn, (nc, T, cfg, pb, I["xinT"], I["xin"], W["ffn1_gu"], W["ffn1_dn"], S["XPRE"], "ffn1", S["WDN_B"])),
            ("ln1", "ln", phase_ln, (nc, T, cfg, pb, S["XPRE"], S["X1"], S["X1T"], W["ln1_g"], W["ln1_b"], "ln1")),
            ("win", "win", phase_win, (nc, T, cfg, pb, S, W, O, "win", I["hist_conv"])),
            ("mix", "mix", phase_mix, (nc, T, cfg, pb, S, W, I, O, "mix")),
            ("outp", "outp", phase_outp, (nc, T, cfg, pb, S, W, "outp")),
            ("ln2", "ln", phase_ln, (nc, T, cfg, pb, S["XPRE"], S["X2"], S["X2T"], W["ln2_g"], W["ln2_b"], "ln2")),
            ("ffn2", "ffn", phase_ffn, (nc, T, cfg, pb, S["X2T"], S["X2"], W["ffn2_gu"], W["ffn2_dn"], S["XPRE"], "ffn2", S["WDN_B"])),
            ("ln3", "ln", phase_ln, (nc, T, cfg, pb, S["XPRE"], O["yout"], None, W["ln3_g"], W["ln3_b"], "ln3")),
        ]
        for name, kind, fn, a in phases:
            if cfg.only is not None and name not in cfg.only:
                continue
            if run(name, kind, fn, *a):
                break
    return nc


def _pp(v, nchunk):
    return np.ascontiguousarray(np.asarray(v, np.float32).reshape(nchunk, 128).T)


def prep_weights(inp):
    f = lambda a: np.ascontiguousarray(np.asarray(a, np.float32))
    Wn = {}
    for n in ("ffn1_gu", "ffn1_dn", "w_in", "rw_w2", "rw_a2", "rw_g2", "w_rw_out", "w_ssm_out", "w_out", "ffn2_gu", "ffn2_dn"):
        Wn[n] = f(inp[n][0])
    for n in ("ln1_g", "ln1_b", "rw_mu", "rw_w0", "rw_a0", "dt_bias", "a_log", "d_skip", "ln2_g", "ln2_b", "ln3_g", "ln3_b"):
        Wn[n] = f(inp[n][0]).reshape(1, -1)
    for n in ("rw_kk", "rw_ka", "rw_rk", "rw_gn_w", "rw_gn_b"):
        Wn[n] = f(inp[n][0]).reshape(1, -1)
    Wn["b_gate"] = _pp(inp["b_gate"][0], 32)
    cw = np.asarray(inp["conv_w"][0], np.float32)
    Wn["conv_w"] = np.ascontiguousarray(cw.reshape(4, 24, 128).transpose(2, 1, 0).reshape(128, 96))
    Wn["conv_b"] = _pp(inp["conv_b"][0], 24)
    Wn["ssm_norm_w"] = _pp(inp["ssm_norm_w"][0], 16)
    return Wn


def core_inputs(inp, cfg, core, Wn):
    R = cfg.R
    b = core % 4
    sa, sb_ = 2 * core, 2 * core + 1
    xin = np.zeros((R, D), np.float32)
    xin[48:64] = inp["x_sample"][sa]
    xin[64 + 48:128] = inp["x_sample"][sb_]
    xin[128 + 48:192] = inp["meta_tokens"]
    nl = cfg.n_long * 64
    xin[192:192 + nl] = inp["x_prompt"][b][:nl]
    m = {"xin": xin, "xinT": np.ascontiguousarray(xin.T)}
    z = np.zeros
    m["hist_rw"] = np.ascontiguousarray(np.concatenate(
        [inp["state_rwkv_shift"][0, sa], inp["state_rwkv_shift"][0, sb_], z((1, RW_COLS), np.float32)], 0).astype(np.float32))
    m["hist_conv"] = np.ascontiguousarray(np.concatenate(
        [inp["state_conv"][0, sa], inp["state_conv"][0, sb_], z((3, CONV_DIM), np.float32)], 0).astype(np.float32))
    m["wkv0"] = np.ascontiguousarray(np.concatenate(
        [inp["state_wkv"][0, sa].reshape(-1, 64), inp["state_wkv"][0, sb_].reshape(-1, 64), z((RW_H * 64, 64), np.float32)], 0).astype(np.float32))
    m["ssm0"] = np.ascontiguousarray(np.concatenate(
        [inp["state_ssm"][0, sa].reshape(-1, SSM_N), inp["state_ssm"][0, sb_].reshape(-1, SSM_N), z((SSM_H * 64, SSM_N), np.float32)], 0).astype(np.float32))
    m.update(Wn)
    return m


def assemble(results, cfg):
    nl = cfg.n_long * 64
    f = np.float32
    y_prompt = np.zeros((4, nl, D), f)
    y_sample = np.zeros((16, 16, D), f)
    p_shift = np.zeros((1, 4, 1, RW_COLS), f)
    p_wkv = np.zeros((1, 4, RW_H, 64, 64), f)
    p_conv = np.zeros((1, 4, 3, CONV_DIM), f)
    p_ssm = np.zeros((1, 4, SSM_H, 64, SSM_N), f)
    s_shift = np.zeros((1, 16, 1, RW_COLS), f)
    s_wkv = np.zeros((1, 16, RW_H, 64, 64), f)
    s_conv = np.zeros((1, 16, 3, CONV_DIM), f)
    s_ssm = np.zeros((1, 16, SSM_H, 64, SSM_N), f)
    for c, r in enumerate(results):
        yo = np.asarray(r["yout"])
        osh = np.asarray(r["o_shift"])
        ocv = np.asarray(r["o_conv"])
        owk = np.asarray(r["o_wkv"]).reshape(3, RW_H, 64, 64)
        osm = np.asarray(r["o_ssm"]).reshape(3, SSM_H, 64, SSM_N)
        for i in range(2):
            s = 2 * c + i
            y_sample[s] = yo[i * 64 + 48:i * 64 + 64]
            s_shift[0, s, 0] = osh[i]
            s_conv[0, s] = ocv[i * 3:(i + 1) * 3]
            s_wkv[0, s] = owk[i]
            s_ssm[0, s] = osm[i]
        if c < 4:
            y_prompt[c] = yo[192:192 + nl]
            p_shift[0, c, 0] = osh[2]
            p_conv[0, c] = ocv[6:9]
            p_wkv[0, c] = owk[2]
            p_ssm[0, c] = osm[2]
    return (y_prompt, y_sample, p_shift, p_wkv, p_conv, p_ssm, s_shift, s_wkv, s_conv, s_ssm)


def kernel(**inputs):
    cfg = Cfg()
    inp = {k: np.asarray(v) for k, v in inputs.items()}
    nc = build_program(cfg)
    Wn = prep_weights(inp)
    maps = [core_inputs(inp, cfg, c, Wn) for c in range(8)]
    res = run_bass_kernel_spmd(nc, maps, core_ids=list(range(8)))
    return assemble(res.results, cfg)
```

```python
from contextlib import ExitStack
import numpy as np
import concourse.bass as bass
import concourse.mybir as mybir
from concourse.bass_utils import run_bass_kernel_spmd

F32 = mybir.dt.float32
BF16 = mybir.dt.bfloat16
AF = mybir.ActivationFunctionType
ALU = mybir.AluOpType
AX = mybir.AxisListType

D = 2048
DFF = 5632
NKC = D // 128
NFC = DFF // 128
RW_DIM = 1024
RW_H = 16
RW_COLS = 3520
SSM_H = 32
SSM_G = 4
SSM_N = 128
CONV_DIM = 3072
N_IN = 12768
ALPHA = 2.0 ** 0.25
LN_EPS = 1e-5
RW_GN_EPS = 64e-5
RMS_EPS = 1e-5
C = 64


class Tracker:
    ENGS = ("pe", "act", "dve", "pool", "sp")

    def __init__(self, nc, stack):
        self.nc = nc
        self.stack = stack
        self.sem = {e: stack.enter_context(nc.semaphore("s_" + e)) for e in self.ENGS}
        self.cnt = {e: 0 for e in self.ENGS}
        self.chan_sem = {}
        self.chan_cnt = {}
        self.seen = {}
        self.lastw = {}
        self.readers = {}
        self.prog = {e: [] for e in self.ENGS}
        self.nsem = len(self.ENGS)
        self.rename = None
        self.defer = None
        self._grp = None
        self.excl = set()

    def _split(self, reads, writes):
        if not self.excl:
            return list(reads), list(writes)
        r = [k for k in reads if k not in self.excl]
        w = list(writes) + [k for k in reads if k in self.excl]
        return r, w

    def _deps(self, reads, writes):
        deps = {}
        def add(d):
            if d is None:
                return
            s, v = d
            k = id(s)
            if k not in deps or deps[k][1] < v:
                deps[k] = (s, v)
        for k in reads:
            add(self.lastw.get(k))
        for k in writes:
            add(self.lastw.get(k))
            for d in self.readers.get(k, ()):
                add(d)
        return list(deps.values())

    def _emit_waits(self, eng, deps):
        for s, v in deps:
            key = (eng, id(s))
            if self.seen.get(key, 0) >= v:
                continue
            self.seen[key] = v
            self.prog[eng].append(lambda e, s=s, v=v: e.wait_ge(s, v))

    def _commit(self, dep, reads, writes):
        for k in reads:
            self.readers.setdefault(k, []).append(dep)
        for k in writes:
            self.lastw[k] = dep
            self.readers[k] = []

    def op(self, eng, fns, reads=(), writes=()):
        if self.defer is not None:
            self.defer.append(("op", (eng, fns, reads, writes), {}))
            return
        if not isinstance(fns, (list, tuple)):
            fns = [fns]
        reads, writes = self._split(reads, writes)
        self._emit_waits(eng, self._deps(reads, writes))
        self.cnt[eng] += 1
        n = self.cnt[eng]
        sem = self.sem[eng]
        for f in fns[:-1]:
            self.prog[eng].append(lambda e, f=f: f(e))
        last = fns[-1]
        self.prog[eng].append(lambda e, f=last, sem=sem: f(e).then_inc(sem, 1))
        self._commit((sem, n), reads, writes)

    def chan(self, name):
        if name not in self.chan_sem:
            self.chan_sem[name] = self.stack.enter_context(self.nc.semaphore("c_" + name))
            self.chan_cnt[name] = 0
            self.nsem += 1
        return self.chan_sem[name]

    def begin_group(self, chan):
        self._grp = (chan, [])

    def end_group(self):
        chan, keys = self._grp
        self._grp = None
        if chan in self.chan_sem:
            dep = (self.chan_sem[chan], self.chan_cnt[chan])
            for k in keys:
                self.lastw[k] = dep

    def dma(self, eng, chan, out, in_, reads=(), writes=(), **kw):
        if self.defer is not None:
            self.defer.append(("dma", (eng, chan, out, in_, reads, writes), kw))
            return
        if self._grp is not None:
            chan = self._grp[0]
            self._grp[1].extend(writes)
        if self.rename is not None:
            chan = chan.replace(self.rename[0], self.rename[1])
        sem = self.chan(chan)
        self._emit_waits(eng, self._deps(reads, writes))
        self.chan_cnt[chan] += 16
        n = self.chan_cnt[chan]
        self.prog[eng].append(lambda e, o=out, i=in_, sem=sem, kw=kw: e.dma_start(out=o, in_=i, **kw).then_inc(sem, 16))
        self._commit((sem, n), reads, writes)

    def capture(self, fn):
        assert self.defer is None
        self.defer = []
        try:
            fn()
        finally:
            lst, self.defer = self.defer, None
        return lst

    def replay(self, item):
        kind, a, kw = item
        if kind == "op":
            self.op(*a)
        else:
            self.dma(*a, **kw)

    def replay_merged(self, streams):
        pos = [0] * len(streams)
        tot = [max(len(x), 1) for x in streams]
        while True:
            best, bi = None, -1
            for i, x in enumerate(streams):
                if pos[i] < len(x):
                    f = pos[i] / tot[i]
                    if best is None or f < best:
                        best, bi = f, i
            if bi < 0:
                break
            self.replay(streams[bi][pos[bi]])
            pos[bi] += 1

    def drain_dmas(self, eng="sp"):
        for name, sem in self.chan_sem.items():
            v = self.chan_cnt[name]
            if v and self.seen.get((eng, id(sem)), 0) < v:
                self.seen[(eng, id(sem))] = v
                self.prog[eng].append(lambda e, s=sem, v=v: e.wait_ge(s, v))

    def flush(self):
        self.drain_dmas("sp")
        nc = self.nc
        prog = self.prog
        with nc.Block() as block:
            @block.tensor
            def _(e):
                for f in prog["pe"]:
                    f(e)

            @block.scalar
            def _(e):
                for f in prog["act"]:
                    f(e)

            @block.vector
            def _(e):
                for f in prog["dve"]:
                    f(e)

            @block.gpsimd
            def _(e):
                for f in prog["pool"]:
                    f(e)

            @block.sync
            def _(e):
                for f in prog["sp"]:
                    f(e)
        self.prog = {e: [] for e in self.ENGS}
        self.lastw = {}
        self.readers = {}


class Cfg:
    def __init__(self, n_long=32, tile_blocks=6, debug=False, stop_after=None, only=None, scratch_in=()):
        self.only = only
        self.mix_stop = None
        self.scratch_in = tuple(scratch_in)
        self.n_long = n_long
        self.nchunks = 3 + n_long
        self.nblocks = (self.nchunks + 1) // 2
        assert self.nblocks % tile_blocks == 0
        self.tile_blocks = tile_blocks
        self.ntiles = self.nblocks // tile_blocks
        self.R = self.nblocks * 128
        self.debug = debug
        self.stop_after = stop_after
        self.seq_end = [64, 128, self.nchunks * 64]


def _sbuf(nc, st, prefix):
    def sb(n, shape, dt=F32):
        return st.enter_context(nc.sbuf_tensor(f"{prefix}_{n}", shape, dt))
    return sb


def make_ident(T, ident, n=128):
    T.op("pool", lambda e: e.memset(ident[:], 0.0), writes=[ident.name])
    T.op("pool", lambda e: e.affine_select(out=ident[:], in_=ident[:], pattern=[[-1, n]], compare_op=ALU.not_equal,
                                           fill=1.0, base=0, channel_multiplier=1),
         reads=[ident.name], writes=[ident.name])


def phase_ln(nc, T, cfg, pb, src, dst_x, dst_xT, g_d, b_d, name):
    nb = cfg.nblocks
    GB = 3
    norm = g_d is not None
    with ExitStack() as st:
        sb = _sbuf(nc, st, name)
        NBUF = 3
        xin = [sb(f"xin{i}", [128, D]) for i in range(NBUF)]
        if norm:
            gbc = sb("gbc", [128, D])
            bbc = sb("bbc", [128, D])
            T.dma("sp", gbc.name, gbc[:], g_d.partition_broadcast(128), writes=[gbc.name])
            T.dma("sp", bbc.name, bbc[:], b_d.partition_broadcast(128), writes=[bbc.name])
            yv = [sb(f"y{i}", [128, D]) for i in range(NBUF)]
            stats = [sb(f"stats{i}", [128, 4, 6]) for i in range(NBUF)]
            mv = [sb(f"mv{i}", [128, 2]) for i in range(NBUF)]
            rstd = [sb(f"rstd{i}", [128, 1]) for i in range(NBUF)]
        if dst_xT is not None:
            ident = sb("ident", [128, 128])
            make_ident(T, ident)
            xTb = [sb(f"xTb{i}", [128, NKC, GB * 128], BF16) for i in range(2)]
            xTv = dst_xT.rearrange("(kc p) r -> p kc r", p=128)
        def head(b):
            s = b % NBUF
            x = xin[s]
            T.dma("sp", x.name, x[:], src[b * 128:(b + 1) * 128, :], writes=[x.name])
            y = x
            if norm:
                y = yv[s]
                for i in range(4):
                    T.op("dve", lambda e, i=i, x=x, s=s: e.bn_stats(out=stats[s][:, i, :], in_=x[:, i * 512:(i + 1) * 512]),
                         reads=[x.name], writes=[(stats[s].name, i)])
                T.op("dve", lambda e, s=s: e.bn_aggr(out=mv[s][:], in_=stats[s][:]),
                     reads=[(stats[s].name, i) for i in range(4)], writes=[mv[s].name])
                T.op("act", lambda e, s=s: e.activation(out=rstd[s][:], in_=mv[s][:, 1:2], func=AF.Sqrt, bias=LN_EPS),
                     reads=[mv[s].name], writes=[rstd[s].name])
                T.op("dve", lambda e, s=s: e.reciprocal(out=rstd[s][:], in_=rstd[s][:]),
                     reads=[rstd[s].name], writes=[rstd[s].name])
                T.op("dve", lambda e, s=s, x=x, y=y: e.tensor_scalar(out=y[:], in0=x[:], scalar1=mv[s][:, 0:1], scalar2=rstd[s][:, 0:1],
                                                                     op0=ALU.subtract, op1=ALU.mult),
                     reads=[x.name, mv[s].name, rstd[s].name], writes=[y.name])
                T.op("dve", lambda e, y=y: e.tensor_tensor(out=y[:], in0=y[:], in1=gbc[:], op=ALU.mult),
                     reads=[y.name, gbc.name], writes=[y.name])
                T.op("pool", lambda e, y=y: e.tensor_tensor(out=y[:], in0=y[:], in1=bbc[:], op=ALU.add),
                     reads=[y.name, bbc.name], writes=[y.name])
            return None

        def tail(b):
            s = b % NBUF
            y = yv[s] if norm else xin[s]
            if dst_x is not None:
                T.dma("act", y.name + "_st", dst_x[b * 128:(b + 1) * 128, :], y[:], reads=[y.name])
            if dst_xT is not None:
                g, gi = divmod(b, GB)
                gs = g % 2
                for q in range(4):
                    bank = pb[(b % 2) * 4 + q]
                    T.op("pe", [lambda e, kc=4 * q + j, j=j, bank=bank, y=y: e.transpose(
                        out=bank[:, j * 128:(j + 1) * 128], in_=y[:, kc * 128:(kc + 1) * 128], identity=ident[:]) for j in range(4)],
                        reads=[y.name, ident.name], writes=[bank.name])
                    eng = "act" if q % 2 == 0 else "dve"
                    outap = xTb[gs][:, 4 * q:4 * q + 4, gi * 128:(gi + 1) * 128]
                    inap = bank[:, :].rearrange("p (j t) -> p j t", j=4)
                    if eng == "act":
                        T.op("act", lambda e, o=outap, i=inap: e.copy(out=o, in_=i), reads=[bank.name], writes=[(xTb[gs].name, gi, q)])
                    else:
                        T.op("dve", lambda e, o=outap, i=inap: e.tensor_copy(out=o, in_=i), reads=[bank.name], writes=[(xTb[gs].name, gi, q)])
                if gi == GB - 1 or b == nb - 1:
                    nbk = gi + 1
                    r0 = g * GB * 128
                    keys = [(xTb[gs].name, a, q) for a in range(nbk) for q in range(4)]
                    T.dma("act", xTb[gs].name + "_st", xTv[:, :, r0:r0 + nbk * 128], xTb[gs][:, :, 0:nbk * 128], reads=keys)

        for b in range(nb):
            head(b)
            if b > 0:
                tail(b - 1)
        tail(nb - 1)
        T.flush()


def phase_ffn(nc, T, cfg, pb, xT_d, xres_d, wgu_d, wdn_d, dst_pre, name, wdn_cache=None):
    TB = cfg.tile_blocks
    TT = TB * 128
    npc = (TT + 511) // 512
    pw = TT // npc
    with ExitStack() as st:
        sb = _sbuf(nc, st, name)
        xT = sb("xT", [128, NKC, TT], BF16)
        hT = sb("hT", [128, NFC, TT], BF16)
        wgu = [[sb(f"wgu{i}{p}", [128, NKC, 512], BF16) for p in "gu"] for i in range(2)]
        wdn = [sb(f"wdn{i}", [128, 4, 512], BF16) for i in range(3)]
        sg = [sb(f"sg{i}", [128, pw]) for i in range(4)]
        xr = sb("xr", [128, TB, 512])
        so = sb("so", [128, TB, 512])
        xTv = xT_d.rearrange("(kc p) r -> p kc r", p=128)
        wguv = wgu_d.rearrange("(kc p) n -> p kc n", p=128)
        q = 0
        wq = 0
        xq = 0
        for t in range(cfg.ntiles):
            r0 = t * TT
            T.dma("pool" if xT_d.dtype == F32 else "sp", xT.name, xT[:], xTv[:, :, r0:r0 + TT], writes=[xT.name])
            for jg in range(NFC // 4):
                s = jg % 2
                for gu in range(2):
                    c0 = gu * DFF + jg * 512
                    w = wgu[s][gu]
                    T.dma("pool", w.name, w[:], wguv[:, :, c0:c0 + 512], writes=[w.name])
                for jj in range(4):
                    j = jg * 4 + jj
                    for pc in range(npc):
                        bg = pb[(2 * q) % 8]
                        bu = pb[(2 * q + 1) % 8]
                        sgt = sg[q % 4]
                        q += 1
                        for bank, w in ((bg, wgu[s][0]), (bu, wgu[s][1])):
                            T.op("pe", [lambda e, kc=kc, bank=bank, w=w, jj=jj, pc=pc: e.matmul(
                                bank[:, 0:pw], lhsT=w[:, kc, jj * 128:(jj + 1) * 128], rhs=xT[:, kc, pc * pw:(pc + 1) * pw],
                                start=(kc == 0), stop=(kc == NKC - 1)) for kc in range(NKC)],
                                reads=[w.name, xT.name], writes=[bank.name])
                        T.op("act", lambda e, bg=bg, sgt=sgt: e.activation(out=sgt[:], in_=bg[:, 0:pw], func=AF.Silu),
                             reads=[bg.name], writes=[sgt.name])
                        T.op("dve", lambda e, bu=bu, sgt=sgt, j=j, pc=pc: e.tensor_tensor(
                            out=hT[:, j, pc * pw:(pc + 1) * pw], in0=sgt[:], in1=bu[:, 0:pw], op=ALU.mult),
                            reads=[sgt.name, bu.name], writes=[(hT.name, j, pc)])
            for cb in range(4):
                T.dma("sp", xr.name, xr[:], xres_d[r0:r0 + TT, cb * 512:(cb + 1) * 512].rearrange("(b p) n -> p b n", p=128), writes=[xr.name])
                T.op("act", lambda e: e.mul(out=xr[:], in_=xr[:], mul=ALPHA), reads=[xr.name], writes=[xr.name])
                for j4 in range(NFC // 4):
                    w = wdn[wq % 3]
                    wq += 1
                    jr = slice(j4 * 512, (j4 + 1) * 512)
                    cs_ = slice(cb * 512, (cb + 1) * 512)
                    if wdn_cache is not None and t > 0:
                        T.dma("pool", w.name, w[:], wdn_cache[jr, cs_].rearrange("(a p) n -> p a n", p=128),
                              reads=[("wdnc", cb, j4)], writes=[w.name])
                    else:
                        T.dma("pool", w.name, w[:], wdn_d[jr, cs_].rearrange("(a p) n -> p a n", p=128), writes=[w.name])
                        if wdn_cache is not None and cfg.ntiles > 1:
                            T.dma("sp", w.name + "_wb", wdn_cache[jr, cs_].rearrange("(a p) n -> p a n", p=128), w[:],
                                  reads=[w.name], writes=[("wdnc", cb, j4)])
                    first, last = (j4 == 0), (j4 == NFC // 4 - 1)
                    wr = [pb[b].name for b in range(TB)] if (first or last) else []
                    T.op("pe", [lambda e, b=b, j=j4 * 4 + jj, jj=jj, w=w: e.matmul(
                        pb[b][:, :], lhsT=hT[:, j, b * 128:(b + 1) * 128], rhs=w[:, jj, :], start=(j == 0), stop=(j == NFC - 1))
                        for jj in range(4) for b in range(TB)],
                        reads=[w.name] + [(hT.name, j4 * 4 + jj, pc) for jj in range(4) for pc in range(npc)], writes=wr)
                for b in range(TB):
                    T.op("dve", lambda e, b=b: e.scalar_tensor_tensor(
                        out=so[:, b, :], in0=pb[b][:, :], scalar=0.5, in1=xr[:, b, :], op0=ALU.mult, op1=ALU.add),
                        reads=[pb[b].name, xr.name], writes=[(so.name, b)])
                T.dma("sp", so.name, dst_pre[r0:r0 + TT, cb * 512:(cb + 1) * 512].rearrange("(b p) n -> p b n", p=128), so[:],
                      reads=[(so.name, b) for b in range(TB)])
        T.flush()


def phase_win(nc, T, cfg, pb, S, W, O, name, I_hist=None):
    R = cfg.R
    nb = cfg.nblocks
    with ExitStack() as st:
        sb = _sbuf(nc, st, name)
        x1T = sb("x1T", [128, NKC, R], BF16)
        wsl = [sb(f"w{i}", [128, NKC, 512], BF16) for i in range(2)]
        stg = [sb(f"stg{i}", [128, 512]) for i in range(4)]
        fstg = [sb(f"fstg{i}", [128, 3 + R]) for i in range(2)]
        cacc = [sb(f"cacc{i}", [128, R]) for i in range(2)]
        cw = sb("cw", [128, 24, 4])
        cb = sb("cb", [128, 24])
        hrow = sb("hrow", [9, CONV_DIM])
        histT = sb("histT", [128, 24, 9])
        ident = sb("ident", [128, 128])
        make_ident(T, ident)
        T.dma("sp", cw.name, cw[:], W["conv_w"].rearrange("p (c i) -> p c i", i=4), writes=[cw.name])
        T.dma("sp", cb.name, cb[:], W["conv_b"][:, :], writes=[cb.name])
        T.dma("sp", hrow.name, hrow[:], I_hist[:, :], writes=[hrow.name])
        for q4 in range(6):
            bk = pb[q4 % 8]
            T.op("pe", [lambda e, j=j, bk=bk, q4=q4: e.transpose(out=bk[:, j * 9:(j + 1) * 9], in_=hrow[:, (q4 * 4 + j) * 128:(q4 * 4 + j + 1) * 128],
                                                          identity=ident[0:9, 0:9]) for j in range(4)], reads=[hrow.name, ident.name], writes=[bk.name])
            T.op("dve", lambda e, bk=bk, q4=q4: e.tensor_copy(out=histT[:, q4 * 4:(q4 + 1) * 4, :], in_=bk[:, 0:36].rearrange("p (j s) -> p j s", j=4)),
                 reads=[bk.name], writes=[histT.name])
        for f_ in fstg:
            T.op("pool", lambda e, f_=f_: e.memset(f_[:], 0.0), writes=[(f_.name, pc) for pc in range(R // 384)])
        tl = [sb(f"tl{i}", [3, 512]) for i in range(2)]
        zt = sb("zt", [128, RW_COLS])
        x1Tv = S["X1T"].rearrange("(kc p) r -> p kc r", p=128)
        wv = W["w_in"].rearrange("(kc p) n -> p kc n", p=128)
        TT = cfg.tile_blocks * 128
        for t in range(cfg.ntiles):
            T.dma("sp", f"{name}_x1T{t % 2}", x1T[:, :, t * TT:(t + 1) * TT], x1Tv[:, :, t * TT:(t + 1) * TT], writes=[(x1T.name, t)])
        x1keys = [(x1T.name, t) for t in range(cfg.ntiles)]
        T.op("pool", lambda e: e.memset(zt[:], 0.0), writes=[zt.name])
        T.dma("sp", f"{name}_z0", S["P_RW"][0:1, :], zt[0:1, :], reads=[zt.name])
        T.dma("sp", f"{name}_z1", S["PT_XBC"].rearrange("(c p) r -> p c r", p=128)[:, :, 0:4],
              zt[:, 0:96].rearrange("p (c r) -> p c r", r=4), reads=[zt.name])
        groups = []
        for c0 in range(0, RW_COLS, 512):
            groups.append((c0, min(512, RW_COLS - c0), S["P_RW"], c0, 1))
        for c0 in range(0, D, 512):
            groups.append((RW_COLS + c0, 512, S["P_Z"], c0, 0))
        groups.append((8640, SSM_H, S["P_DT"], 0, 0))
        q = 0
        wq = 0
        sq = 0
        for (c0, n, dst, dc0, roff) in groups:
            w = wsl[wq % 2]
            wq += 1
            T.dma("pool", w.name, w[:, :, 0:n], wv[:, :, c0:c0 + n], writes=[w.name])
            for b in range(nb):
                bank = pb[q % 8]
                q += 1
                T.op("pe", [lambda e, kc=kc, bank=bank, w=w, b=b, n=n: e.matmul(
                    bank[:, 0:n], lhsT=x1T[:, kc, b * 128:(b + 1) * 128], rhs=w[:, kc, 0:n],
                    start=(kc == 0), stop=(kc == NKC - 1)) for kc in range(NKC)],
                    reads=[w.name] + x1keys, writes=[bank.name])
                sg = stg[sq % 4]
                if dst is S["P_Z"]:
                    T.op("act", lambda e, sg=sg, bank=bank, n=n: e.activation(out=sg[:, 0:n], in_=bank[:, 0:n], func=AF.Silu),
                         reads=[bank.name], writes=[sg.name])
                elif sq % 2 == 0:
                    T.op("act", lambda e, sg=sg, bank=bank, n=n: e.copy(out=sg[:, 0:n], in_=bank[:, 0:n]), reads=[bank.name], writes=[sg.name])
                else:
                    T.op("dve", lambda e, sg=sg, bank=bank, n=n: e.tensor_copy(out=sg[:, 0:n], in_=bank[:, 0:n]), reads=[bank.name], writes=[sg.name])
                sq += 1
                T.dma("sp", sg.name, dst[roff + b * 128:roff + (b + 1) * 128, dc0:dc0 + n], sg[:, 0:n], reads=[sg.name])
        PW = 384
        npc = R // PW
        fq = 0
        tq = 0
        for cg in range(6):
            c0 = 5568 + cg * 512
            w = wsl[wq % 2]
            wq += 1
            T.dma("pool", w.name, w[:], wv[:, :, c0:c0 + 512], writes=[w.name])
            for cc in range(4):
                fs = fstg[fq % 2]
                fq += 1
                for pc in range(npc):
                    bank = pb[q % 8]
                    q += 1
                    T.op("pe", [lambda e, kc=kc, bank=bank, w=w, cc=cc, pc=pc: e.matmul(
                        bank[:, 0:PW], lhsT=w[:, kc, cc * 128:(cc + 1) * 128], rhs=x1T[:, kc, pc * PW:(pc + 1) * PW],
                        start=(kc == 0), stop=(kc == NKC - 1)) for kc in range(NKC)],
                        reads=[w.name] + x1keys, writes=[bank.name])
                    T.op("act", lambda e, fs=fs, bank=bank, pc=pc: e.copy(out=fs[:, 3 + pc * PW:3 + (pc + 1) * PW], in_=bank[:, 0:PW]),
                         reads=[bank.name], writes=[(fs.name, pc)])
                    sq += 1
                ch = cg * 4 + cc
                for seq in range(3):
                    c0_ = 3 + seq * 64 + 45
                    T.op("pool", lambda e, fs=fs, ch=ch, seq=seq, c0_=c0_: e.tensor_copy(out=fs[:, c0_:c0_ + 3], in_=histT[:, ch, 3 * seq:3 * seq + 3]),
                         reads=[histT.name], writes=[(fs.name, 0)])
                ca = cacc[(fq - 1) % 2]
                fkeys = [(fs.name, pc) for pc in range(npc)]
                T.op("dve", lambda e, fs=fs, ca=ca, ch=ch: e.tensor_scalar(out=ca[:], in0=fs[:, 0:R], scalar1=cw[:, ch, 0:1], scalar2=None, op0=ALU.mult),
                     reads=fkeys + [cw.name], writes=[ca.name])
                for i_ in range(1, 4):
                    T.op("dve", lambda e, fs=fs, ca=ca, ch=ch, i_=i_: e.scalar_tensor_tensor(
                        out=ca[:], in0=fs[:, i_:i_ + R], scalar=cw[:, ch, i_:i_ + 1], in1=ca[:], op0=ALU.mult, op1=ALU.add),
                        reads=fkeys + [cw.name, ca.name], writes=[ca.name])
                T.op("act", lambda e, ca=ca, ch=ch: e.activation(out=ca[:], in_=ca[:], func=AF.Silu, bias=cb[:, ch:ch + 1]),
                     reads=[ca.name, cb.name], writes=[ca.name])
                T.dma("sp", ca.name, S["PT_XBC"][ch * 128:(ch + 1) * 128, 4:4 + R], ca[:], reads=[ca.name])
            for si, rend in enumerate(cfg.seq_end):
                bank = pb[q % 8]
                q += 1
                T.op("pe", [lambda e, kc=kc, bank=bank, w=w, rend=rend: e.matmul(
                    bank[0:3, :], lhsT=x1T[:, kc, rend - 3:rend], rhs=w[:, kc, :],
                    start=(kc == 0), stop=(kc == NKC - 1)) for kc in range(NKC)],
                    reads=[w.name] + x1keys, writes=[bank.name])
                tt = tl[tq % 2]
                tq += 1
                T.op("dve", lambda e, tt=tt, bank=bank: e.tensor_copy(out=tt[:], in_=bank[0:3, :]), reads=[bank.name], writes=[tt.name])
                T.dma("sp", tt.name, O["o_conv"][si * 3:(si + 1) * 3, cg * 512:(cg + 1) * 512], tt[:], reads=[tt.name])
        T.flush()


class _Stop(Exception):
    pass


def phase_mix(nc, T, cfg, pb, S, W, I, O, name):
    try:
        _phase_mix(nc, T, cfg, pb, S, W, I, O, name)
    except _Stop:
        pass


def _phase_mix(nc, T, cfg, pb, S, W, I, O, name):
    R = cfg.R

    def chk(n):
        if cfg.mix_stop == n:
            T.flush()
            raise _Stop()
    NH = 4
    HW = NH * 64
    NSTR = 2
    with ExitStack() as st:
        sb = _sbuf(nc, st, name)
        bq = [0]
        bpool = [list(range(8))]

        def bank():
            p = bpool[0]
            b = pb[p[bq[0] % len(p)]]
            bq[0] += 1
            return b

        kpref = [""]
        LOCALK = set(["ew", "av", "kkn", "kp", "Ep", "Em", "Wp", "rt", "kt", "bt", "at", "gg", "tA", "tB", "Us", "Ys", "yo", "vb", "btb", "ktb",
                      "n2", "rn", "s1", "s2", "mean", "var", "bs"])

        class _TP:
            @staticmethod
            def _m(keys):
                return [kpref[0] + k if (isinstance(k, str) and k in LOCALK) else k for k in keys]

            @staticmethod
            def op(eng, fns, reads=(), writes=()):
                T.op(eng, fns, reads=_TP._m(reads), writes=_TP._m(writes))
        TP = _TP

        def TT(eng, out, in0, in1, op, r, w):
            TP.op(eng, lambda e: e.tensor_tensor(out=out, in0=in0, in1=in1, op=op), reads=r, writes=w)

        def TS(eng, out, in0, s1, s2, op0, op1, r, w):
            if s2 is None:
                TP.op(eng, lambda e: e.tensor_scalar(out=out, in0=in0, scalar1=s1, scalar2=None, op0=op0), reads=r, writes=w)
            else:
                TP.op(eng, lambda e: e.tensor_scalar(out=out, in0=in0, scalar1=s1, scalar2=s2, op0=op0, op1=op1), reads=r, writes=w)

        def STT(out, in0, sc, in1, op0, op1, r, w):
            TP.op("dve", lambda e: e.scalar_tensor_tensor(out=out, in0=in0, scalar=sc, in1=in1, op0=op0, op1=op1), reads=r, writes=w)

        def ACT(out, in_, func, r, w, bias=0.0, scale=1.0):
            TP.op("act", lambda e: e.activation(out=out, in_=in_, func=func, bias=bias, scale=scale), reads=r, writes=w)

        def CP(eng, out, in_, r, w):
            if eng == "act":
                TP.op("act", lambda e: e.copy(out=out, in_=in_), reads=r, writes=w)
            else:
                TP.op(eng, lambda e: e.tensor_copy(out=out, in_=in_), reads=r, writes=w)

        def RED(out, in_, r, w):
            TP.op("dve", lambda e: e.tensor_reduce(out=out, in_=in_, axis=AX.X, op=ALU.add), reads=r, writes=w)

        def RECIP(out, in_, r, w):
            TP.op("dve", lambda e: e.reciprocal(out=out, in_=in_), reads=r, writes=w)

        def MM(specs, r, w):
            TP.op("pe", [lambda e, sp=sp: e.matmul(sp[0], lhsT=sp[1], rhs=sp[2], start=sp[3], stop=sp[4]) for sp in specs], reads=r, writes=w)

        def TR(specs, r, w):
            TP.op("pe", [lambda e, sp=sp: e.transpose(out=sp[0], in_=sp[1], identity=sp[2]) for sp in specs], reads=r, writes=w)

        def LD(t, out, in_, w, r=()):
            T.dma("sp", t, out, in_, reads=r, writes=w)

        ident = sb("ident", [128, 128])
        make_ident(T, ident)
        tri = sb("tri", [64, 64])
        msl = sb("msl", [64, 2, 64])
        mgt = sb("mgt", [64, 64])
        ones = sb("ones", [64, 128])
        rowmask = sb("rowmask", [64, 1])

        def sel(t, ap, pattern, base, cm):
            T.op("pool", lambda e: e.memset(ap, 1.0), writes=[t.name])
            T.op("pool", lambda e: e.affine_select(out=ap, in_=ap, pattern=pattern, compare_op=ALU.is_ge, fill=0.0,
                                                   base=base, channel_multiplier=cm), reads=[t.name], writes=[t.name])
        sel(tri, tri[:], [[1, 64]], 0, -1)
        sel(msl, msl[:, 0, :], [[1, 64]], -1, -1)
        sel(msl, msl[:, 1, :], [[1, 64]], 0, -1)
        sel(mgt, mgt[:], [[-1, 64]], -1, 1)
        sel(rowmask, rowmask[:], [[0, 1]], -48, 1)
        T.op("pool", lambda e: e.memset(ones[:], 1.0), writes=[ones.name])

        T.begin_group("mix_setup")

        def bc_load(n, src, cols):
            t = sb(n, [64, cols])
            LD(t.name, t[:], src.partition_broadcast(64), [t.name])
            return t
        mu_bc = bc_load("mu_bc", W["rw_mu"], RW_COLS)
        w0_bc = bc_load("w0_bc", W["rw_w0"], RW_DIM)
        a0_bc = bc_load("a0_bc", W["rw_a0"], RW_DIM)
        kk_bc = bc_load("kk_bc", W["rw_kk"], RW_DIM)
        ka_bc = bc_load("ka_bc", W["rw_ka"], RW_DIM)
        rk_bc = bc_load("rk_bc", W["rw_rk"], RW_DIM)
        dtb_bc = bc_load("dtb_bc", W["dt_bias"], SSM_H)
        aneg_bc = bc_load("aneg_bc", W["a_log"], SSM_H)
        dsk_bc = bc_load("dsk_bc", W["d_skip"], SSM_H)
        gnw_bc = bc_load("gnw_bc", W["rw_gn_w"], RW_DIM)
        gnb_bc = bc_load("gnb_bc", W["rw_gn_b"], RW_DIM)
        w2 = sb("w2", [128, RW_DIM])
        a2 = sb("a2", [128, RW_DIM], BF16)
        g2 = sb("g2", [128, 2, RW_DIM], BF16)
        T.op("pool", lambda e: e.memset(w2[:], 0.0), writes=[w2.name])
        T.op("pool", lambda e: e.memset(a2[:], 0.0), writes=[a2.name])
        LD(w2.name, w2[0:96, :], W["rw_w2"][:, :], [w2.name])
        T.dma("pool", a2.name, a2[0:96, :], W["rw_a2"][:, :], writes=[a2.name])
        T.dma("pool", g2.name, g2[:], W["rw_g2"].rearrange("(c p) n -> p c n", p=128), writes=[g2.name])
        cw = sb("cw", [128, 24, 4])
        cb = sb("cb", [128, 24])
        snw = sb("snw", [128, 16])
        LD(cw.name, cw[:], W["conv_w"].rearrange("p (c i) -> p c i", i=4), [cw.name])
        LD(cb.name, cb[:], W["conv_b"][:, :], [cb.name])
        LD(snw.name, snw[:], W["ssm_norm_w"][:, :], [snw.name])
        T.end_group()
        ACT(aneg_bc[:], aneg_bc[:], AF.Exp, [aneg_bc.name], [aneg_bc.name])
        T.op("act", lambda e: e.mul(out=aneg_bc[:], in_=aneg_bc[:], mul=-1.0), reads=[aneg_bc.name], writes=[aneg_bc.name])
        XAB = sb("XAB", [128, 2, 24, 64])
        XA = XAB[:, 0]
        XB2 = XAB[:, 1]
        XAk, XBk = "XA", "XB2"
        histT = sb("histT", [128, 24, 9])
        hrow = XAB[0:9, 0].rearrange("p c t -> p (c t)")
        for half in range(2):
            LD("mix_hrow", hrow, I["hist_conv"][:, half * 1536:(half + 1) * 1536], [XAk])
            for q4 in range(3):
                bk = bank()
                TR([(bk[:, j * 9:(j + 1) * 9], hrow[:, (q4 * 4 + j) * 128:(q4 * 4 + j + 1) * 128], ident[0:9, 0:9]) for j in range(4)],
                   [XAk, ident.name], [bk.name])
                c0_ = half * 12 + q4 * 4
                CP("dve", histT[:, c0_:c0_ + 4, :], bk[:, 0:36].rearrange("p (j s) -> p j s", j=4), [bk.name], [histT.name])

        chk(1)
        ST = sb("ST", [64, RW_H, 64])
        STb = sb("STb", [64, RW_H, 64], BF16)
        HT = sb("HT", [128, SSM_H * 64])
        stg_ws = [sb(f"stg_w{i}", [64, 8, 64]) for i in range(NSTR)]
        stg_h = XAB[:].rearrange("p a c t -> p (a c t)")[:, 0:2048].rearrange("p (c n) -> p c n", c=16)
        SHk = [XAk, XBk]

        def load_states_rw(seq, sid):
            stg_w = stg_ws[sid]
            r0_ = seq * 1024 + sid * 512
            LD(stg_w.name, stg_w[:], I["wkv0"][r0_:r0_ + 512, :].rearrange("(h v) k -> v h k", v=64), [stg_w.name])
            bk = bank()
            TR([(bk[0:64, j * 64:(j + 1) * 64], stg_w[:, j, :], ident[0:64, 0:64]) for j in range(8)],
               [stg_w.name, ident.name], [bk.name])
            CP("act", ST[:, sid * 8:(sid + 1) * 8, :], bk[0:64, :].rearrange("p (h v) -> p h v", h=8), [bk.name],
               [(ST.name, 2 * sid), (ST.name, 2 * sid + 1)])
            CP("dve", STb[:, sid * 8:(sid + 1) * 8, :], bk[0:64, :].rearrange("p (h v) -> p h v", h=8), [bk.name],
               [(STb.name, 2 * sid), (STb.name, 2 * sid + 1)])

        def load_states_ssd(seq):
            LD("mix_stg_h", stg_h, I["ssm0"][seq * 2048:(seq + 1) * 2048, :].rearrange("(c p) n -> p c n", p=128), SHk)
            for g in range(4):
                bk = bank()
                TR([(bk[:, j * 128:(j + 1) * 128], stg_h[:, g * 4 + j, :], ident[:]) for j in range(4)],
                   SHk + [ident.name], [bk.name])
                CP("dve", HT[:, g * 512:(g + 1) * 512], bk[:, :], [bk.name], [(HT.name, g)])

        def store_states_rw(seq, sid):
            stg_w = stg_ws[sid]
            r0_ = seq * 1024 + sid * 512
            bk = bank()
            TR([(bk[0:64, j * 64:(j + 1) * 64], ST[:, sid * 8 + j, :], ident[0:64, 0:64]) for j in range(8)],
               [(ST.name, 2 * sid), (ST.name, 2 * sid + 1), ident.name], [bk.name])
            CP("act", stg_w[:], bk[0:64, :].rearrange("p (h k) -> p h k", h=8), [bk.name], [stg_w.name])
            T.dma("sp", stg_w.name + "_st", O["o_wkv"][r0_:r0_ + 512, :].rearrange("(h v) k -> v h k", v=64), stg_w[:],
                  reads=[stg_w.name])

        def store_states_ssd(seq):
            for g in range(4):
                bk = bank()
                TR([(bk[:, j * 128:(j + 1) * 128], HT[:, (g * 4 + j) * 128:(g * 4 + j + 1) * 128], ident[:]) for j in range(4)],
                   [(HT.name, g), ident.name], [bk.name])
                CP("dve", stg_h[:, g * 4:(g + 1) * 4, :], bk[:, :].rearrange("p (j n) -> p j n", j=4), [bk.name], SHk)
            T.dma("sp", "mix_stg_h_st", O["o_ssm"][seq * 2048:(seq + 1) * 2048, :].rearrange("(c p) n -> p c n", p=128), stg_h,
                  reads=SHk)

        for si, rend in enumerate(cfg.seq_end):
            T.dma("sp", f"{name}_shift", O["o_shift"][si:si + 1, :], S["P_RW"][rend:rend + 1, :])

        identb = sb("identb", [64, 64], BF16)
        CP("pool", identb[:], ident[0:64, 0:64], [ident.name], [identb.name])
        names = ["ew", "av", "kkn", "kp", "Ep", "Em", "Wp", "rt", "kt", "bt", "at", "gg", "tA", "tB", "Us", "Ys", "yo"]

        def mk_rw(i):
            p = f"r{i}_"
            return dict(
                cur_l=sb(p + "cur_l", [64, 448]), prev_l=sb(p + "prev_l", [64, 448]), lT=sb(p + "lT", [128, 64]),
                lTb=sb(p + "lTb", [128, 3, 64], BF16),
                cur3=[sb(p + f"cur3{j}", [64, 3, HW]) for j in range(2)], prev3=[sb(p + f"prev3{j}", [64, 3, HW]) for j in range(2)],
                W_={n: sb(p + n, [64, HW], BF16 if n in ("Us", "vb", "btb", "ktb") else F32) for n in names + ["vb", "btb", "ktb"]},
                ARTb=sb(p + "ARTb", [64, NH, 2, 64], BF16),
                btT=sb(p + "btT", [64, NH, 64], BF16), ktT=sb(p + "ktT", [64, NH, 64], BF16),
                AB=sb(p + "AB", [64, NH, 2, 64], BF16), AK=sb(p + "AK", [64, NH, 2, 64], BF16),
                Pm=[sb(p + f"Pm{j}", [64, NH, 64], BF16) for j in range(2)],
                Qm=[sb(p + f"Qm{j}", [64, NH, 64], BF16) for j in range(2)],
                XT=sb(p + "XT", [64, NH, 64], BF16), RHSb=sb(p + "RHSb", [64, HW], BF16), WC=sb(p + "WC", [64, NH]),
                sm={n: sb(p + n, [64, NH]) for n in ["n2", "rn", "s1", "s2", "mean", "var", "bs"]},
                yT=[sb(p + f"yT{j}", [128, HW // 128, 64], BF16) for j in range(2)])
        RWB = [mk_rw(i) for i in range(NSTR)]
        Btm = sb("Btm", [64, 512], BF16)
        pdt = sb("pdt", [64, SSM_H])
        dts = {n: sb(n, [64, SSM_H]) for n in ["dt", "adt", "acs", "eacs", "toend"]}
        cdec = sb("cdec", [128, SSM_H])
        gn = ["xs", "zz", "yy", "xdt", "xdtw", "ML", "EX", "MT", "t1"]
        G_ = {n: sb("g_" + n, [64, 512], BF16 if n in ("xdt", "xdtw", "MT") else F32) for n in gn}
        cbm = sb("cbm", [64, 64])
        ss1 = sb("ss1", [64, 1])
        yT2s = [sb(f"yT2{j}", [128, 4, 64], BF16) for j in range(2)]
        ZZ = [G_["zz"], sb("g_zz1", [64, 512])]
        prw3 = S["P_RW"][:, 0:3072].rearrange("t (j c) -> t j c", j=3)
        hrw3 = I["hist_rw"][:, 0:3072].rearrange("t (j c) -> t j c", j=3)
        yrwT = S["YRWT"].rearrange("(c p) r -> p c r", p=128)
        yssT = S["YSSMT"].rearrange("(c p) r -> p c r", p=128)
        xbcv = S["PT_XBC"].rearrange("(c p) r -> p c r", p=128)

        def h3(ap):
            return ap.rearrange("p (h k) -> p h k", h=NH)

        def h3s(ap):
            return ap.rearrange("p (h k) -> p h k", h=8)

        def bc8s(ap):
            return ap.unsqueeze(2).to_broadcast([64, 8, 64])

        def bc8(ap):
            return ap.unsqueeze(2).to_broadcast([64, NH, 64])

        def rwkv_chunk(c, sid):
            Bf = RWB[sid]
            cur_l, prev_l, lT, W_ = Bf["cur_l"], Bf["prev_l"], Bf["lT"], Bf["W_"]
            lTb = Bf["lTb"]
            btT, ktT, AB, AK, Pm, Qm = Bf["btT"], Bf["ktT"], Bf["AB"], Bf["AK"], Bf["Pm"], Bf["Qm"]
            ARTb = Bf["ARTb"]
            XT, RHSb, WC, sm = Bf["XT"], Bf["RHSb"], Bf["WC"], Bf["sm"]
            t0 = c * 64
            seq = min(c, 2)
            short = c < 3
            if c < 3:
                load_states_rw(seq, sid)
            LD(cur_l.name, cur_l[:], S["P_RW"][1 + t0:1 + t0 + 64, 3072:3520], [cur_l.name])
            LD(prev_l.name, prev_l[:], S["P_RW"][t0:t0 + 64, 3072:3520], [prev_l.name])
            if short:
                LD(prev_l.name, prev_l[48:49, :], I["hist_rw"][seq:seq + 1, 3072:3520], [prev_l.name])
            TT("dve", prev_l[:], prev_l[:], cur_l[:], ALU.subtract, [prev_l.name, cur_l.name], [prev_l.name])
            TT("pool", prev_l[:], prev_l[:], mu_bc[:, 3072:3520], ALU.mult, [prev_l.name, mu_bc.name], [prev_l.name])
            TT("dve", cur_l[:], cur_l[:], prev_l[:], ALU.add, [prev_l.name, cur_l.name], [cur_l.name])
            ACT(cur_l[:, 0:96], cur_l[:, 0:96], AF.Tanh, [cur_l.name], [cur_l.name])
            ACT(cur_l[:, 192:448], cur_l[:, 192:448], AF.Sigmoid, [cur_l.name], [cur_l.name])
            bk = bank()
            TR([(bk[:, j * 64:(j + 1) * 64], cur_l[:, o_:o_ + 128], ident[0:64, 0:64]) for j, o_ in enumerate((0, 96, 192, 320))],
               [cur_l.name, ident.name], [bk.name])
            CP("act", lT[:], bk[:, 0:64], [bk.name], [lT.name])
            CP("dve", lTb[:], bk[:, 64:256].rearrange("p (a t) -> p a t", a=3), [bk.name], [lTb.name])
            for qq in range(2):
                hh = 2 * sid + qq
                cur3, prev3, yT = Bf["cur3"][qq], Bf["prev3"][qq], Bf["yT"][qq]
                cs = hh * HW
                h0 = hh * NH
                w = W_
                LD(cur3.name, cur3[:], prw3[1 + t0:1 + t0 + 64, :, cs:cs + HW], [cur3.name])
                LD(prev3.name, prev3[:], prw3[t0:t0 + 64, :, cs:cs + HW], [prev3.name])
                if short:
                    LD(prev3.name, prev3[48:49, :, :], hrw3[seq:seq + 1, :, cs:cs + HW], [prev3.name])
                mu3 = mu_bc[:, 0:3072].rearrange("p (j c) -> p j c", j=3)[:, :, cs:cs + HW]
                TT("dve", prev3[:], prev3[:], cur3[:], ALU.subtract, [prev3.name, cur3.name], [prev3.name])
                TT("pool", prev3[:], prev3[:], mu3, ALU.mult, [prev3.name, mu_bc.name], [prev3.name])
                TT("dve", cur3[:], cur3[:], prev3[:], ALU.add, [prev3.name, cur3.name], [cur3.name])
                r_, k_, v_ = cur3[:, 0, :], cur3[:, 1, :], cur3[:, 2, :]
                c3 = [cur3.name]
                b_lw, b_la, b_lg = bank(), bank(), bank()
                MM([(b_lw[0:64, 0:HW], lT[:], w2[:, cs:cs + HW], True, True)], [lT.name, w2.name], [b_lw.name])
                MM([(b_la[0:64, 0:HW], lTb[:, 0, :], a2[:, cs:cs + HW], True, True)], [lTb.name, a2.name], [b_la.name])
                MM([(b_lg[0:64, 0:HW], lTb[:, 1, :], g2[:, 0, cs:cs + HW], True, False),
                    (b_lg[0:64, 0:HW], lTb[:, 2, :], g2[:, 1, cs:cs + HW], False, True)], [lTb.name, g2.name], [b_lg.name])
                CP("act", w["gg"][:], b_lg[0:64, 0:HW], [b_lg.name], ["gg"])
                TT("dve", w["tA"][:], b_lw[0:64, 0:HW], w0_bc[:, cs:cs + HW], ALU.add, [b_lw.name, w0_bc.name], ["tA"])
                ACT(w["tA"][:], w["tA"][:], AF.Exp, ["tA"], ["tA"], scale=-1.0)
                ACT(w["tA"][:], w["tA"][:], AF.Ln, ["tA"], ["tA"], bias=1.0)
                ACT(w["ew"][:], w["tA"][:], AF.Exp, ["tA"], ["ew"], bias=-0.5, scale=-1.0)
                if short:
                    TS("dve", w["ew"][:], w["ew"][:], rowmask[:, 0:1], None, ALU.mult, None, ["ew", rowmask.name], ["ew"])
                TT("dve", w["av"][:], b_la[0:64, 0:HW], a0_bc[:, cs:cs + HW], ALU.add, [b_la.name, a0_bc.name], ["av"])
                ACT(w["av"][:], w["av"][:], AF.Sigmoid, ["av"], ["av"])
                TT("pool", w["kkn"][:], k_, kk_bc[:, cs:cs + HW], ALU.mult, c3 + [kk_bc.name], ["kkn"])
                ACT(w["tB"][:], w["kkn"][:], AF.Square, ["kkn"], ["tB"])
                RED(sm["n2"][:], h3(w["tB"][:]), ["tB"], ["n2"])
                TS("dve", sm["n2"][:], sm["n2"][:], 1e-24, None, ALU.max, None, ["n2"], ["n2"])
                ACT(sm["n2"][:], sm["n2"][:], AF.Ln, ["n2"], ["n2"])
                ACT(sm["rn"][:], sm["n2"][:], AF.Exp, ["n2"], ["rn"], scale=-0.5)
                TT("dve", h3(w["kkn"][:]), h3(w["kkn"][:]), bc8(sm["rn"][:]), ALU.mult, ["kkn", "rn"], ["kkn"])
                if short:
                    TS("dve", w["kkn"][:], w["kkn"][:], rowmask[:, 0:1], None, ALU.mult, None, ["kkn", rowmask.name], ["kkn"])
                STT(w["tB"][:], w["av"][:], -1.0, ka_bc[:, cs:cs + HW], ALU.add, ALU.mult, ["av", ka_bc.name], ["tB"])
                STT(w["kp"][:], w["tB"][:], 1.0, k_, ALU.add, ALU.mult, ["tB"] + c3, ["kp"])
                if short:
                    TS("dve", w["kp"][:], w["kp"][:], rowmask[:, 0:1], None, ALU.mult, None, ["kp", rowmask.name], ["kp"])
                b_cn = bank()
                MM([(b_cn[0:64, 0:HW], tri[:], w["ew"][:], True, True)], [tri.name, "ew"], [b_cn.name])
                b_wc = bank()
                MM([(b_wc[0:64, j:j + 1], w["ew"][:, j * 64:(j + 1) * 64], ones[:, 0:1], True, True) for j in range(NH)],
                   ["ew", ones.name], [b_wc.name])
                ACT(WC[:], b_wc[0:64, 0:NH], AF.Exp, [b_wc.name], [WC.name], scale=-1.0)
                ACT(w["Ep"][:], b_cn[0:64, 0:HW], AF.Exp, [b_cn.name], ["Ep"], scale=-1.0)
                ACT(w["Em"][:], b_cn[0:64, 0:HW], AF.Exp, [b_cn.name], ["Em"])
                TT("dve", w["tA"][:], b_cn[0:64, 0:HW], w["ew"][:], ALU.subtract, [b_cn.name, "ew"], ["tA"])
                ACT(w["Wp"][:], w["tA"][:], AF.Exp, ["tA"], ["Wp"], scale=-1.0)
                TT("dve", w["rt"][:], r_, w["Ep"][:], ALU.mult, c3 + ["Ep"], ["rt"])
                TT("pool", w["kt"][:], w["kp"][:], w["Em"][:], ALU.mult, ["kp", "Em"], ["kt"])
                TT("pool", w["bt"][:], w["kkn"][:], w["av"][:], ALU.mult, ["kkn", "av"], ["bt"])
                TT("pool", w["bt"][:], w["bt"][:], w["Em"][:], ALU.mult, ["bt", "Em"], ["bt"])
                STT(w["at"][:], w["kkn"][:], -1.0, w["Wp"][:], ALU.mult, ALU.mult, ["kkn", "Wp"], ["at"])
                i64 = ident[0:64, 0:64]
                for src, dst, dkey, eng in (("at", ARTb[:, :, 0, :], (ARTb.name, 0), "act"), ("rt", ARTb[:, :, 1, :], (ARTb.name, 1), "dve"),
                                            ("bt", btT[:], btT.name, "act"), ("kt", ktT[:], ktT.name, "dve")):
                    bk = bank()
                    TR([(bk[0:64, j * 64:(j + 1) * 64], w[src][:, j * 64:(j + 1) * 64], i64) for j in range(NH)], [src, ident.name], [bk.name])
                    CP(eng, dst, bk[0:64, 0:HW].rearrange("p (h t) -> p h t", h=NH), [bk.name], [dkey])
                ARTbk = [(ARTb.name, 0), (ARTb.name, 1)]
                CP("act", w["vb"][:], v_, c3, ["vb"])
                CP("act", w["btb"][:], w["bt"][:], ["bt"], ["btb"])
                CP("act", w["ktb"][:], w["kt"][:], ["kt"], ["ktb"])
                for lhs, lkey, dstt in ((btT, btT.name, AB), (ktT, ktT.name, AK)):
                    for hb in range(NH // 4):
                        bk = bank()
                        MM([(bk[0:64, j * 128:(j + 1) * 128], lhs[:, hb * 4 + j, :], ARTb[:, hb * 4 + j, :, :].rearrange("p a t -> p (a t)"), True, True)
                            for j in range(4)], [lkey] + ARTbk, [bk.name])
                        TT("dve", dstt[:, hb * 4:(hb + 1) * 4, :, :], bk[0:64, :].rearrange("p (h a t) -> p h a t", h=4, a=2),
                           msl[:].unsqueeze(1).to_broadcast([64, 4, 2, 64]), ALU.mult, [bk.name, msl.name], [(dstt.name, hb)])
                ABk = [(AB.name, i_) for i_ in range(NH // 4)]
                AKk = [(AK.name, i_) for i_ in range(NH // 4)]
                bk = bank()
                MM([(bk[0:64, j * 64:(j + 1) * 64], ARTb[:, j, 0, :], btT[:, j, :], True, True) for j in range(NH)], ARTbk + [btT.name], [bk.name])
                TT("dve", Pm[0][:], bk[0:64, 0:HW].rearrange("p (h s) -> p h s", h=NH), mgt[:].unsqueeze(1).to_broadcast([64, NH, 64]), ALU.mult,
                   [bk.name, mgt.name], [Pm[0].name])
                Q0 = AB[:, :, 0, :]
                TT("pool", XT[:], Q0, identb[:].unsqueeze(1).to_broadcast([64, NH, 64]), ALU.add, ABk + [identb.name], [XT.name])
                pi = 0
                for lvl in range(1, 6):
                    Pc, Qc, Pn, Qn = Pm[pi], Qm[pi], Pm[1 - pi], Qm[1 - pi]
                    Qcv = Q0 if lvl == 1 else Qc[:]
                    Qck = ABk if lvl == 1 else [Qc.name]
                    bp = bank()
                    MM([(bp[0:64, j * 64:(j + 1) * 64], Qcv[:, j, :], Pc[:, j, :], True, True) for j in range(NH)], [Pc.name] + Qck, [bp.name])
                    CP("act", Pn[:], bp[0:64, 0:HW].rearrange("p (h s) -> p h s", h=NH), [bp.name], [Pn.name])
                    if lvl < 5:
                        bq_ = bank()
                        MM([(bq_[0:64, j * 64:(j + 1) * 64], Pc[:, j, :], Qcv[:, j, :], True, True) for j in range(NH)], [Pc.name] + Qck, [bq_.name])
                        CP("dve", Qn[:], bq_[0:64, 0:HW].rearrange("p (h s) -> p h s", h=NH), [bq_.name], [Qn.name])
                    bz = bank()
                    MM([(bz[0:64, j * 64:(j + 1) * 64], Pn[:, j, :], XT[:, j, :], True, True) for j in range(NH)], [Pn.name, XT.name], [bz.name])
                    TT("dve", XT[:], XT[:], bz[0:64, 0:HW].rearrange("p (h s) -> p h s", h=NH), ALU.add, [XT.name, bz.name], [XT.name])
                    pi = 1 - pi
                STk = (ST.name, hh)
                STbk = (STb.name, hh)
                bk = bank()
                sp_ = []
                for j in range(NH):
                    o = bk[0:64, j * 64:(j + 1) * 64]
                    sp_.append((o, ARTb[:, j, 0, :], STb[:, h0 + j, :], True, False))
                    sp_.append((o, AK[:, j, 0, :], w["vb"][:, j * 64:(j + 1) * 64], False, True))
                MM(sp_, ARTbk + AKk + ["vb", STbk], [bk.name])
                CP("act", RHSb[:], bk[0:64, 0:HW], [bk.name], [RHSb.name])
                bk = bank()
                MM([(bk[0:64, j * 64:(j + 1) * 64], XT[:, j, :], RHSb[:, j * 64:(j + 1) * 64], True, True) for j in range(NH)],
                   [XT.name, RHSb.name], [bk.name])
                CP("dve", w["Us"][:], bk[0:64, 0:HW], [bk.name], ["Us"])
                by = bank()
                sp_ = []
                for j in range(NH):
                    o = by[0:64, j * 64:(j + 1) * 64]
                    sp_.append((o, ARTb[:, j, 1, :], STb[:, h0 + j, :], True, False))
                    sp_.append((o, AB[:, j, 1, :], w["Us"][:, j * 64:(j + 1) * 64], False, False))
                    sp_.append((o, AK[:, j, 1, :], w["vb"][:, j * 64:(j + 1) * 64], False, True))
                MM(sp_, ARTbk + ABk + AKk + ["vb", STbk, "Us"], [by.name])
                bs_ = bank()
                sp_ = []
                for j in range(NH):
                    o = bs_[0:64, j * 64:(j + 1) * 64]
                    sp_.append((o, w["btb"][:, j * 64:(j + 1) * 64], w["Us"][:, j * 64:(j + 1) * 64], True, False))
                    sp_.append((o, w["ktb"][:, j * 64:(j + 1) * 64], w["vb"][:, j * 64:(j + 1) * 64], False, True))
                MM(sp_, ["btb", "ktb", "Us", "vb"], [bs_.name])
                STh = ST[:, h0:h0 + NH, :]
                TT("dve", STh, STh, bs_[0:64, 0:HW].rearrange("p (h v) -> p h v", h=NH), ALU.add, [STk, bs_.name], [STk])
                TT("pool", STh, STh, bc8(WC[:]), ALU.mult, [STk, WC.name], [STk])
                CP("act", STb[:, h0:h0 + NH, :], STh, [STk], [STbk])
                CP("act", w["Ys"][:], by[0:64, 0:HW], [by.name], ["Ys"])
                RED(sm["s1"][:], h3(w["Ys"][:]), ["Ys"], ["s1"])
                ACT(w["tA"][:], w["Ys"][:], AF.Square, ["Ys"], ["tA"])
                RED(sm["s2"][:], h3(w["tA"][:]), ["tA"], ["s2"])
                TS("dve", sm["mean"][:], sm["s1"][:], 1.0 / 64, None, ALU.mult, None, ["s1"], ["mean"])
                TT("dve", sm["var"][:], sm["mean"][:], sm["mean"][:], ALU.mult, ["mean"], ["var"])
                STT(sm["var"][:], sm["s2"][:], 1.0 / 64, sm["var"][:], ALU.mult, ALU.subtract, ["s2", "var"], ["var"])
                ACT(sm["var"][:], sm["var"][:], AF.Ln, ["var"], ["var"], bias=RW_GN_EPS)
                ACT(sm["var"][:], sm["var"][:], AF.Exp, ["var"], ["var"], scale=-0.5)
                TT("dve", h3(w["Ys"][:]), h3(w["Ys"][:]), bc8(sm["mean"][:]), ALU.subtract, ["Ys", "mean"], ["Ys"])
                TT("pool", h3(w["Ys"][:]), h3(w["Ys"][:]), bc8(sm["var"][:]), ALU.mult, ["Ys", "var"], ["Ys"])
                TT("pool", w["Ys"][:], w["Ys"][:], gnw_bc[:, cs:cs + HW], ALU.mult, ["Ys", gnw_bc.name], ["Ys"])
                TT("pool", w["Ys"][:], w["Ys"][:], gnb_bc[:, cs:cs + HW], ALU.add, ["Ys", gnb_bc.name], ["Ys"])
                TT("pool", w["tB"][:], r_, w["kp"][:], ALU.mult, c3 + ["kp"], ["tB"])
                TT("pool", w["tB"][:], w["tB"][:], rk_bc[:, cs:cs + HW], ALU.mult, ["tB", rk_bc.name], ["tB"])
                RED(sm["bs"][:], h3(w["tB"][:]), ["tB"], ["bs"])
                TT("dve", h3(w["tB"][:]), h3(v_), bc8(sm["bs"][:]), ALU.mult, c3 + ["bs"], ["tB"])
                TT("pool", w["Ys"][:], w["Ys"][:], w["tB"][:], ALU.add, ["Ys", "tB"], ["Ys"])
                TT("dve", w["yo"][:], w["Ys"][:], w["gg"][:], ALU.mult, ["Ys", "gg"], ["yo"])
                bk = bank()
                NJ = HW // 128
                TR([(bk[:, j * 64:(j + 1) * 64], w["yo"][:, j * 128:(j + 1) * 128], i64) for j in range(NJ)], ["yo", ident.name], [bk.name])
                CP("act", yT[:], bk[:, 0:NJ * 64].rearrange("p (j t) -> p j t", j=NJ), [bk.name], [yT.name])
                T.dma("act", yT.name, yrwT[:, hh * NJ:(hh + 1) * NJ, t0:t0 + 64], yT[:], reads=[yT.name])
            if c in (0, 1, cfg.nchunks - 1):
                store_states_rw(seq, sid)

        def ssd_chunk(c):
            t0 = c * 64
            seq = min(c, 2)
            short = c < 3
            if c < 3:
                load_states_ssd(seq)
            LD("mix_Fin", XB2, xbcv[:, :, 4 + t0:4 + t0 + 64], [XBk])
            LD(pdt.name, pdt[:], S["P_DT"][t0:t0 + 64, :], [pdt.name])
            XB = XB2
            d = dts
            TT("dve", d["dt"][:], pdt[:], dtb_bc[:], ALU.add, [pdt.name, dtb_bc.name], ["dt"])
            ACT(d["dt"][:], d["dt"][:], AF.Exp, ["dt"], ["dt"])
            ACT(d["dt"][:], d["dt"][:], AF.Ln, ["dt"], ["dt"], bias=1.0)
            if short:
                TS("dve", d["dt"][:], d["dt"][:], rowmask[:, 0:1], None, ALU.mult, None, ["dt", rowmask.name], ["dt"])
            TT("dve", d["adt"][:], d["dt"][:], aneg_bc[:], ALU.mult, ["dt", aneg_bc.name], ["adt"])
            b_ac = bank()
            MM([(b_ac[0:64, 0:SSM_H], tri[:], d["adt"][:], True, True)], [tri.name, "adt"], [b_ac.name])
            b_tot = bank()
            MM([(b_tot[:, 0:SSM_H], ones[:], d["adt"][:], True, True)], [ones.name, "adt"], [b_tot.name])
            CP("dve", d["acs"][:], b_ac[0:64, 0:SSM_H], [b_ac.name], ["acs"])
            ACT(d["eacs"][:], b_ac[0:64, 0:SSM_H], AF.Exp, [b_ac.name], ["eacs"])
            ACT(cdec[:], b_tot[:, 0:SSM_H], AF.Exp, [b_tot.name], [cdec.name])
            TT("dve", d["toend"][:], b_tot[0:64, 0:SSM_H], d["acs"][:], ALU.subtract, [b_tot.name, "acs"], ["toend"])
            ACT(d["toend"][:], d["toend"][:], AF.Exp, ["toend"], ["toend"])
            bk = bank()
            TR([(bk[0:64, j * 128:(j + 1) * 128], XB[:, 16 + j, :], ident[:]) for j in range(4)], [XBk, ident.name], [bk.name])
            CP("act", Btm[:], bk[0:64, :], [bk.name], [Btm.name])
            g_ = G_
            for g in range(4):
                hs = slice(8 * g, 8 * g + 8)
                yT2 = yT2s[g % 2]
                zz = ZZ[g % 2]
                HTk = (HT.name, g)
                HTg = HT[:, g * 512:(g + 1) * 512]
                bk = bank()
                TR([(bk[0:64, j * 128:(j + 1) * 128], XB[:, 4 * g + j, :], ident[:]) for j in range(4)], [XBk, ident.name], [bk.name])
                CP("act", g_["xs"][:], bk[0:64, :], [bk.name], ["xs"])
                LD(zz.name, zz[:], S["P_Z"][t0:t0 + 64, g * 512:(g + 1) * 512], [zz.name])
                TT("dve", h3s(g_["xdt"][:]), h3s(g_["xs"][:]), bc8s(d["dt"][:, hs]), ALU.mult, ["xs", "dt"], ["xdt"])
                TT("pool", h3s(g_["xdtw"][:]), h3s(g_["xdt"][:]), bc8s(d["toend"][:, hs]), ALU.mult, ["xdt", "toend"], ["xdtw"])
                bcb = bank()
                MM([(bcb[0:64, 0:64], XB[:, 16 + g, :], XB[:, 20 + g, :], True, True)], [XBk], [bcb.name])
                TT("dve", cbm[:], bcb[0:64, 0:64], msl[:, 1, :], ALU.mult, [bcb.name, msl.name], [cbm.name])
                TT("pool", h3s(g_["ML"][:]), bc8s(d["adt"][:, hs]), mgt[:].unsqueeze(1).to_broadcast([64, 8, 64]), ALU.mult,
                   ["adt", mgt.name], ["ML"])
                bsg = bank()
                MM([(bsg[0:64, j * 64:(j + 1) * 64], g_["ML"][:, j * 64:(j + 1) * 64], tri[:], True, True) for j in range(8)],
                   ["ML", tri.name], [bsg.name])
                ACT(g_["EX"][:], bsg[0:64, :], AF.Exp, [bsg.name], ["EX"])
                TT("pool", h3s(g_["MT"][:]), h3s(g_["EX"][:]), cbm[:].unsqueeze(1).to_broadcast([64, 8, 64]), ALU.mult, ["EX", cbm.name], ["MT"])
                byd = bank()
                MM([(byd[0:64, j * 64:(j + 1) * 64], g_["MT"][:, j * 64:(j + 1) * 64], g_["xdt"][:, j * 64:(j + 1) * 64], True, True)
                    for j in range(8)], ["MT", "xdt"], [byd.name])
                byo = bank()
                MM([(byo[0:64, :], XB[:, 20 + g, :], HTg, True, True)], [XBk, HTk], [byo.name])
                TT("dve", h3s(g_["yy"][:]), byo[0:64, :].rearrange("p (h k) -> p h k", h=8), bc8s(d["eacs"][:, hs]), ALU.mult,
                   [byo.name, "eacs"], ["yy"])
                TT("dve", g_["yy"][:], g_["yy"][:], byd[0:64, :], ALU.add, ["yy", byd.name], ["yy"])
                TT("pool", h3s(g_["t1"][:]), h3s(g_["xs"][:]), bc8s(dsk_bc[:, hs]), ALU.mult, ["xs", dsk_bc.name], ["t1"])
                TT("pool", g_["yy"][:], g_["yy"][:], g_["t1"][:], ALU.add, ["yy", "t1"], ["yy"])
                TT("dve", g_["yy"][:], g_["yy"][:], zz[:], ALU.mult, ["yy", zz.name], ["yy"])
                ACT(g_["t1"][:], g_["yy"][:], AF.Square, ["yy"], ["t1"])
                T.op("dve", lambda e, o=ss1[:], i=g_["t1"][:]: e.tensor_reduce(out=o, in_=i, axis=AX.X, op=ALU.add), reads=["t1"], writes=[ss1.name])
                ACT(ss1[:], ss1[:], AF.Ln, [ss1.name], [ss1.name], bias=RMS_EPS, scale=1.0 / 512)
                ACT(ss1[:], ss1[:], AF.Exp, [ss1.name], [ss1.name], scale=-0.5)
                TS("dve", g_["yy"][:], g_["yy"][:], ss1[:, 0:1], None, ALU.mult, None, ["yy", ss1.name], ["yy"])
                bk = bank()
                TR([(bk[:, j * 64:(j + 1) * 64], g_["yy"][:, j * 128:(j + 1) * 128], ident[0:64, 0:64]) for j in range(4)], ["yy", ident.name], [bk.name])
                TT("dve", yT2[:], bk[:, 0:256].rearrange("p (j t) -> p j t", j=4), snw[:, 4 * g:4 * g + 4].unsqueeze(2).to_broadcast([128, 4, 64]),
                   ALU.mult, [bk.name, snw.name], [yT2.name])
                T.dma("pool", yT2.name, yssT[:, 4 * g:4 * g + 4, t0:t0 + 64], yT2[:], reads=[yT2.name])
                bst = bank()
                MM([(bst[:, :], Btm[:, g * 128:(g + 1) * 128], g_["xdtw"][:], True, True)], [Btm.name, "xdtw"], [bst.name])
                TT("pool", HTg.rearrange("p (h k) -> p h k", h=8), HTg.rearrange("p (h k) -> p h k", h=8),
                   cdec[:, hs].unsqueeze(2).to_broadcast([128, 8, 64]), ALU.mult, [HTk, cdec.name], [HTk])
                TT("dve", HTg, HTg, bst[:, :], ALU.add, [HTk, bst.name], [HTk])
            if c in (0, 1, cfg.nchunks - 1):
                store_states_ssd(seq)

        def rw_stream(sid):
            bpool[0] = [3 * sid, 3 * sid + 1, 3 * sid + 2]
            kpref[0] = f"s{sid}:"
            for c in range(cfg.nchunks):
                rwkv_chunk(c, sid)
            kpref[0] = ""

        def ssd_stream():
            bpool[0] = [6, 7]
            for c in range(cfg.nchunks):
                ssd_chunk(c)

        streams = [T.capture(lambda i=i: rw_stream(i)) for i in range(NSTR)]
        streams.append(T.capture(ssd_stream))
        T.replay_merged(streams)
        T.flush()


def phase_outp(nc, T, cfg, pb, S, W, name):
    TB = cfg.tile_blocks
    TT = TB * 128
    npc = (TT + 511) // 512
    pw = TT // npc
    GATE0 = 8672
    with ExitStack() as st:
        sb = _sbuf(nc, st, name)
        yrT = sb("yrT", [128, 8, TT], BF16)
        ysT = sb("ysT", [128, NKC, TT], BF16)
        x1T = sb("x1T", [128, NKC, TT], BF16)
        mT = sb("mT", [128, NKC, TT], BF16)
        bg = sb("bg", [128, 32])
        wra = [sb(f"wra{i}", [128, 8, 256], BF16) for i in range(2)]
        wsa = [sb(f"wsa{i}", [128, NKC, 256], BF16) for i in range(2)]
        wga = [sb(f"wga{i}", [128, NKC, 256], BF16) for i in range(2)]
        wgb = [sb(f"wgb{i}", [128, NKC, 256], BF16) for i in range(2)]
        wo = [sb(f"wo{i}", [128, NKC, 512], BF16) for i in range(2)]
        sga = [sb(f"sga{i}", [128, pw]) for i in range(2)]
        sgb = [sb(f"sgb{i}", [128, pw]) for i in range(2)]
        xr = sb("xr", [128, TB, 512])
        so = [sb(f"so{i}", [128, 512]) for i in range(2)]
        T.dma("sp", bg.name, bg[:], W["b_gate"][:, :], writes=[bg.name])
        yrv = S["YRWT"].rearrange("(c p) r -> p c r", p=128)
        ysv = S["YSSMT"].rearrange("(c p) r -> p c r", p=128)
        x1v = S["X1T"].rearrange("(c p) r -> p c r", p=128)
        wrv = W["w_rw_out"].rearrange("(c p) n -> p c n", p=128)
        wsv = W["w_ssm_out"].rearrange("(c p) n -> p c n", p=128)
        wiv = W["w_in"].rearrange("(c p) n -> p c n", p=128)
        wov = W["w_out"].rearrange("(c p) n -> p c n", p=128)
        q = 0
        xq = 0
        for t in range(cfg.ntiles):
            r0 = t * TT
            T.dma("sp", yrT.name, yrT[:], yrv[:, :, r0:r0 + TT], writes=[yrT.name])
            T.dma("sp", ysT.name, ysT[:], ysv[:, :, r0:r0 + TT], writes=[ysT.name])
            T.dma("sp", x1T.name, x1T[:], x1v[:, :, r0:r0 + TT], writes=[x1T.name])
            for cc in range(NKC):
                s = (cc // 2) % 2
                cj = cc % 2
                if cj == 0:
                    c0 = (cc // 2) * 256
                    cg_ = cc // 2
                    srcs = ((wra[s], wrv[:, :, c0:c0 + 256], 0, 8), (wsa[s], wsv[:, :, c0:c0 + 256], 8, NKC),
                            (wga[s], wiv[:, :, GATE0 + c0:GATE0 + c0 + 256], 24, NKC),
                            (wgb[s], wiv[:, :, GATE0 + D + c0:GATE0 + D + c0 + 256], 40, NKC))
                    for wt, src_ap, k0, nk in srcs:
                        cview = S["WOC_B"][cg_].rearrange("p (k n) -> p k n", n=256)[:, k0:k0 + nk, :]
                        ck = ("woc", cg_, k0)
                        if t > 0:
                            T.dma("pool", wt.name, wt[:], cview, reads=[ck], writes=[wt.name])
                        else:
                            T.dma("pool", wt.name, wt[:], src_ap, writes=[wt.name])
                            if cfg.ntiles > 1:
                                T.dma("sp", wt.name + "_wb", cview, wt[:], reads=[wt.name], writes=[ck])
                for pc in range(npc):
                    ps = slice(pc * pw, (pc + 1) * pw)
                    banks = [pb[(4 * q + i) % 8] for i in range(4)]
                    sa, sb2 = sga[q % 2], sgb[q % 2]
                    q += 1
                    for bank, w, act, nk in ((banks[0], wra[s], yrT, 8), (banks[1], wsa[s], ysT, NKC),
                                             (banks[2], wga[s], x1T, NKC), (banks[3], wgb[s], x1T, NKC)):
                        T.op("pe", [lambda e, kc=kc, bank=bank, w=w, act=act, ps=ps, nk=nk, cj=cj: e.matmul(
                            bank[:, 0:pw], lhsT=w[:, kc, cj * 128:(cj + 1) * 128], rhs=act[:, kc, ps], start=(kc == 0), stop=(kc == nk - 1)) for kc in range(nk)],
                            reads=[w.name, act.name], writes=[bank.name])
                    T.op("act", lambda e, sa=sa, bank=banks[2], cc=cc: e.activation(out=sa[:], in_=bank[:, 0:pw], func=AF.Sigmoid, bias=bg[:, cc:cc + 1]),
                         reads=[banks[2].name, bg.name], writes=[sa.name])
                    T.op("act", lambda e, sb2=sb2, bank=banks[3], cc=cc: e.activation(out=sb2[:], in_=bank[:, 0:pw], func=AF.Sigmoid, bias=bg[:, 16 + cc:17 + cc]),
                         reads=[banks[3].name, bg.name], writes=[sb2.name])
                    T.op("dve", lambda e, sa=sa, bank=banks[0]: e.tensor_tensor(out=sa[:], in0=sa[:], in1=bank[:, 0:pw], op=ALU.mult),
                         reads=[sa.name, banks[0].name], writes=[sa.name])
                    T.op("dve", lambda e, sb2=sb2, bank=banks[1]: e.tensor_tensor(out=sb2[:], in0=sb2[:], in1=bank[:, 0:pw], op=ALU.mult),
                         reads=[sb2.name, banks[1].name], writes=[sb2.name])
                    T.op("pool", lambda e, sa=sa, sb2=sb2, cc=cc, ps=ps: e.tensor_tensor(out=mT[:, cc, ps], in0=sa[:], in1=sb2[:], op=ALU.add),
                         reads=[sa.name, sb2.name], writes=[(mT.name, cc, pc)])
            mkeys = [(mT.name, cc, pc) for cc in range(NKC) for pc in range(npc)]
            for cb in range(4):
                w = wo[cb % 2]
                T.dma("pool", w.name, w[:], wov[:, :, cb * 512:(cb + 1) * 512], writes=[w.name])
                T.dma("sp", xr.name, xr[:], S["X1"][r0:r0 + TT, cb * 512:(cb + 1) * 512].rearrange("(b p) n -> p b n", p=128), writes=[xr.name])
                for b in range(TB):
                    bank = pb[q % 8]
                    q += 1
                    T.op("pe", [lambda e, kc=kc, bank=bank, w=w, b=b: e.matmul(
                        bank[:, :], lhsT=mT[:, kc, b * 128:(b + 1) * 128], rhs=w[:, kc, :], start=(kc == 0), stop=(kc == NKC - 1))
                        for kc in range(NKC)], reads=[w.name] + mkeys, writes=[bank.name])
                    rows = r0 + b * 128
                    o = so[xq % 2]
                    xq += 1
                    T.op("dve", lambda e, bank=bank, b=b, o=o: e.scalar_tensor_tensor(
                        out=o[:], in0=xr[:, b, :], scalar=ALPHA, in1=bank[:, :], op0=ALU.mult, op1=ALU.add),
                        reads=[bank.name, xr.name], writes=[o.name])
                    T.dma("sp", o.name, S["XPRE"][rows:rows + 128, cb * 512:(cb + 1) * 512], o[:], reads=[o.name])
        T.flush()


WEIGHT_SPECS = [
    ("ffn1_gu", [D, 2 * DFF]), ("ffn1_dn", [DFF, D]), ("ln1_g", [1, D]), ("ln1_b", [1, D]),
    ("w_in", [D, N_IN]), ("b_gate", [128, 32]),
    ("rw_mu", [1, RW_COLS]), ("rw_w0", [1, RW_DIM]), ("rw_w2", [96, RW_DIM]), ("rw_a0", [1, RW_DIM]),
    ("rw_a2", [96, RW_DIM]), ("rw_g2", [256, RW_DIM]), ("rw_kk", [1, RW_DIM]), ("rw_ka", [1, RW_DIM]),
    ("rw_rk", [1, RW_DIM]), ("rw_gn_w", [1, RW_DIM]), ("rw_gn_b", [1, RW_DIM]),
    ("conv_w", [128, 24 * 4]), ("conv_b", [128, 24]), ("dt_bias", [1, SSM_H]), ("a_log", [1, SSM_H]),
    ("d_skip", [1, SSM_H]), ("ssm_norm_w", [128, 16]),
    ("w_rw_out", [RW_DIM, D]), ("w_ssm_out", [D, D]), ("w_out", [D, D]), ("ln2_g", [1, D]), ("ln2_b", [1, D]),
    ("ffn2_gu", [D, 2 * DFF]), ("ffn2_dn", [DFF, D]), ("ln3_g", [1, D]), ("ln3_b", [1, D]),
]


def build_program(cfg):
    nc = bass.Bass("TRN2", target_bir_lowering=False)
    R = cfg.R
    dbg = "ExternalOutput" if cfg.debug else None

    def din(name, shape, dt=F32):
        return nc.dram_tensor(name, shape, dt, kind="ExternalInput").ap()

    def dout(name, shape, dt=F32):
        return nc.dram_tensor(name, shape, dt, kind="ExternalOutput").ap()

    def dscr(name, shape, dt=F32):
        if name in cfg.scratch_in:
            return nc.dram_tensor(name, shape, dt, kind="ExternalInput").ap()
        if dbg:
            return nc.dram_tensor(name, shape, dt, kind=dbg).ap()
        return nc.dram_tensor(name, shape, dt).ap()

    I = {}
    I["xin"] = din("xin", [R, D])
    I["xinT"] = din("xinT", [D, R])
    I["hist_rw"] = din("hist_rw", [3, RW_COLS])
    I["hist_conv"] = din("hist_conv", [9, CONV_DIM])
    I["wkv0"] = din("wkv0", [3 * RW_H * 64, 64])
    I["ssm0"] = din("ssm0", [3 * SSM_H * 64, SSM_N])
    W = {n: din(n, shp) for n, shp in WEIGHT_SPECS}
    O = {}
    O["yout"] = dout("yout", [R, D])
    O["o_shift"] = dout("o_shift", [3, RW_COLS])
    O["o_conv"] = dout("o_conv", [9, CONV_DIM])
    O["o_wkv"] = dout("o_wkv", [3 * RW_H * 64, 64])
    O["o_ssm"] = dout("o_ssm", [3 * SSM_H * 64, SSM_N])
    S = {}
    S["X0T"] = dscr("X0T", [D, R], BF16)
    S["XPRE"] = dscr("XPRE", [R, D])
    S["X1"] = dscr("X1", [R, D])
    S["X1T"] = dscr("X1T", [D, R], BF16)
    S["X2"] = dscr("X2", [R, D])
    S["X2T"] = dscr("X2T", [D, R], BF16)
    S["P_RW"] = dscr("P_RW", [1 + R, RW_COLS])
    S["P_Z"] = dscr("P_Z", [R, D])
    S["P_DT"] = dscr("P_DT", [R, SSM_H])
    S["PT_XBC"] = dscr("PT_XBC", [CONV_DIM, 4 + R])
    S["YRWT"] = dscr("YRWT", [RW_DIM, R], BF16)
    S["YSSMT"] = dscr("YSSMT", [D, R], BF16)
    S["WDN_B"] = nc.dram_tensor("WDN_B", [DFF, D], BF16).ap()
    S["WOC_B"] = nc.dram_tensor("WOC_B", [8, 128, 56 * 256], BF16).ap()

    with ExitStack() as st:
        T = Tracker(nc, st)
        pb = [st.enter_context(nc.psum_tensor(f"pb{i}", [128, 512], F32)) for i in range(8)]
        T.excl = set(b.name for b in pb)

        def run(name, kind, fn, *a):
            T.rename = (name, kind)
            fn(*a)
            return cfg.stop_after == name

        phases = [
            ("ffn1", "ffn", phase_ffn, (nc, T, cfg, pb, I["xinT"], I["xin"], W["ffn1_gu"], W["ffn1_dn"], S["XPRE"], "ffn1", S["WDN_B"])),
            ("ln1", "ln", phase_ln, (nc, T, cfg, pb, S["XPRE"], S["X1"], S["X1T"], W["ln1_g"], W["ln1_b"], "ln1")),
            ("win", "win", phase_win, (nc, T, cfg, pb, S, W, O, "win", I["hist_conv"])),
            ("mix", "mix", phase_mix, (nc, T, cfg, pb, S, W, I, O, "mix")),
            ("outp", "outp", phase_outp, (nc, T, cfg, pb, S, W, "outp")),
            ("ln2", "ln", phase_ln, (nc, T, cfg, pb, S["XPRE"], S["X2"], S["X2T"], W["ln2_g"], W["ln2_b"], "ln2")),
            ("ffn2", "ffn", phase_ffn, (nc, T, cfg, pb, S["X2T"], S["X2"], W["ffn2_gu"], W["ffn2_dn"], S["XPRE"], "ffn2", S["WDN_B"])),
            ("ln3", "ln", phase_ln, (nc, T, cfg, pb, S["XPRE"], O["yout"], None, W["ln3_g"], W["ln3_b"], "ln3")),
        ]
        for name, kind, fn, a in phases:
            if cfg.only is not None and name not in cfg.only:
                continue
            if run(name, kind, fn, *a):
                break
    return nc


def _pp(v, nchunk):
    return np.ascontiguousarray(np.asarray(v, np.float32).reshape(nchunk, 128).T)


def prep_weights(inp):
    f = lambda a: np.ascontiguousarray(np.asarray(a, np.float32))
    Wn = {}
    for n in ("ffn1_gu", "ffn1_dn", "w_in", "rw_w2", "rw_a2", "rw_g2", "w_rw_out", "w_ssm_out", "w_out", "ffn2_gu", "ffn2_dn"):
        Wn[n] = f(inp[n][0])
    for n in ("ln1_g", "ln1_b", "rw_mu", "rw_w0", "rw_a0", "dt_bias", "a_log", "d_skip", "ln2_g", "ln2_b", "ln3_g", "ln3_b"):
        Wn[n] = f(inp[n][0]).reshape(1, -1)
    for n in ("rw_kk", "rw_ka", "rw_rk", "rw_gn_w", "rw_gn_b"):
        Wn[n] = f(inp[n][0]).reshape(1, -1)
    Wn["b_gate"] = _pp(inp["b_gate"][0], 32)
    cw = np.asarray(inp["conv_w"][0], np.float32)
    Wn["conv_w"] = np.ascontiguousarray(cw.reshape(4, 24, 128).transpose(2, 1, 0).reshape(128, 96))
    Wn["conv_b"] = _pp(inp["conv_b"][0], 24)
    Wn["ssm_norm_w"] = _pp(inp["ssm_norm_w"][0], 16)
    return Wn


def core_inputs(inp, cfg, core, Wn):
    R = cfg.R
    b = core % 4
    sa, sb_ = 2 * core, 2 * core + 1
    xin = np.zeros((R, D), np.float32)
    xin[48:64] = inp["x_sample"][sa]
    xin[64 + 48:128] = inp["x_sample"][sb_]
    xin[128 + 48:192] = inp["meta_tokens"]
    nl = cfg.n_long * 64
    xin[192:192 + nl] = inp["x_prompt"][b][:nl]
    m = {"xin": xin, "xinT": np.ascontiguousarray(xin.T)}
    z = np.zeros
    m["hist_rw"] = np.ascontiguousarray(np.concatenate(
        [inp["state_rwkv_shift"][0, sa], inp["state_rwkv_shift"][0, sb_], z((1, RW_COLS), np.float32)], 0).astype(np.float32))
    m["hist_conv"] = np.ascontiguousarray(np.concatenate(
        [inp["state_conv"][0, sa], inp["state_conv"][0, sb_], z((3, CONV_DIM), np.float32)], 0).astype(np.float32))
    m["wkv0"] = np.ascontiguousarray(np.concatenate(
        [inp["state_wkv"][0, sa].reshape(-1, 64), inp["state_wkv"][0, sb_].reshape(-1, 64), z((RW_H * 64, 64), np.float32)], 0).astype(np.float32))
    m["ssm0"] = np.ascontiguousarray(np.concatenate(
        [inp["state_ssm"][0, sa].reshape(-1, SSM_N), inp["state_ssm"][0, sb_].reshape(-1, SSM_N), z((SSM_H * 64, SSM_N), np.float32)], 0).astype(np.float32))
    m.update(Wn)
    return m


def assemble(results, cfg):
    nl = cfg.n_long * 64
    f = np.float32
    y_prompt = np.zeros((4, nl, D), f)
    y_sample = np.zeros((16, 16, D), f)
    p_shift = np.zeros((1, 4, 1, RW_COLS), f)
    p_wkv = np.zeros((1, 4, RW_H, 64, 64), f)
    p_conv = np.zeros((1, 4, 3, CONV_DIM), f)
    p_ssm = np.zeros((1, 4, SSM_H, 64, SSM_N), f)
    s_shift = np.zeros((1, 16, 1, RW_COLS), f)
    s_wkv = np.zeros((1, 16, RW_H, 64, 64), f)
    s_conv = np.zeros((1, 16, 3, CONV_DIM), f)
    s_ssm = np.zeros((1, 16, SSM_H, 64, SSM_N), f)
    for c, r in enumerate(results):
        yo = np.asarray(r["yout"])
        osh = np.asarray(r["o_shift"])
        ocv = np.asarray(r["o_conv"])
        owk = np.asarray(r["o_wkv"]).reshape(3, RW_H, 64, 64)
        osm = np.asarray(r["o_ssm"]).reshape(3, SSM_H, 64, SSM_N)
        for i in range(2):
            s = 2 * c + i
            y_sample[s] = yo[i * 64 + 48:i * 64 + 64]
            s_shift[0, s, 0] = osh[i]
            s_conv[0, s] = ocv[i * 3:(i + 1) * 3]
            s_wkv[0, s] = owk[i]
            s_ssm[0, s] = osm[i]
        if c < 4:
            y_prompt[c] = yo[192:192 + nl]
            p_shift[0, c, 0] = osh[2]
            p_conv[0, c] = ocv[6:9]
            p_wkv[0, c] = owk[2]
            p_ssm[0, c] = osm[2]
    return (y_prompt, y_sample, p_shift, p_wkv, p_conv, p_ssm, s_shift, s_wkv, s_conv, s_ssm)


def kernel(**inputs):
    cfg = Cfg()
    inp = {k: np.asarray(v) for k, v in inputs.items()}
    nc = build_program(cfg)
    Wn = prep_weights(inp)
    maps = [core_inputs(inp, cfg, c, Wn) for c in range(8)]
    res = run_bass_kernel_spmd(nc, maps, core_ids=list(range(8)))
    return assemble(res.results, cfg)
```
